# Optimizing a Trainium2 kernel written in Bass

```python
import math
import jax
import jax.numpy as jnp
from jax import lax
import numpy as np

D_MODEL = 1024
BATCH = 16
SEQ = 2048
DEPTH = 2

N_BRANCH = 4
BRANCH_WIDTH = D_MODEL // N_BRANCH
HEAD_DIM = 64
SB_HEADS = BRANCH_WIDTH // HEAD_DIM
RET_HEADS = BRANCH_WIDTH // HEAD_DIM
SSM_GROUP = 16
SSM_GROUPS = BRANCH_WIDTH // SSM_GROUP
SSM_STATE = 64
LRU_BLOCKS = 4
LRU_BLOCK = BRANCH_WIDTH // LRU_BLOCKS
CONV_WIDTH = 4
LRU_C = 8.0
Q_BLOCK = 128
RET_CHUNK = 128
ROPE_BASE = 10000.0
EPS = 1e-6
SPLITS = (3 * BRANCH_WIDTH, 4 * BRANCH_WIDTH, 7 * BRANCH_WIDTH, 8 * BRANCH_WIDTH, 12 * BRANCH_WIDTH)
IN_COLS = 12 * BRANCH_WIDTH + N_BRANCH * D_MODEL

kernel_name = 'hybrid_gated_sb_s5_retention_rglru'


def _rms_norm(x, g):
    xf = x.astype(jnp.float32)
    y = xf * lax.rsqrt(jnp.mean(xf * xf, axis=-1, keepdims=True) + EPS) * g.astype(jnp.float32)
    return y.astype(x.dtype)


def _split_heads(t, n_heads):
    b, s, _ = t.shape
    return t.reshape(b, s, n_heads, -1).transpose(0, 2, 1, 3)


def _merge_heads(t):
    b, h, s, d = t.shape
    return t.transpose(0, 2, 1, 3).reshape(b, s, h * d)


def _rotary(t):
    s, d = t.shape[-2], t.shape[-1]
    half = d // 2
    inv_freq = ROPE_BASE ** (-jnp.arange(half, dtype=jnp.float32) / half)
    ang = jnp.arange(s, dtype=jnp.float32)[:, None] * inv_freq[None, :]
    cos, sin = jnp.cos(ang), jnp.sin(ang)
    t1, t2 = t[..., :half], t[..., half:]
    return jnp.concatenate([t1 * cos - t2 * sin, t1 * sin + t2 * cos], axis=-1)


def stick_breaking_attention(q, k, v):
    b, hh, s, d = q.shape
    q = q.astype(jnp.float32) * (d ** -0.5)
    k = k.astype(jnp.float32)
    v = v.astype(jnp.float32)
    outs = []
    for blk in range(s // Q_BLOCK):
        lo = blk * Q_BLOCK
        hi = lo + Q_BLOCK
        z = jnp.einsum('bhqd,bhkd->bhqk', q[:, :, lo:hi], k[:, :, :hi])
        t_idx = lo + jnp.arange(Q_BLOCK)[:, None]
        s_idx = jnp.arange(hi)[None, :]
        mask = s_idx < t_idx
        log_fail = jnp.where(mask, jax.nn.log_sigmoid(-z), 0.0)
        after = lax.cumsum(log_fail, axis=3, reverse=True) - log_fail
        w = jnp.where(mask, jnp.exp(jax.nn.log_sigmoid(z) + after), 0.0)
        outs.append(jnp.einsum('bhqk,bhkd->bhqd', w, v[:, :, :hi]))
    return jnp.concatenate(outs, axis=2)


def retention(q, k, v):
    b, hh, s, d = q.shape
    q = _rotary(q.astype(jnp.float32))
    k = _rotary(k.astype(jnp.float32)) * (d ** -0.5)
    v = v.astype(jnp.float32)
    log_g = jnp.log1p(-(2.0 ** (-5.0 - jnp.arange(hh, dtype=jnp.float32))))
    nc = s // RET_CHUNK
    qc = q.reshape(b, hh, nc, RET_CHUNK, d)
    kc = k.reshape(b, hh, nc, RET_CHUNK, d)
    vc = v.reshape(b, hh, nc, RET_CHUNK, d)
    i = jnp.arange(RET_CHUNK, dtype=jnp.float32)
    rel = i[:, None] - i[None, :]
    decay = jnp.where(rel >= 0, jnp.exp(log_g[:, None, None] * jnp.maximum(rel, 0.0)), 0.0)
    scores = jnp.einsum('bhnid,bhnjd->bhnij', qc, kc) * decay[None, :, None]
    inner = jnp.einsum('bhnij,bhnje->bhnie', scores, vc)
    zeta = jnp.exp(log_g[:, None] * (RET_CHUNK - 1 - i)[None, :])
    xi = jnp.exp(log_g[:, None] * (i + 1.0)[None, :])
    kv = jnp.einsum('bhnjd,hj,bhnje->bhnde', kc, zeta, vc)
    chunk_decay = jnp.exp(log_g * RET_CHUNK)[:, None, None]

    def step(r, kv_n):
        return chunk_decay * r + kv_n, r

    _, r_prev = lax.scan(step, jnp.zeros((b, hh, d, d), jnp.float32), kv.transpose(2, 0, 1, 3, 4))
    r_prev = r_prev.transpose(1, 2, 0, 3, 4)
    cross = jnp.einsum('bhnid,hi,bhnde->bhnie', qc, xi, r_prev)
    o = (inner + cross).reshape(b, hh, s, d)
    mu = jnp.mean(o, axis=-1, keepdims=True)
    var = jnp.mean(jnp.square(o - mu), axis=-1, keepdims=True)
    return (o - mu) * lax.rsqrt(var + EPS)


def s5_branch(u, a_re, a_im, log_dt, b_re, b_im, c_re, c_im, d_skip, w_glu, b_glu):
    bsz, s, _ = u.shape
    u = u.astype(jnp.float32)
    ug = u.reshape(bsz, s, SSM_GROUPS, SSM_GROUP)
    a_re = a_re.astype(jnp.float32)
    a_im = a_im.astype(jnp.float32)
    dt = jnp.exp(log_dt.astype(jnp.float32))[:, None]
    mag = jnp.exp(dt * a_re)
    ab_re = mag * jnp.cos(dt * a_im)
    ab_im = mag * jnp.sin(dt * a_im)
    den = a_re * a_re + a_im * a_im
    num_re = ab_re - 1.0
    f_re = (num_re * a_re + ab_im * a_im) / den
    f_im = (ab_im * a_re - num_re * a_im) / den
    b_re = b_re.astype(jnp.float32)
    b_im = b_im.astype(jnp.float32)
    bb_re = f_re[..., None] * b_re - f_im[..., None] * b_im
    bb_im = f_re[..., None] * b_im + f_im[..., None] * b_re
    x_re = jnp.einsum('bsgc,gnc->sbgn', ug, bb_re)
    x_im = jnp.einsum('bsgc,gnc->sbgn', ug, bb_im)
    a_seq_re = jnp.broadcast_to(ab_re[None, None], (s, 1, SSM_GROUPS, SSM_STATE))
    a_seq_im = jnp.broadcast_to(ab_im[None, None], (s, 1, SSM_GROUPS, SSM_STATE))

    def combine(e1, e2):
        a1r, a1i, b1r, b1i = e1
        a2r, a2i, b2r, b2i = e2
        return (a2r * a1r - a2i * a1i, a2r * a1i + a2i * a1r,
                a2r * b1r - a2i * b1i + b2r, a2r * b1i + a2i * b1r + b2i)

    _, _, h_re, h_im = lax.associative_scan(combine, (a_seq_re, a_seq_im, x_re, x_im), axis=0)
    y = (jnp.einsum('sbgn,gcn->bsgc', h_re, c_re.astype(jnp.float32))
         - jnp.einsum('sbgn,gcn->bsgc', h_im, c_im.astype(jnp.float32)))
    y = y.reshape(bsz, s, BRANCH_WIDTH) + d_skip.astype(jnp.float32) * u
    y = jax.nn.gelu(y)
    return y * jax.nn.sigmoid(y @ w_glu.astype(jnp.float32) + b_glu.astype(jnp.float32))


def rg_lru_branch(u, conv_w, conv_b, w_a, b_a, w_x, b_x, lam):
    bsz, s, _ = u.shape
    u = u.astype(jnp.float32)
    xc = lax.conv_general_dilated(
        u, conv_w.astype(jnp.float32)[:, None, :], window_strides=(1,),
        padding=[(CONV_WIDTH - 1, 0)], dimension_numbers=('NWC', 'WIO', 'NWC'),
        feature_group_count=BRANCH_WIDTH) + conv_b.astype(jnp.float32)
    xb = xc.reshape(bsz, s, LRU_BLOCKS, LRU_BLOCK)
    r = jax.nn.sigmoid(jnp.einsum('bsnc,ncd->bsnd', xb, w_a.astype(jnp.float32)) + b_a.astype(jnp.float32))
    ig = jax.nn.sigmoid(jnp.einsum('bsnc,ncd->bsnd', xb, w_x.astype(jnp.float32)) + b_x.astype(jnp.float32))
    r = r.reshape(bsz, s, BRANCH_WIDTH)
    ig = ig.reshape(bsz, s, BRANCH_WIDTH)
    log_a = -LRU_C * r * jax.nn.softplus(-lam.astype(jnp.float32))
    a = jnp.exp(log_a)
    bterm = jnp.sqrt(-jnp.expm1(2.0 * log_a)) * (ig * xc)

    def combine(e1, e2):
        a1, b1 = e1
        a2, b2 = e2
        return a2 * a1, a2 * b1 + b2

    _, h = lax.associative_scan(combine, (a, bterm), axis=1)
    return h


def _layer(x, pre_g, post_g, w_in, ssm_a_re, ssm_a_im, ssm_log_dt, ssm_b_re, ssm_b_im,
           ssm_c_re, ssm_c_im, ssm_d, ssm_w_glu, ssm_b_glu, lru_conv_w, lru_conv_b,
           lru_w_a, lru_b_a, lru_w_x, lru_b_x, lru_lambda, w_branch, w_out):
    dtype = x.dtype
    bsz, s, _ = x.shape
    h = _rms_norm(x, pre_g)
    proj = jnp.einsum('bsd,dc->bsc', h, w_in)
    sb_qkv, ssm_u, ret_qkv, lru_u, gates, merge = jnp.split(proj, SPLITS, axis=-1)
    q, k, v = jnp.split(sb_qkv, 3, axis=-1)
    y_sb = _merge_heads(stick_breaking_attention(
        _split_heads(q, SB_HEADS), _split_heads(k, SB_HEADS), _split_heads(v, SB_HEADS)))
    y_ssm = s5_branch(ssm_u, ssm_a_re, ssm_a_im, ssm_log_dt, ssm_b_re, ssm_b_im,
                      ssm_c_re, ssm_c_im, ssm_d, ssm_w_glu, ssm_b_glu)
    q, k, v = jnp.split(ret_qkv, 3, axis=-1)
    y_ret = _merge_heads(retention(
        _split_heads(q, RET_HEADS), _split_heads(k, RET_HEADS), _split_heads(v, RET_HEADS)))
    y_lru = rg_lru_branch(lru_u, lru_conv_w, lru_conv_b, lru_w_a, lru_b_a, lru_w_x, lru_b_x, lru_lambda)
    mix = jnp.stack([y_sb, y_ssm, y_ret, y_lru], axis=2).astype(dtype)
    mix = mix * jax.nn.silu(gates.reshape(bsz, s, N_BRANCH, BRANCH_WIDTH))
    y = jnp.einsum('bsnw,nwd->bsnd', mix, w_branch)
    gate = jax.nn.sigmoid(merge.reshape(bsz, s, N_BRANCH, D_MODEL))
    merged = jnp.sum(gate * y, axis=2)
    out = jnp.einsum('bsd,de->bse', merged, w_out)
    return x + _rms_norm(out, post_g)


def setup_inputs(seed: int = 0) -> dict:
    key = jax.random.key(seed)
    ks = jax.random.split(key, 24)
    f32 = jnp.float32
    W, G, N, C = BRANCH_WIDTH, SSM_GROUPS, SSM_STATE, SSM_GROUP
    nrm = lambda k, shape, scale: jax.random.normal(k, shape, f32) * scale
    x = jax.random.normal(ks[0], (BATCH, SEQ, D_MODEL), f32)
    pre_norm_g = 1.0 + nrm(ks[1], (DEPTH, D_MODEL), 0.02)
    post_norm_g = 1.0 + nrm(ks[2], (DEPTH, D_MODEL), 0.02)
    w_in = nrm(ks[3], (DEPTH, D_MODEL, IN_COLS), D_MODEL ** -0.5)
    ssm_a_re = -0.5 + nrm(ks[4], (DEPTH, G, N), 0.01)
    ssm_a_im = jnp.pi * jnp.arange(N, dtype=f32)[None, None, :] + nrm(ks[5], (DEPTH, G, N), 0.01)
    ssm_log_dt = jax.random.uniform(ks[6], (DEPTH, G), f32, minval=math.log(1e-3), maxval=math.log(1e-1))
    ssm_b_re = nrm(ks[7], (DEPTH, G, N, C), (2.0 * C) ** -0.5)
    ssm_b_im = nrm(ks[8], (DEPTH, G, N, C), (2.0 * C) ** -0.5)
    ssm_c_re = nrm(ks[9], (DEPTH, G, C, N), (2.0 * N) ** -0.5)
    ssm_c_im = nrm(ks[10], (DEPTH, G, C, N), (2.0 * N) ** -0.5)
    ssm_d = nrm(ks[11], (DEPTH, W), 1.0)
    ssm_w_glu = nrm(ks[12], (DEPTH, W, W), W ** -0.5)
    ssm_b_glu = nrm(ks[13], (DEPTH, W), 0.02)
    lru_conv_w = nrm(ks[14], (DEPTH, CONV_WIDTH, W), CONV_WIDTH ** -0.5)
    lru_conv_b = nrm(ks[15], (DEPTH, W), 0.02)
    lru_w_a = nrm(ks[16], (DEPTH, LRU_BLOCKS, LRU_BLOCK, LRU_BLOCK), LRU_BLOCK ** -0.5)
    lru_b_a = nrm(ks[17], (DEPTH, LRU_BLOCKS, LRU_BLOCK), 0.02)
    lru_w_x = nrm(ks[18], (DEPTH, LRU_BLOCKS, LRU_BLOCK, LRU_BLOCK), LRU_BLOCK ** -0.5)
    lru_b_x = nrm(ks[19], (DEPTH, LRU_BLOCKS, LRU_BLOCK), 0.02)
    a_c = jax.random.uniform(ks[20], (DEPTH, W), f32, minval=0.9, maxval=0.999)
    a0 = a_c ** (1.0 / LRU_C)
    lru_lambda = jnp.log(a0) - jnp.log1p(-a0)
    w_branch = nrm(ks[21], (DEPTH, N_BRANCH, W, D_MODEL), W ** -0.5)
    w_out = nrm(ks[22], (DEPTH, D_MODEL, D_MODEL), D_MODEL ** -0.5)
    return {'x': x, 'pre_norm_g': pre_norm_g, 'post_norm_g': post_norm_g, 'w_in': w_in,
            'ssm_a_re': ssm_a_re, 'ssm_a_im': ssm_a_im, 'ssm_log_dt': ssm_log_dt,
            'ssm_b_re': ssm_b_re, 'ssm_b_im': ssm_b_im, 'ssm_c_re': ssm_c_re, 'ssm_c_im': ssm_c_im,
            'ssm_d': ssm_d, 'ssm_w_glu': ssm_w_glu, 'ssm_b_glu': ssm_b_glu,
            'lru_conv_w': lru_conv_w, 'lru_conv_b': lru_conv_b, 'lru_w_a': lru_w_a, 'lru_b_a': lru_b_a,
            'lru_w_x': lru_w_x, 'lru_b_x': lru_b_x, 'lru_lambda': lru_lambda,
            'w_branch': w_branch, 'w_out': w_out}


def reference(x, pre_norm_g, post_norm_g, w_in, ssm_a_re, ssm_a_im, ssm_log_dt, ssm_b_re,
              ssm_b_im, ssm_c_re, ssm_c_im, ssm_d, ssm_w_glu, ssm_b_glu, lru_conv_w, lru_conv_b,
              lru_w_a, lru_b_a, lru_w_x, lru_b_x, lru_lambda, w_branch, w_out):
    for l in range(DEPTH):
        x = _layer(x, pre_norm_g[l], post_norm_g[l], w_in[l], ssm_a_re[l], ssm_a_im[l],
                   ssm_log_dt[l], ssm_b_re[l], ssm_b_im[l], ssm_c_re[l], ssm_c_im[l], ssm_d[l],
                   ssm_w_glu[l], ssm_b_glu[l], lru_conv_w[l], lru_conv_b[l], lru_w_a[l], lru_b_a[l],
                   lru_w_x[l], lru_b_x[l], lru_lambda[l], w_branch[l], w_out[l])
    return x
```

```python
import math
import numpy as np
import concourse.bass as bass
import concourse.mybir as mybir
from concourse.bass_utils import run_bass_kernel_spmd

F32 = mybir.dt.float32; BF16 = mybir.dt.bfloat16; I32 = mybir.dt.int32
AF = mybir.ActivationFunctionType; ALU = mybir.AluOpType; AX = mybir.AxisListType
D = 1024; TC = 512; NBK = 4; EPS = 1e-6
NCOLT = 60
PI = math.pi


class T:
    __slots__ = ('ap', 'w', 'r')
    def __init__(s, ap): s.ap = ap; s.w = None; s.r = {}
    def __getitem__(s, idx): return s.ap[idx]


class Eng:
    def __init__(s, nc, name, eng, is_pe=False):
        s.name = name; s.eng = eng; s.sem = nc.alloc_semaphore("sem_" + name); s.cnt = 0; s.known = {}; s.is_pe = is_pe
        s.dsems = []; s.dvals = []; s.dnext = 0


class K:
    def __init__(s, nc, ndma=(16, 16, 4)):
        s.nc = nc
        s.PE = Eng(nc, "pe", nc.tensor, True); s.ACT = Eng(nc, "act", nc.scalar); s.DVE = Eng(nc, "dve", nc.vector)
        s.POOL = Eng(nc, "pool", nc.gpsimd); s.SP = Eng(nc, "sp", nc.sync)
        s.engs = [s.PE, s.ACT, s.DVE, s.POOL, s.SP]
        for E, n in ((s.SP, ndma[0]), (s.POOL, ndma[1]), (s.ACT, ndma[2])):
            E.dsems = [nc.alloc_semaphore(f"d_{E.name}_{i}") for i in range(n)]; E.dvals = [0] * n
        s.nt = 0
    def sb(s, shape, dt, name=None):
        s.nt += 1
        return T(s.nc.alloc_sbuf_tensor("s_" + (name or f"t{s.nt}"), list(shape), dt).ap())
    def ps(s, shape, dt=F32, name=None):
        s.nt += 1
        return T(s.nc.alloc_psum_tensor("ps_" + (name or f"p{s.nt}"), list(shape), dt).ap())
    def _waits(s, E, reads, writes):
        deps = {}
        def add(ev, war=False):
            if ev is None: return
            key, sem, val, who = ev
            if who is E and (E.is_pe or war): return
            if deps.get(key, (None, 0))[1] < val: deps[key] = (sem, val)
        for t in reads: add(t.w)
        for t in writes:
            add(t.w)
            for ev in t.r.values(): add(ev, True)
        for key, (sem, val) in deps.items():
            if E.known.get(key, 0) >= val: continue
            E.eng.wait_ge(sem, val); E.known[key] = val
    def _post(s, ev, reads, writes):
        for t in writes: t.w = ev; t.r = {}
        for t in reads: t.r[ev[0]] = ev
    def op(s, E, fn, reads=(), writes=()):
        s._waits(E, reads, writes)
        ins = fn(E.eng); E.cnt += 1; ins.then_inc(E.sem, 1)
        s._post((E.name, E.sem, E.cnt, E), reads, writes)
        return ins
    def dma(s, Q, out, in_, reads=(), writes=(), **kw):
        i = Q.dnext; Q.dnext = (i + 1) % len(Q.dsems); sem = Q.dsems[i]; key = f"d_{Q.name}_{i}"
        if Q.dvals[i] > 0 and Q.known.get(key, 0) < Q.dvals[i]:
            Q.eng.wait_ge(sem, Q.dvals[i]); Q.known[key] = Q.dvals[i]
        s._waits(Q, reads, writes)
        ins = Q.eng.dma_start(out=out, in_=in_, **kw); Q.dvals[i] += 16; ins.then_inc(sem, 16)
        ev = (key, sem, Q.dvals[i], None)
        s._post(ev, reads, writes)
        return ev
    def barrier(s):
        for E in s.engs:
            for Fg in s.engs:
                if Fg is E or Fg.cnt == 0: continue
                if E.known.get(Fg.name, 0) < Fg.cnt:
                    E.eng.wait_ge(Fg.sem, Fg.cnt); E.known[Fg.name] = Fg.cnt
            for Q in s.engs:
                for i, sem in enumerate(Q.dsems):
                    key = f"d_{Q.name}_{i}"
                    if Q.dvals[i] > 0 and E.known.get(key, 0) < Q.dvals[i]:
                        E.eng.wait_ge(sem, Q.dvals[i]); E.known[key] = Q.dvals[i]


def build(S, NSEQ, DEPTH, dbg=None, parts=None):
    nc = bass.Bass("TRN2", target_bir_lowering=False)
    k = K(nc)
    if parts is None: parts = {"setup", "prenorm", "proj", "sb", "s5", "ret", "lru", "O"}
    PE, ACT, DVE, POOL, SP = k.PE, k.ACT, k.DVE, k.POOL, k.SP
    NCH = S // TC
    NBLK = S // 128
    L = DEPTH
    def din(name, shape): return nc.dram_tensor(name, list(shape), F32, kind="ExternalInput").ap()
    x_d = din("x", [NSEQ, S, D]); win_d = din("win", [L, D, 7680]); wbr_d = din("wbr", [L, 4, 256, D])
    wout_d = din("wout", [L, D, D]); wglu_d = din("wglu", [L, 256, 256]); pp_d = din("pp", [L, 128, 40])
    lruw_d = din("lruw", [L, 2, 128, 2, 128]); s5row_d = din("s5row", [L, 3, 1024]); s5col_d = din("s5col", [L, 128, 3, 16])
    bblk_d = din("bblk", [L, 2, 128, 2, 512]); cab_d = din("cab", [L, 128, 32, 128])
    ropeC_d = din("ropeC", [128, S]); ropeS_d = din("ropeS", [128, S])
    dtab_d = din("dtab", [128, 2, 256]); ztab_d = din("ztab", [128, 256]); xitab_d = din("xitab", [128, 2, 128])
    gdec_d = din("gdec", [128, 2]); cm_d = din("cmats", [128, 8, 128]); misc_d = din("misc", [128, 132])
    y_d = nc.dram_tensor("y", [NSEQ, S, D], F32, kind="ExternalOutput").ap()
    dbg_d = {}
    if dbg:
        for nm, shp in dbg.items():
            dbg_d[nm] = nc.dram_tensor(nm, list(shp), F32, kind="ExternalOutput").ap()

    xTb = nc.alloc_sbuf_tensor("s_xT", [128, 8, S], F32).ap()
    xTc = [T(xTb[:, :, c * TC:(c + 1) * TC]) for c in range(S // TC)]
    hT = k.sb([128, 8, TC], BF16, "hT")
    WT = [k.sb([128, 8, 128], BF16, f"WT{i}") for i in range(3)]
    wtn = [0]
    pp = k.sb([128, 40], F32, "pp")
    cmats = k.sb([128, 8, 128], BF16, "cmats")
    identb = cmats[:, 0, :]; ntri = cmats[:, 1, :]; nones = cmats[:, 2, :]; tri = cmats[:, 3, :]
    Jm = cmats[:, 5, :]; onesb = cmats[:, 6, :]
    identf = k.sb([128, 128], F32, "identf")
    misc = k.sb([128, 132], F32, "misc")
    mask2 = k.sb([128, 256], BF16, "mask2")
    zrow = k.sb([1, 256], BF16, "zrow")
    ropeC = k.sb([128, TC], BF16, "ropeC"); ropeS = k.sb([128, TC], BF16, "ropeS")
    dtab = k.sb([128, 2, 256], BF16, "dtab"); ztab = k.sb([128, 256], BF16, "ztab"); xitab = k.sb([128, 2, 128], BF16, "xitab")
    gdec = k.sb([128, 2], F32, "gdec")
    KT = k.sb([128, 2, S], BF16, "KT"); Vc = k.sb([128, NBLK, 256], BF16, "Vc"); QT = k.sb([128, 2, TC], BF16, "QT")
    sbE = [k.sb([128, 256], F32, f"sbE{i}") for i in range(2)]
    sbSP = [k.sb([128, 256], BF16, f"sbSP{i}") for i in range(2)]
    sbW = [k.sb([128, 256], BF16, f"sbW{i}") for i in range(2)]
    sbC = [k.sb([128, 256], BF16, f"sbC{i}") for i in range(2)]
    osb = k.sb([128, 256], BF16, "osb")
    uT = k.sb([128, 2, TC], BF16, "uT")
    Bx = [k.sb([128, 1024], BF16, f"Bx{i}") for i in range(2)]; Bsw = [k.sb([128, 1024], BF16, f"Bsw{i}") for i in range(2)]
    Pr = k.sb([128, 16, 64], BF16, "Pr"); Pi = k.sb([128, 16, 64], BF16, "Pi")
    T1 = k.sb([128, 16, 128], BF16, "T1"); T2 = k.sb([128, 16, 128], BF16, "T2")
    W1 = k.sb([128, 16], F32, "W1"); W2 = k.sb([128, 16], F32, "W2")
    Cm = k.sb([128, 32, 128], BF16, "Cm")
    G1 = [k.sb([128, 4, 128], BF16, f"G1_{i}") for i in range(2)]; G2 = [k.sb([128, 4, 128], BF16, f"G2_{i}") for i in range(2)]
    A1 = [k.sb([128, 4, 128], BF16, f"A1_{i}") for i in range(2)]; A2 = [k.sb([128, 4, 128], BF16, f"A2_{i}") for i in range(2)]
    carry = [k.sb([128, 4], F32, f"carry{i}") for i in range(4)]
    sfull = k.sb([128, 4], F32, "sfull"); U1 = k.sb([128, 4], BF16, "U1"); U2 = k.sb([128, 4], BF16, "U2")
    ygb = k.sb([128, 2, TC], BF16, "ygb"); wglu = k.sb([128, 2, 256], BF16, "wglu")
    uL = k.sb([128, 2, TC + 4], F32, "uL"); hst = k.sb([128, 2], F32, "hst")
    WA = k.sb([128, 2, 128], BF16, "WA"); WX = k.sb([128, 2, 128], BF16, "WX")
    cl = k.sb([128, 2], F32, "cl"); cl2 = k.sb([128, 2], F32, "cl2"); cltmp = k.sb([128, 2], F32, "cltmp")
    Rst = k.sb([128, 2, 128], F32, "Rst"); Rb = k.sb([128, 2, 128], BF16, "Rb")
    st4 = [k.sb([128, 4], F32, f"st4_{i}") for i in range(6)]
    yTb = nc.alloc_sbuf_tensor("s_yT", [128, 8, TC], BF16).ap()
    yTt = [T(yTb[:, i, :]) for i in range(8)]
    WB = [k.sb([128, 8, 128], BF16, f"WB{i}") for i in range(2)]
    colp = k.sb([128, 3, 16], F32, "colp"); colq = k.sb([128, 3, 16], F32, "colq")
    UN = nc.alloc_sbuf_tensor("UN", [128, 7168], F32).ap()
    def carve(off_f32, n_f32, dt, shape):
        v = UN[:, off_f32:off_f32 + n_f32]
        if dt is BF16: v = v.bitcast(BF16)
        if len(shape) == 3: v = v.rearrange("p (a b) -> p a b", a=shape[1])
        return T(v)
    LT = [carve(512 * i, 512, F32, [128, 512]) for i in range(5)]
    xcb = carve(2560, 256, BF16, [128, 512])
    qrot = carve(2816, 512, BF16, [128, 2, 512]); krot = carve(3328, 512, BF16, [128, 2, 512])
    qxi = carve(3840, 512, BF16, [128, 2, 512])
    vt = carve(4352, 512, BF16, [128, 4, 256]); kz = carve(4864, 128, BF16, [128, 256])
    PTt = carve(4992, 256, BF16, [128, 2, 256])
    osbf = carve(5248, 256, F32, [128, 256]); osq = carve(5504, 256, F32, [128, 256]); onb = carve(5760, 128, BF16, [128, 256])
    rt1 = carve(5888, 512, F32, [128, 512]); rt2 = carve(6400, 512, F32, [128, 512])
    merged = carve(0, 2048, BF16, [128, 8, 512])
    prodS = [carve(2048 + 256 * i, 256, BF16, [128, 512]) for i in range(4)]
    sg = [carve(3072, 512, F32, [128, 512]), carve(3584, 512, F32, [128, 512])]
    outT = carve(4096, 2048, F32, [128, 8, 256])
    stg = [carve(4096, 1024, F32, [128, 1024]), carve(5120, 1024, F32, [128, 1024])]
    sqb = [carve(6144, 256, BF16, [128, 512]), carve(6400, 256, BF16, [128, 512])]
    rstd = carve(6656, 512, F32, [128, 512])
    SRt = [carve(512 * i, 512, F32, [128, 512]) for i in range(14)]
    B = [k.ps([128, 512], F32, f"bank{i}") for i in range(8)]

    def mm(out_t, out_ap, lhsT, rhs, start, stop, reads):
        k.op(PE, lambda e: e.matmul(out_ap, lhsT=lhsT, rhs=rhs, start=start, stop=stop), reads=reads, writes=[out_t])
    def act(out_t, out_ap, in_ap, func, reads, bias=None, scale=None):
        kw = {}
        if bias is not None: kw['bias'] = bias
        if scale is not None: kw['scale'] = scale
        k.op(ACT, lambda e: e.activation(out=out_ap, in_=in_ap, func=func, **kw), reads=reads, writes=[out_t])
    def tt(E, out_t, out_ap, a, b, op, reads):
        k.op(E, lambda e: e.tensor_tensor(out=out_ap, in0=a, in1=b, op=op), reads=reads, writes=[out_t])
    def ts(E, out_t, out_ap, a, s1, s2, op0, op1, reads):
        if op1 is None:
            k.op(E, lambda e: e.tensor_scalar(out=out_ap, in0=a, scalar1=s1, scalar2=None, op0=op0), reads=reads, writes=[out_t])
        else:
            k.op(E, lambda e: e.tensor_scalar(out=out_ap, in0=a, scalar1=s1, scalar2=s2, op0=op0, op1=op1), reads=reads, writes=[out_t])
    def stt(out_t, out_ap, a, sc, b, op0, op1, reads):
        k.op(DVE, lambda e: e.scalar_tensor_tensor(out=out_ap, in0=a, scalar=sc, in1=b, op0=op0, op1=op1), reads=reads, writes=[out_t])
    def cp(E, out_t, out_ap, in_ap, reads):
        if E is ACT:
            k.op(E, lambda e: e.copy(out=out_ap, in_=in_ap), reads=reads, writes=[out_t])
        else:
            k.op(E, lambda e: e.tensor_copy(out=out_ap, in_=in_ap), reads=reads, writes=[out_t])
    def memset(E, t, ap, v):
        k.op(E, lambda e: e.memset(ap, v), writes=[t])
    def dbg_out(name, t, ap):
        if name in dbg_d:
            k.dma(SP, dbg_d[name], ap, reads=[t])

    def load_w(src_ap):
        w = WT[wtn[0] % 3]; wtn[0] += 1
        k.dma(POOL, w[:], src_ap, writes=[w])
        return w
    def win_tile(l, ct):
        return win_d[l].rearrange("(k p) c -> p k c", p=128)[:, :, ct * 128:(ct + 1) * 128]
    pbn = [0]
    def proj_fm(l, ct):
        w = load_w(win_tile(l, ct))
        b = B[pbn[0] % 2]; pbn[0] += 1
        for kk in range(8):
            mm(b, b[:], w[:, kk, :], hT[:, kk, :], kk == 0, kk == 7, [w, hT])
        return b
    def proj_tm(l, ct0, dst_t, dst_fn):
        w0 = load_w(win_tile(l, ct0)); w1 = load_w(win_tile(l, ct0 + 1))
        for half in range(2):
            b = B[pbn[0] % 2]; pbn[0] += 1
            for bi in range(2):
                blk = half * 2 + bi
                for j, w in enumerate((w0, w1)):
                    o = bi * 256 + j * 128
                    for kk in range(8):
                        mm(b, b[:, o:o + 128], hT[:, kk, blk * 128:(blk + 1) * 128], w[:, kk, :], kk == 0, kk == 7, [w, hT])
            for bi in range(2):
                blk = half * 2 + bi
                cp(ACT, dst_t, dst_fn(blk), b[:, bi * 256:(bi + 1) * 256], [b])

    def sincos(ang, vw, tmps):
        ki, kr, sn, cs = tmps
        kiv = vw(ki).bitcast(I32)
        ts(DVE, kr, vw(kr), vw(ang), 1.0 / (2 * PI), None, ALU.mult, None, [ang])
        cp(DVE, ki, kiv, vw(kr), [kr])
        cp(DVE, kr, vw(kr), kiv, [ki])
        C1 = 6.28125; C2 = 2 * PI - 6.28125
        stt(sn, vw(sn), vw(kr), -C1, vw(ang), ALU.mult, ALU.add, [kr, ang])
        stt(sn, vw(sn), vw(kr), -C2, vw(sn), ALU.mult, ALU.add, [kr, sn])
        ts(DVE, sn, vw(sn), vw(sn), -PI, PI, ALU.max, ALU.min, [sn])
        ts(DVE, cs, vw(cs), vw(sn), PI / 2, -2 * PI, ALU.is_gt, ALU.mult, [sn])
        stt(cs, vw(cs), vw(sn), PI / 2, vw(cs), ALU.add, ALU.add, [sn, cs])
        ts(DVE, cs, vw(cs), vw(cs), -PI, PI, ALU.max, ALU.min, [cs])
        act(cs, vw(cs), vw(cs), AF.Sin, [cs])
        act(sn, vw(sn), vw(sn), AF.Sin, [sn])

    k.dma(POOL, cmats[:], cm_d, writes=[cmats])
    k.dma(SP, misc[:], misc_d, writes=[misc])
    k.dma(POOL, dtab[:], dtab_d, writes=[dtab]); k.dma(POOL, ztab[:], ztab_d, writes=[ztab]); k.dma(POOL, xitab[:], xitab_d, writes=[xitab])
    k.dma(SP, gdec[:], gdec_d, writes=[gdec])
    memset(POOL, zrow, zrow[:], 0.0)
    for i in range(2):
        cp(POOL, mask2, mask2[:, i * 128:(i + 1) * 128], cmats[:, 4, :], [cmats])
    cp(DVE, identf, identf[:], cmats[:, 0, :], [cmats])
    tauc = misc[:, 0:1]; sgn1 = misc[:, 1:2]; eps_ap = misc[:, 2:3]; one_ap = misc[:, 3:4]; taurow = misc[:, 4:132]


    flat = lambda t: t.ap
    v3 = lambda t: t.ap.rearrange("p (g n) -> p g n", g=8)
    v4 = lambda t: t.ap.rearrange("p (g n) -> p g n", g=4)
    sm = lambda t: t.ap[:, 0:16]
    MUL, ADD, SUB = ALU.mult, ALU.add, ALU.subtract

    def recip(out_t, out_ap, in_ap, reads):
        k.op(DVE, lambda e: e.reciprocal(out=out_ap, in_=in_ap), reads=reads, writes=[out_t])
    def transpose(out_t, out_ap, in_ap, ident_ap, reads):
        k.op(PE, lambda e: e.transpose(out=out_ap, in_=in_ap, identity=ident_ap), reads=reads, writes=[out_t])

    def layer_setup(l):
        k.dma(SP, pp[:], pp_d[l], writes=[pp])
        k.dma(POOL, WA[:], lruw_d[l, 0], writes=[WA]); k.dma(POOL, WX[:], lruw_d[l, 1], writes=[WX])
        k.dma(POOL, wglu[:], wglu_d[l].rearrange("(k p) c -> p k c", p=128), writes=[wglu])
        k.dma(POOL, Cm[:], cab_d[l], writes=[Cm])
        k.dma(SP, colp[:], s5col_d[l], writes=[colp])
        act(cltmp, cltmp[:], pp[:, 34:36], AF.Exp, [pp], scale=-1.0)
        act(cltmp, cltmp[:], cltmp[:], AF.Ln, [cltmp, misc], bias=one_ap)
        ts(DVE, cl, cl[:], cltmp[:], -8.0, None, MUL, None, [cltmp])
        ts(DVE, cl2, cl2[:], cltmp[:], -16.0, None, MUL, None, [cltmp])
        memset(POOL, uL, uL[:, :, 0:4], 0.0); memset(POOL, hst, hst[:], 0.0)
        memset(POOL, Rst, Rst[:], 0.0); memset(POOL, Rb, Rb[:], 0.0)
        for q in range(4): memset(POOL, carry[q], carry[q][:], 0.0)
        act(colq, colq[:, 0, :], colp[:, 2, :], AF.Exp, [colp])
        tt(DVE, colq, colq[:, 1, :], colq[:, 0, :], colp[:, 0, :], MUL, [colq, colp])
        tt(DVE, colq, colq[:, 2, :], colq[:, 0, :], colp[:, 1, :], MUL, [colq, colp])
        for kh in range(2):
            are, aim, dt_, dre, dim_, mag, ki, kr, sn, cs, fr, fi, bre, bim = SRt
            for i, t in enumerate((are, aim, dt_)):
                k.dma(SP, t[:], bass.AP(s5row_d.tensor, (l * 3 + i) * 1024 + kh * 512, [[0, 128], [1, 512]]), writes=[t])
            k.dma(SP, bre[:], bblk_d[l, 0][:, kh, :], writes=[bre]); k.dma(SP, bim[:], bblk_d[l, 1][:, kh, :], writes=[bim])
            act(dt_, dt_[:], dt_[:], AF.Exp, [dt_])
            tt(DVE, dre, dre[:], dt_[:], are[:], MUL, [dt_, are]); tt(DVE, dim_, dim_[:], dt_[:], aim[:], MUL, [dt_, aim])
            act(mag, mag[:], dre[:], AF.Exp, [dre])
            sincos(dim_, flat, (ki, kr, sn, cs))
            tt(DVE, cs, cs[:], mag[:], cs[:], MUL, [mag, cs]); tt(DVE, sn, sn[:], mag[:], sn[:], MUL, [mag, sn])
            ts(DVE, cs, cs[:], cs[:], -1.0, None, ADD, None, [cs])
            tt(DVE, mag, mag[:], are[:], are[:], MUL, [are]); tt(DVE, ki, ki[:], aim[:], aim[:], MUL, [aim])
            tt(DVE, mag, mag[:], mag[:], ki[:], ADD, [mag, ki]); recip(mag, mag[:], mag[:], [mag])
            tt(DVE, fr, fr[:], cs[:], are[:], MUL, [cs, are]); tt(DVE, ki, ki[:], sn[:], aim[:], MUL, [sn, aim])
            tt(DVE, fr, fr[:], fr[:], ki[:], ADD, [fr, ki]); tt(DVE, fr, fr[:], fr[:], mag[:], MUL, [fr, mag])
            tt(DVE, fi, fi[:], sn[:], are[:], MUL, [sn, are]); tt(DVE, ki, ki[:], cs[:], aim[:], MUL, [cs, aim])
            tt(DVE, fi, fi[:], fi[:], ki[:], SUB, [fi, ki]); tt(DVE, fi, fi[:], fi[:], mag[:], MUL, [fi, mag])
            tt(DVE, kr, kr[:], fr[:], bre[:], MUL, [fr, bre]); tt(DVE, ki, ki[:], fi[:], bim[:], MUL, [fi, bim])
            tt(DVE, sn, sn[:], fr[:], bim[:], MUL, [fr, bim]); tt(DVE, cs, cs[:], fi[:], bre[:], MUL, [fi, bre])
            bxv = Bx[kh].ap.rearrange("p (g a n) -> p g a n", g=8, a=2)
            bsv = Bsw[kh].ap.rearrange("p (g a n) -> p g a n", g=8, a=2)
            tt(DVE, Bx[kh], bxv[:, :, 0, :], v3(kr), v3(ki), SUB, [kr, ki])
            tt(DVE, Bsw[kh], bsv[:, :, 1, :], v3(kr), v3(ki), SUB, [kr, ki])
            tt(DVE, Bx[kh], bxv[:, :, 1, :], v3(sn), v3(cs), ADD, [sn, cs])
            stt(Bsw[kh], bsv[:, :, 0, :], v3(sn), -1.0, v3(cs), MUL, SUB, [sn, cs])
            ts(DVE, mag, mag[:], dim_[:], tauc, None, MUL, None, [dim_, misc])
            sincos(mag, flat, (ki, kr, sn, cs))
            ts(DVE, fr, fr[:], dre[:], tauc, None, MUL, None, [dre, misc]); act(fr, fr[:], fr[:], AF.Exp, [fr], scale=-1.0)
            tt(DVE, Pr, Pr[:, 8 * kh:8 * kh + 8, :], v3(fr), v3(cs), MUL, [fr, cs])
            stt(Pi, Pi[:, 8 * kh:8 * kh + 8, :], v3(fr), -1.0, v3(sn), MUL, MUL, [fr, sn])
        for q in range(4):
            ang, mexp, ki, kr, sn, cs = SRt[0:6]
            taub = taurow.unsqueeze(1).to_broadcast([128, 4, 128])
            dimb = colq[:, 2, 4 * q:4 * q + 4].unsqueeze(2).to_broadcast([128, 4, 128])
            dreb = colq[:, 1, 4 * q:4 * q + 4].unsqueeze(2).to_broadcast([128, 4, 128])
            tt(DVE, ang, v4(ang), dimb, taub, MUL, [colq, misc])
            tt(DVE, mexp, v4(mexp), dreb, taub, MUL, [colq, misc]); act(mexp, mexp[:], mexp[:], AF.Exp, [mexp])
            sincos(ang, flat, (ki, kr, sn, cs))
            stt(T1, T1[:, 4 * q:4 * q + 4, :], v4(mexp), sgn1, v4(cs), MUL, MUL, [mexp, misc, cs])
            stt(T2, T2[:, 4 * q:4 * q + 4, :], v4(mexp), -1.0, v4(sn), MUL, MUL, [mexp, sn])
        angw, mw, ki, kr, sn, cs = SRt[6:12]
        ts(DVE, angw, sm(angw), colq[:, 2, :], 128.0, None, MUL, None, [colq])
        ts(DVE, mw, sm(mw), colq[:, 1, :], 128.0, None, MUL, None, [colq]); act(mw, sm(mw), sm(mw), AF.Exp, [mw])
        sincos(angw, sm, (ki, kr, sn, cs))
        tt(DVE, W1, W1[:], sm(mw), sm(cs), MUL, [mw, cs]); tt(DVE, W2, W2[:], sm(mw), sm(sn), MUL, [mw, sn])

    def load_seq(sq):
        for blk in range(NBLK):
            st = stg[blk % 2]; c = blk // 4; o = (blk % 4) * 128
            k.dma(SP, st[:], x_d[sq, blk * 128:(blk + 1) * 128, :], writes=[st])
            for half in range(2):
                b = B[2 + half]
                for j in range(4):
                    f = half * 4 + j
                    transpose(b, b[:, j * 128:(j + 1) * 128], st[:, f * 128:(f + 1) * 128], identf[:], [st, identf])
                cp(DVE if half == 0 else ACT, xTc[c], xTc[c][:, half * 4:(half + 1) * 4, o:o + 128],
                   b[:].rearrange("p (a b) -> p a b", a=4), [b])

    def store_seq(sq):
        for blk in range(NBLK):
            st = stg[blk % 2]; c = blk // 4; o = (blk % 4) * 128
            for half in range(2):
                b = B[2 + half]
                for j in range(4):
                    f = half * 4 + j
                    transpose(b, b[:, j * 128:(j + 1) * 128], xTc[c][:, f, o:o + 128], identf[:], [xTc[c], identf])
                cp(DVE if half == 0 else ACT, st, st[:, half * 512:(half + 1) * 512], b[:], [b])
            k.dma(SP, y_d[sq, blk * 128:(blk + 1) * 128, :], st[:], reads=[st])

    def prenorm(l, c):
        xc_ = xTc[c]; ssb = B[5]
        for f in range(8):
            s_ = sqb[f % 2]
            act(s_, s_[:], xc_[:, f, :], AF.Square, [xc_])
            mm(ssb, ssb[:], onesb, s_[:], f == 0, f == 7, [cmats, s_])
        act(sg[0], sg[0][:], ssb[:], AF.Sqrt, [ssb, misc], bias=eps_ap, scale=1.0 / D)
        recip(rstd, rstd[:], sg[0][:], [sg[0]])
        for f in range(8):
            stt(hT, hT[:, f, :], xc_[:, f, :], pp[:, f:f + 1], rstd[:], MUL, MUL, [xc_, pp, rstd])

    def phaseM(l, c):
        t0 = c * TC
        k.dma(POOL, ropeC[:], ropeC_d[:, t0:t0 + TC], writes=[ropeC]); k.dma(POOL, ropeS[:], ropeS_d[:, t0:t0 + TC], writes=[ropeS])
        for i in range(2):
            b = proj_fm(l, i); act(QT, QT[:, i, :], b[:], AF.Copy, [b], scale=0.125)
        for i in range(2):
            b = proj_fm(l, 2 + i); cp(DVE, KT, KT[:, i, t0:t0 + TC], b[:], [b])
        proj_tm(l, 4, Vc, lambda blk: Vc[:, c * 4 + blk, :])
        for i in range(2):
            b = proj_fm(l, 6 + i); cp(ACT, uT, uT[:, i, :], b[:], [b])
        for (ct, ctsw, dst) in ((8, 56, qrot), (10, 58, krot)):
            for i in range(2):
                b = proj_fm(l, ct + i); tt(DVE, rt1, rt1[:], b[:], ropeC[:], MUL, [b, ropeC])
                b2 = proj_fm(l, ctsw + i); tt(DVE, rt2, rt2[:], b2[:], ropeS[:], MUL, [b2, ropeS])
                tt(POOL, dst, dst[:, i, :], rt1[:], rt2[:], ADD, [rt1, rt2])
        proj_tm(l, 12, vt, lambda blk: vt[:, blk, :])
        for i in range(2):
            b = proj_fm(l, 14 + i); cp(DVE, uL, uL[:, i, 3:3 + TC], b[:], [b])

        def sec_sb():
            Z = [B[2], B[3]]; PO = B[4]; PTb = B[5]; pv = PTb.ap.bitcast(BF16)
            for qi in range(4):
                qb = c * 4 + qi
                mm(PO, PO[:, 0:256], zrow[0:1, 0:128], zrow[0:1, 0:256], True, False, [zrow])
                for a in range(qb, -1, -1):
                    diag = (a == qb)
                    for par in range(2):
                        z = Z[par]
                        mm(z, z[:, 0:256], zrow[0:1, 0:128], zrow[0:1, 0:256], True, False, [zrow])
                        for ti in range(2):
                            mm(z, z[:, ti * 128:(ti + 1) * 128], KT[64 * par:64 * par + 64, ti, a * 128:(a + 1) * 128],
                               QT[64 * par:64 * par + 64, ti, qi * 128:(qi + 1) * 128], False, False, [KT, QT])
                    for par in range(2):
                        z = Z[par]; e_ = sbE[par]; sp_ = sbSP[par]; w_ = sbW[par]; c_ = sbC[par]
                        act(e_, e_[:], z[:, 0:256], AF.Exp, [z])
                        act(sp_, sp_[:], e_[:], AF.Ln, [e_, misc], bias=one_ap)
                        if diag: tt(POOL, sp_, sp_[:], sp_[:], mask2[:], MUL, [sp_, mask2])
                        mm(z, z[:, 0:256], ntri, sp_[:], False, diag, [cmats, sp_])
                        if not diag: mm(z, z[:, 0:256], nones, c_[:], False, True, [cmats, c_])
                        act(w_, w_[:], z[:, 0:256], AF.Exp, [z])
                        if diag: tt(POOL, w_, w_[:], w_[:], mask2[:], MUL, [w_, mask2])
                        if a > 0:
                            if diag: cp(POOL, c_, c_[:], sp_[:], [sp_])
                            else: tt(POOL, c_, c_[:], c_[:], sp_[:], ADD, [c_, sp_])
                        for ti in range(2):
                            h = 2 * ti + par
                            mm(PO, PO[:, h * 64:(h + 1) * 64], w_[:, ti * 128:(ti + 1) * 128], Vc[:, a, h * 64:(h + 1) * 64],
                               False, (a == 0 and par == 1 and ti == 1), [w_, Vc])
                cp(ACT, osb, osb[:], PO[:, 0:256], [PO])
                for ti in range(2):
                    transpose(PTb, pv[:, ti * 128:(ti + 1) * 128], osb[:, ti * 128:(ti + 1) * 128], identb, [osb, cmats])
                cp(DVE, yTt[0], yTb[:, 0:2, qi * 128:(qi + 1) * 128], pv[:, 0:256].rearrange("p (a b) -> p a b", a=2), [PTb])
                yTt[1].w = yTt[0].w; yTt[1].r = {}

        if 'sb' in parts: sec_sb()

        def sec_s5():
            Yb = [B[1], B[4]]
            x4 = lambda b: b.ap.rearrange("p (g a n) -> p g a n", g=4, a=2)
            g4 = lambda t: t.ap.rearrange("p g (a n) -> p g a n", a=2)
            for sc in range(4):
                for q in range(4):
                    kh = q // 2; hq = q % 2; st_ = (sc * 4 + q) % 2
                    mm(B[6], B[6][:], uT[:, kh, sc * 128:(sc + 1) * 128], Bx[kh][:, hq * 512:(hq + 1) * 512], True, True, [uT, Bx[kh]])
                    mm(B[7], B[7][:], uT[:, kh, sc * 128:(sc + 1) * 128], Bsw[kh][:, hq * 512:(hq + 1) * 512], True, True, [uT, Bsw[kh]])
                    prb = Pr[:, 4 * q:4 * q + 4, :].unsqueeze(2).to_broadcast([128, 4, 2, 64])
                    pib = Pi[:, 4 * q:4 * q + 4, :].unsqueeze(2).to_broadcast([128, 4, 2, 64])
                    tt(DVE, G1[st_], g4(G1[st_]), x4(B[6]), prb, MUL, [B[6], Pr])
                    tt(DVE, G2[st_], g4(G2[st_]), x4(B[7]), pib, MUL, [B[7], Pi])
                    for gl in range(4):
                        mm(B[0], B[0][:, gl * 128:(gl + 1) * 128], G1[st_][:, gl, :], tri, True, False, [G1[st_], cmats])
                        mm(B[0], B[0][:, gl * 128:(gl + 1) * 128], G2[st_][:, gl, :], tri, False, True, [G2[st_], cmats])
                    for gl in range(4):
                        g = 4 * q + gl
                        stt(A1[st_], A1[st_][:, gl, :], B[0][:, gl * 128:(gl + 1) * 128], carry[q][:, gl:gl + 1], T1[:, g, :], ADD, MUL, [B[0], carry[q], T1])
                        stt(A2[st_], A2[st_][:, gl, :], B[0][:, gl * 128:(gl + 1) * 128], carry[q][:, gl:gl + 1], T2[:, g, :], ADD, MUL, [B[0], carry[q], T2])
                    yb = Yb[kh]
                    for gl in range(4):
                        g = 4 * q + gl
                        mm(yb, yb[:, sc * 128:(sc + 1) * 128], Cm[:, 2 * g, :], A1[st_][:, gl, :], (hq == 0 and gl == 0), False, [Cm, A1[st_]])
                        mm(yb, yb[:, sc * 128:(sc + 1) * 128], Cm[:, 2 * g + 1, :], A2[st_][:, gl, :], False, (hq == 1 and gl == 3), [Cm, A2[st_]])
                    slast = B[0][:].rearrange("p (g t) -> p g t", g=4)[:, :, 127:128].rearrange("p g o -> p (g o)")
                    tt(DVE, sfull, sfull[:], slast, carry[q][:], ADD, [B[0], carry[q]])
                    tt(DVE, U1, U1[:], sfull[:], W1[:, 4 * q:4 * q + 4], MUL, [sfull, W1])
                    tt(DVE, U2, U2[:], sfull[:], W2[:, 4 * q:4 * q + 4], MUL, [sfull, W2])
                    cb = B[5]
                    mm(cb, cb[:, 0:4], identb, U1[:], True, False, [cmats, U1]); mm(cb, cb[:, 0:4], Jm, U2[:], False, True, [cmats, U2])
                    cp(ACT, carry[q], carry[q][:], cb[:, 0:4], [cb])
            for kh in range(2):
                yv = LT[kh]; x2 = LT[2]
                stt(yv, yv[:], uT[:, kh, :], pp[:, 16 + kh:17 + kh], Yb[kh][:], MUL, ADD, [uT, pp, Yb[kh]])
                act(x2, x2[:], yv[:], AF.Square, [yv])
                ts(DVE, x2, x2[:], x2[:], 0.044715, 1.0, MUL, ADD, [x2])
                tt(DVE, x2, x2[:], x2[:], yv[:], MUL, [x2, yv])
                act(x2, x2[:], x2[:], AF.Sigmoid, [x2], scale=1.5957691216057308)
                tt(DVE, yv, yv[:], yv[:], x2[:], MUL, [yv, x2])
                cp(POOL, ygb, ygb[:, kh, :], yv[:], [yv])
            for e in range(2):
                b = B[6 + e]
                for kh in range(2):
                    mm(b, b[:], wglu[:, kh, e * 128:(e + 1) * 128], ygb[:, kh, :], kh == 0, kh == 1, [wglu, ygb])
                act(LT[2 + e], LT[2 + e][:], b[:], AF.Sigmoid, [b, pp], bias=pp[:, 18 + e:19 + e])
                tt(DVE, yTt[2 + e], yTb[:, 2 + e, :], LT[e][:], LT[2 + e][:], MUL, [LT[e], LT[2 + e]])

        if 's5' in parts: sec_s5()

        def sec_ret():
            for ti in range(2):
                tt(DVE, qxi, qxi[:, ti, :].rearrange("p (n i) -> p n i", n=4), qrot[:, ti, :].rearrange("p (n i) -> p n i", n=4),
                   xitab[:, ti, :].unsqueeze(1).to_broadcast([128, 4, 128]), MUL, [qrot, xitab])
            pv2 = B[6].ap.bitcast(BF16); pv = B[5].ap.bitcast(BF16)
            for n in range(4):
                SX = [B[2], B[3]]
                for par in range(2):
                    for ti in range(2):
                        mm(SX[par], SX[par][:, ti * 128:(ti + 1) * 128], krot[64 * par:64 * par + 64, ti, n * 128:(n + 1) * 128],
                           qrot[64 * par:64 * par + 64, ti, n * 128:(n + 1) * 128], True, True, [krot, qrot])
                    tt(DVE, PTt, PTt[:, par, :], SX[par][:, 0:256], dtab[:, par, :], MUL, [SX[par], dtab])
                po = B[4]
                mm(po, po[:, 0:256], zrow[0:1, 0:128], zrow[0:1, 0:256], True, False, [zrow])
                for par in range(2):
                    for ti in range(2):
                        h = 2 * ti + par
                        mm(po, po[:, h * 64:(h + 1) * 64], PTt[:, par, ti * 128:(ti + 1) * 128], vt[:, n, h * 64:(h + 1) * 64], False, False, [PTt, vt])
                for ti in range(2):
                    mm(po, po[:, ti * 128:(ti + 1) * 128], qxi[:, ti, n * 128:(n + 1) * 128], Rb[:, ti, :], False, ti == 1, [qxi, Rb])
                if 'ret1' in parts: continue
                cp(ACT, osbf, osbf[:], po[:, 0:256], [po])
                o3 = osbf.ap.rearrange("p (h e) -> p h e", h=4); q3 = osq.ap.rearrange("p (h e) -> p h e", h=4)
                k.op(DVE, lambda e: e.tensor_reduce(out=st4[0][:], in_=o3, axis=AX.X, op=ADD), reads=[osbf], writes=[st4[0]])
                act(osq, osq[:], osbf[:], AF.Square, [osbf])
                k.op(DVE, lambda e: e.tensor_reduce(out=st4[1][:], in_=q3, axis=AX.X, op=ADD), reads=[osq], writes=[st4[1]])
                ts(DVE, st4[2], st4[2][:], st4[0][:], 1.0 / 64, None, MUL, None, [st4[0]])
                tt(DVE, st4[3], st4[3][:], st4[2][:], st4[2][:], MUL, [st4[2]])
                stt(st4[3], st4[3][:], st4[1][:], 1.0 / 64, st4[3][:], MUL, SUB, [st4[1], st4[3]])
                act(st4[4], st4[4][:], st4[3][:], AF.Sqrt, [st4[3], misc], bias=eps_ap)
                recip(st4[5], st4[5][:], st4[4][:], [st4[4]])
                for h in range(4):
                    ts(DVE, onb, onb[:, h * 64:(h + 1) * 64], osbf[:, h * 64:(h + 1) * 64], st4[2][:, h:h + 1], st4[5][:, h:h + 1], SUB, MUL, [osbf, st4[2], st4[5]])
                if 'ret2' in parts: continue
                for ti in range(2):
                    transpose(B[5], pv[:, ti * 128:(ti + 1) * 128], onb[:, ti * 128:(ti + 1) * 128], identb, [onb, cmats])
                cp(ACT, yTt[4], yTb[:, 4:6, n * 128:(n + 1) * 128], pv[:, 0:256].rearrange("p (a b) -> p a b", a=2), [B[5]])
                yTt[5].w = yTt[4].w; yTt[5].r = {}
                if 'ret3' in parts: continue
                for ti in range(2):
                    transpose(B[6], pv2[:, ti * 128:(ti + 1) * 128], krot[:, ti, n * 128:(n + 1) * 128], identb, [krot, cmats])
                tt(DVE, kz, kz[:], pv2[:, 0:256], ztab[:], MUL, [B[6], ztab])
                kvb = B[7]
                for ti in range(2):
                    mm(kvb, kvb[:, ti * 128:(ti + 1) * 128], kz[:, ti * 128:(ti + 1) * 128], vt[:, n, ti * 128:(ti + 1) * 128], True, True, [kz, vt])
                tt(DVE, osq, osq.ap.rearrange("p (a b) -> p a b", a=2), kvb[:, 0:256].rearrange("p (a b) -> p a b", a=2),
                   cmats[:, 7:8, :].to_broadcast([128, 2, 128]), MUL, [kvb, cmats])
                for ti in range(2):
                    stt(Rst, Rst[:, ti, :], Rst[:, ti, :], gdec[:, ti:ti + 1], osq[:, ti * 128:(ti + 1) * 128], MUL, ADD, [Rst, gdec, osq])
                cp(POOL, Rb, Rb[:], Rst[:], [Rst])
        if 'ret' in parts: sec_ret()

        def sec_lru():
            for i in range(2):
                xc = LT[0]; r = LT[1]; ig = LT[2]; a_ = LT[3]; h_ = LT[4]
                ts(DVE, xc, xc[:], uL[:, i, 0:TC], pp[:, 20 + 4 * i:21 + 4 * i], pp[:, 28 + i:29 + i], MUL, ADD, [uL, pp])
                for kk in range(1, 4):
                    stt(xc, xc[:], uL[:, i, kk:kk + TC], pp[:, 20 + 4 * i + kk:21 + 4 * i + kk], xc[:], MUL, ADD, [uL, pp, xc])
                cp(POOL, xcb, xcb[:], xc[:], [xc])
                mm(B[6], B[6][:], WA[:, i, :], xcb[:], True, True, [WA, xcb]); mm(B[7], B[7][:], WX[:, i, :], xcb[:], True, True, [WX, xcb])
                act(r, r[:], B[6][:], AF.Sigmoid, [B[6], pp], bias=pp[:, 30 + i:31 + i])
                act(ig, ig[:], B[7][:], AF.Sigmoid, [B[7], pp], bias=pp[:, 32 + i:33 + i])
                act(a_, a_[:], r[:], AF.Exp, [r, cl], scale=cl[:, i:i + 1])
                act(r, r[:], r[:], AF.Exp, [r, cl2], scale=cl2[:, i:i + 1])
                act(r, r[:], r[:], AF.Sqrt, [r, misc], bias=one_ap, scale=-1.0)
                tt(POOL, ig, ig[:], ig[:], xc[:], MUL, [ig, xc]); tt(DVE, r, r[:], r[:], ig[:], MUL, [r, ig])
                k.op(DVE, lambda e: e.tensor_tensor_scan(out=h_[:], data0=a_[:], data1=r[:], initial=hst[:, i:i + 1], op0=MUL, op1=ADD),
                     reads=[a_, r, hst], writes=[h_])
                cp(ACT, hst, hst[:, i:i + 1], h_[:, TC - 1:TC], [h_])
                cp(POOL, yTt[6 + i], yTb[:, 6 + i, :], h_[:], [h_])
                cp(POOL, uL, uL[:, i, 0:3], uL[:, i, TC:TC + 3], [uL])

        if 'lru' in parts: sec_lru()

    def phaseO(l, c):
        xc_ = xTc[c]
        for ct in range(8):
            b = proj_fm(l, 16 + ct)
            act(prodS[ct % 4], prodS[ct % 4][:], b[:], AF.Silu, [b])
            tt(DVE, yTt[ct], yTb[:, ct, :], yTb[:, ct, :], prodS[ct % 4][:], MUL, [yTt[ct], prodS[ct % 4]])
        if 'O1' in parts: return
        for f in range(8):
            wb = WB[f % 2]
            k.dma(POOL, wb[:], wbr_d[l].rearrange("n (k p) d -> p (n k) d", p=128)[:, :, f * 128:(f + 1) * 128], writes=[wb])
            for n in range(4):
                bm = proj_fm(l, 24 + n * 8 + f)
                by = B[2 + n % 2]
                for kk in range(2):
                    mm(by, by[:], wb[:, n * 2 + kk, :], yTb[:, 2 * n + kk, :], kk == 0, kk == 1, [wb, yTt[2 * n + kk]])
                s_ = sg[n % 2]
                act(s_, s_[:], bm[:], AF.Sigmoid, [bm])
                tt(DVE, prodS[n], prodS[n][:], s_[:], by[:], MUL, [s_, by])
            bs = B[4]
            for n in range(4):
                mm(bs, bs[:], identb, prodS[n][:], n == 0, n == 3, [cmats, prodS[n]])
            cp(ACT, merged, merged[:, f, :], bs[:], [bs])
        if 'O2' in parts: return
        for half in range(2):
            cs_ = slice(half * 256, (half + 1) * 256)
            ssb = B[5]
            for e in range(8):
                w = load_w(wout_d[l].rearrange("(k p) c -> p k c", p=128)[:, :, e * 128:(e + 1) * 128])
                b = B[pbn[0] % 2]; pbn[0] += 1
                for d in range(8):
                    mm(b, b[:, 0:256], w[:, d, :], merged[:, d, cs_], d == 0, d == 7, [w, merged])
                cp(DVE, outT, outT[:, e, :], b[:, 0:256], [b])
                if 'x1' in parts: continue
                act(sqb[e % 2], sqb[e % 2][:, 0:256], outT[:, e, :], AF.Square, [outT])
                mm(ssb, ssb[:, 0:256], onesb, sqb[e % 2][:, 0:256], e == 0, e == 7, [cmats, sqb[e % 2]])
            if 'x1' in parts or 'x2' in parts: continue
            act(sg[0], sg[0][:, 0:256], ssb[:, 0:256], AF.Sqrt, [ssb, misc], bias=eps_ap, scale=1.0 / D)
            recip(rstd, rstd[:, 0:256], sg[0][:, 0:256], [sg[0]])
            for e in range(8):
                stt(sg[1], sg[1][:, 0:256], outT[:, e, :], pp[:, 8 + e:9 + e], rstd[:, 0:256], MUL, MUL, [outT, pp, rstd])
                tt(DVE, xc_, xc_[:, e, cs_], xc_[:, e, cs_], sg[1][:, 0:256], ADD, [xc_, sg[1]])

    for sq in range(NSEQ):
        k.barrier(); load_seq(sq)
        for l in range(L):
            k.barrier()
            if 'setup' in parts: layer_setup(l)
            k.barrier()
            if 'prenorm' in parts: prenorm(l, 0)
            k.barrier()
            for c in range(NCH):
                if 'proj' in parts: phaseM(l, c)
                if "yT" in dbg_d and sq == 0 and l == 0:
                    k.dma(POOL, dbg_d["yT"][:, :, c * TC:(c + 1) * TC], yTb[:], reads=yTt)
                k.barrier()
                if 'O' in parts: phaseO(l, c)
                if c + 1 < NCH and 'prenorm' in parts: prenorm(l, c + 1)
                k.barrier()
        store_seq(sq)
    k.barrier()
    import os
    if os.environ.get('KSTAT'): print('ENGINE COUNTS', {E.name: E.cnt for E in k.engs}, 'dma', {E.name: sum(E.dvals)//16 for E in k.engs})
    return nc


def _prep_shared(inp, S, L):
    f32 = np.float32
    g = lambda n: np.asarray(inp[n], dtype=f32)[:L]
    w_in = g('w_in')
    perm = []
    for base in (1024, 1280):
        for h in range(4):
            b0 = base + 64 * h
            perm += list(range(b0 + 32, b0 + 64)) + list(range(b0, b0 + 32))
    win = np.ascontiguousarray(np.concatenate([w_in, w_in[:, :, perm]], axis=2))
    pre_g, post_g = g('pre_norm_g'), g('post_norm_g')
    pp = np.zeros((L, 128, 40), f32)
    for l in range(L):
        pp[l, :, 0:8] = pre_g[l].reshape(8, 128).T
        pp[l, :, 8:16] = post_g[l].reshape(8, 128).T
        pp[l, :, 16:18] = g('ssm_d')[l].reshape(2, 128).T
        pp[l, :, 18:20] = g('ssm_b_glu')[l].reshape(2, 128).T
        cw = g('lru_conv_w')[l]
        for i in range(2):
            pp[l, :, 20 + 4 * i:24 + 4 * i] = cw[:, 128 * i:128 * (i + 1)].T
        pp[l, :, 28:30] = g('lru_conv_b')[l].reshape(2, 128).T
        pp[l, :, 30:32] = g('lru_b_a')[l].reshape(2, 128).T
        pp[l, :, 32:34] = g('lru_b_x')[l].reshape(2, 128).T
        pp[l, :, 34:36] = g('lru_lambda')[l].reshape(2, 128).T
    lruw = np.zeros((L, 2, 128, 2, 128), f32)
    for l in range(L):
        for ax, nm in enumerate(('lru_w_a', 'lru_w_x')):
            w = g(nm)[l]
            for i in range(2):
                for bl in range(2):
                    lruw[l, ax, 64 * bl:64 * bl + 64, i, 64 * bl:64 * bl + 64] = w[2 * i + bl]
    a_re, a_im, ldt = g('ssm_a_re'), g('ssm_a_im'), g('ssm_log_dt')
    s5row = np.stack([a_re.reshape(L, 1024), a_im.reshape(L, 1024), np.repeat(ldt, 64, axis=1)], axis=1)
    s5col = np.zeros((L, 128, 3, 16), f32)
    for l in range(L):
        s5col[l, :, 0, :] = np.concatenate([a_re[l].T, a_re[l].T], axis=0)
        s5col[l, :, 1, :] = np.concatenate([a_im[l].T, a_im[l].T], axis=0)
        s5col[l, :, 2, :] = np.broadcast_to(ldt[l][None, :], (128, 16))
    bblk = np.zeros((L, 2, 128, 2, 512), f32)
    for l in range(L):
        for ri, nm in enumerate(('ssm_b_re', 'ssm_b_im')):
            bb = g(nm)[l]
            for gg in range(16):
                kh, gl = gg // 8, gg % 8
                bblk[l, ri, gl * 16:(gl + 1) * 16, kh, gl * 64:(gl + 1) * 64] = bb[gg].T
    cab = np.zeros((L, 128, 32, 128), f32)
    c_re, c_im = g('ssm_c_re'), g('ssm_c_im')
    for l in range(L):
        for gg in range(16):
            co = 16 * (gg % 8)
            cab[l, 0:64, 2 * gg, co:co + 16] = c_re[l, gg].T
            cab[l, 64:128, 2 * gg, co:co + 16] = c_im[l, gg].T
            cab[l, 0:64, 2 * gg + 1, co:co + 16] = c_im[l, gg].T
            cab[l, 64:128, 2 * gg + 1, co:co + 16] = c_re[l, gg].T
    p = np.arange(128)
    invf = (np.float32(10000.0) ** (-(np.arange(32, dtype=f32) / np.float32(32)))).astype(f32)
    ang = (np.arange(S, dtype=f32)[None, :] * invf[(p % 64) % 32][:, None]).astype(f32)
    ropeC = np.cos(ang).astype(f32)
    ropeS = (np.sin(ang) * np.where((p % 64) < 32, -1.0, 1.0)[:, None]).astype(f32)
    gam = 1.0 - 2.0 ** (-5.0 - np.arange(4))
    ii = np.arange(128)
    dtab = np.zeros((128, 2, 256), f32)
    for par in range(2):
        for ti in range(2):
            h = 2 * ti + par
            rel = ii[None, :] - ii[:, None]
            dtab[:, par, ti * 128:(ti + 1) * 128] = np.where(rel >= 0, gam[h] ** np.maximum(rel, 0), 0.0) / 8.0
    ztab = np.zeros((128, 256), f32)
    for h in range(4):
        ztab[:, h * 64:(h + 1) * 64] = (gam[h] ** (127 - ii) / 8.0)[:, None]
    xitab = np.zeros((128, 2, 128), f32); gdec = np.zeros((128, 2), f32)
    for ti in range(2):
        for hl in range(2):
            h = 2 * ti + hl
            xitab[64 * hl:64 * hl + 64, ti, :] = (gam[h] ** (ii + 1.0))[None, :]
            gdec[64 * hl:64 * hl + 64, ti] = gam[h] ** 128
    cm = np.zeros((128, 8, 128), f32)
    cm[:, 0, :] = np.eye(128)
    cm[:, 1, :] = -1.0 * (ii[:, None] >= ii[None, :])
    cm[:, 2, :] = -1.0
    cm[:, 3, :] = (ii[:, None] <= ii[None, :])
    cm[:, 4, :] = (ii[:, None] < ii[None, :])
    for m in range(64):
        cm[m + 64, 5, m] = -1.0
        cm[m, 5, m + 64] = 1.0
    cm[:, 6, :] = 1.0
    cm[0:64, 7, 0:64] = 1.0; cm[64:128, 7, 64:128] = 1.0
    misc = np.zeros((128, 132), f32)
    misc[:, 0] = p; misc[:, 1] = np.where(p < 64, 1.0, -1.0); misc[:, 2] = EPS; misc[:, 3] = 1.0
    misc[:, 4:132] = np.arange(128)[None, :]
    return dict(win=win, wbr=g('w_branch'), wout=g('w_out'), wglu=g('ssm_w_glu'), pp=pp, lruw=lruw, s5row=np.ascontiguousarray(s5row),
                s5col=s5col, bblk=bblk, cab=cab, ropeC=ropeC, ropeS=ropeS, dtab=dtab, ztab=ztab, xitab=xitab, gdec=gdec,
                cmats=cm, misc=misc)


def run(inp, S, NSEQ, DEPTH, ncores, dbg=None, parts=None):
    shared = _prep_shared(inp, S, DEPTH)
    x = np.asarray(inp['x'], dtype=np.float32)
    nc = build(S, NSEQ, DEPTH, dbg, parts)
    in_maps = []
    for i in range(ncores):
        m = dict(shared); m['x'] = np.ascontiguousarray(x[i * NSEQ:(i + 1) * NSEQ]); in_maps.append(m)
    res = run_bass_kernel_spmd(nc, in_maps, core_ids=list(range(ncores)))
    return res


def kernel(**inputs):
    x = np.asarray(inputs['x'])
    Bn, S, _ = x.shape
    ncores = 8
    NSEQ = Bn // ncores
    res = run(inputs, S, NSEQ, 2, ncores)
    return np.concatenate([r["y"] for r in res.results], axis=0).astype(np.float32)
```

```python
import math
import numpy as np
import concourse.bass as bass
import concourse.mybir as mybir
from concourse.bass_utils import run_bass_kernel_spmd

F32 = mybir.dt.float32; BF16 = mybir.dt.bfloat16; I32 = mybir.dt.int32
AF = mybir.ActivationFunctionType; ALU = mybir.AluOpType; AX = mybir.AxisListType
D = 1024; TC = 512; NBK = 4; EPS = 1e-6
NCOLT = 60
PI = math.pi


class T:
    __slots__ = ('ap', 'w', 'r')
    def __init__(s, ap): s.ap = ap; s.w = None; s.r = {}
    def __getitem__(s, idx): return s.ap[idx]


class Eng:
    def __init__(s, nc, name, eng, is_pe=False):
        s.name = name; s.eng = eng; s.sem = nc.alloc_semaphore("sem_" + name); s.cnt = 0; s.known = {}; s.is_pe = is_pe
        s.dsems = []; s.dvals = []; s.dnext = 0


class K:
    def __init__(s, nc, ndma=(16, 16, 4)):
        s.nc = nc
        s.PE = Eng(nc, "pe", nc.tensor, True); s.ACT = Eng(nc, "act", nc.scalar); s.DVE = Eng(nc, "dve", nc.vector)
        s.POOL = Eng(nc, "pool", nc.gpsimd); s.SP = Eng(nc, "sp", nc.sync)
        s.engs = [s.PE, s.ACT, s.DVE, s.POOL, s.SP]
        for E, n in ((s.SP, ndma[0]), (s.POOL, ndma[1]), (s.ACT, ndma[2])):
            E.dsems = [nc.alloc_semaphore(f"d_{E.name}_{i}") for i in range(n)]; E.dvals = [0] * n
        s.nt = 0
    def sb(s, shape, dt, name=None):
        s.nt += 1
        return T(s.nc.alloc_sbuf_tensor("s_" + (name or f"t{s.nt}"), list(shape), dt).ap())
    def ps(s, shape, dt=F32, name=None):
        s.nt += 1
        return T(s.nc.alloc_psum_tensor("ps_" + (name or f"p{s.nt}"), list(shape), dt).ap())
    def _waits(s, E, reads, writes):
        deps = {}
        def add(ev, war=False):
            if ev is None: return
            key, sem, val, who = ev
            if who is E and (E.is_pe or war): return
            if deps.get(key, (None, 0))[1] < val: deps[key] = (sem, val)
        for t in reads: add(t.w)
        for t in writes:
            add(t.w)
            for ev in t.r.values(): add(ev, True)
        for key, (sem, val) in deps.items():
            if E.known.get(key, 0) >= val: continue
            E.eng.wait_ge(sem, val); E.known[key] = val
    def _post(s, ev, reads, writes):
        for t in writes: t.w = ev; t.r = {}
        for t in reads: t.r[ev[0]] = ev
    def op(s, E, fn, reads=(), writes=()):
        s._waits(E, reads, writes)
        ins = fn(E.eng); E.cnt += 1; ins.then_inc(E.sem, 1)
        s._post((E.name, E.sem, E.cnt, E), reads, writes)
        return ins
    def dma(s, Q, out, in_, reads=(), writes=(), **kw):
        i = Q.dnext; Q.dnext = (i + 1) % len(Q.dsems); sem = Q.dsems[i]; key = f"d_{Q.name}_{i}"
        if Q.dvals[i] > 0 and Q.known.get(key, 0) < Q.dvals[i]:
            Q.eng.wait_ge(sem, Q.dvals[i]); Q.known[key] = Q.dvals[i]
        s._waits(Q, reads, writes)
        ins = Q.eng.dma_start(out=out, in_=in_, **kw); Q.dvals[i] += 16; ins.then_inc(sem, 16)
        ev = (key, sem, Q.dvals[i], None)
        s._post(ev, reads, writes)
        return ev
    def barrier(s):
        for E in s.engs:
            for Fg in s.engs:
                if Fg is E or Fg.cnt == 0: continue
                if E.known.get(Fg.name, 0) < Fg.cnt:
                    E.eng.wait_ge(Fg.sem, Fg.cnt); E.known[Fg.name] = Fg.cnt
            for Q in s.engs:
                for i, sem in enumerate(Q.dsems):
                    key = f"d_{Q.name}_{i}"
                    if Q.dvals[i] > 0 and E.known.get(key, 0) < Q.dvals[i]:
                        E.eng.wait_ge(sem, Q.dvals[i]); E.known[key] = Q.dvals[i]


def build(S, NSEQ, DEPTH, dbg=None, parts=None):
    nc = bass.Bass("TRN2", target_bir_lowering=False)
    k = K(nc)
    if parts is None: parts = {"setup", "prenorm", "proj", "sb", "s5", "ret", "lru", "O"}
    PE, ACT, DVE, POOL, SP = k.PE, k.ACT, k.DVE, k.POOL, k.SP
    NCH = S // TC
    NBLK = S // 128
    L = DEPTH
    def din(name, shape): return nc.dram_tensor(name, list(shape), F32, kind="ExternalInput").ap()
    x_d = din("x", [NSEQ, S, D]); win_d = din("win", [L, D, 7680]); wbr_d = din("wbr", [L, 4, 256, D])
    wout_d = din("wout", [L, D, D]); wglu_d = din("wglu", [L, 256, 256]); pp_d = din("pp", [L, 128, 40])
    lruw_d = din("lruw", [L, 2, 128, 2, 128]); s5row_d = din("s5row", [L, 3, 1024]); s5col_d = din("s5col", [L, 128, 3, 16])
    bblk_d = din("bblk", [L, 2, 128, 2, 512]); cab_d = din("cab", [L, 128, 32, 128])
    ropeC_d = din("ropeC", [128, S]); ropeS_d = din("ropeS", [128, S])
    dtab_d = din("dtab", [128, 2, 256]); ztab_d = din("ztab", [128, 256]); xitab_d = din("xitab", [128, 2, 128])
    gdec_d = din("gdec", [128, 2]); cm_d = din("cmats", [128, 8, 128]); misc_d = din("misc", [128, 132])
    y_d = nc.dram_tensor("y", [NSEQ, S, D], F32, kind="ExternalOutput").ap()
    dbg_d = {}
    if dbg:
        for nm, shp in dbg.items():
            dbg_d[nm] = nc.dram_tensor(nm, list(shp), F32, kind="ExternalOutput").ap()

    xTb = nc.alloc_sbuf_tensor("s_xT", [128, 8, S], F32).ap()
    xTc = [T(xTb[:, :, c * TC:(c + 1) * TC]) for c in range(S // TC)]
    hT = k.sb([128, 8, TC], BF16, "hT")
    WT = [k.sb([128, 8, 128], BF16, f"WT{i}") for i in range(3)]
    wtn = [0]
    pp = k.sb([128, 40], F32, "pp")
    cmats = k.sb([128, 8, 128], BF16, "cmats")
    identb = cmats[:, 0, :]; ntri = cmats[:, 1, :]; nones = cmats[:, 2, :]; tri = cmats[:, 3, :]
    Jm = cmats[:, 5, :]; onesb = cmats[:, 6, :]
    identf = k.sb([128, 128], F32, "identf")
    misc = k.sb([128, 132], F32, "misc")
    mask2 = k.sb([128, 256], BF16, "mask2")
    zrow = k.sb([1, 256], BF16, "zrow")
    ropeC = k.sb([128, TC], BF16, "ropeC"); ropeS = k.sb([128, TC], BF16, "ropeS")
    dtab = k.sb([128, 2, 256], BF16, "dtab"); ztab = k.sb([128, 256], BF16, "ztab"); xitab = k.sb([128, 2, 128], BF16, "xitab")
    gdec = k.sb([128, 2], F32, "gdec")
    KT = k.sb([128, 2, S], BF16, "KT"); Vc = k.sb([128, NBLK, 256], BF16, "Vc"); QT = k.sb([128, 2, TC], BF16, "QT")
    sbE = [k.sb([128, 256], F32, f"sbE{i}") for i in range(2)]
    sbSP = [k.sb([128, 256], BF16, f"sbSP{i}") for i in range(2)]
    sbW = [k.sb([128, 256], BF16, f"sbW{i}") for i in range(2)]
    sbC = [k.sb([128, 256], BF16, f"sbC{i}") for i in range(2)]
    osb = k.sb([128, 256], BF16, "osb")
    uT = k.sb([128, 2, TC], BF16, "uT")
    Bx = [k.sb([128, 1024], BF16, f"Bx{i}") for i in range(2)]; Bsw = [k.sb([128, 1024], BF16, f"Bsw{i}") for i in range(2)]
    Pr = k.sb([128, 16, 64], BF16, "Pr"); Pi = k.sb([128, 16, 64], BF16, "Pi")
    T1 = k.sb([128, 16, 128], BF16, "T1"); T2 = k.sb([128, 16, 128], BF16, "T2")
    W1 = k.sb([128, 16], F32, "W1"); W2 = k.sb([128, 16], F32, "W2")
    Cm = k.sb([128, 32, 128], BF16, "Cm")
    G1 = [k.sb([128, 4, 128], BF16, f"G1_{i}") for i in range(2)]; G2 = [k.sb([128, 4, 128], BF16, f"G2_{i}") for i in range(2)]
    A1 = [k.sb([128, 4, 128], BF16, f"A1_{i}") for i in range(2)]; A2 = [k.sb([128, 4, 128], BF16, f"A2_{i}") for i in range(2)]
    carry = [k.sb([128, 4], F32, f"carry{i}") for i in range(4)]
    sfull = k.sb([128, 4], F32, "sfull"); U1 = k.sb([128, 4], BF16, "U1"); U2 = k.sb([128, 4], BF16, "U2")
    ygb = k.sb([128, 2, TC], BF16, "ygb"); wglu = k.sb([128, 2, 256], BF16, "wglu")
    uL = k.sb([128, 2, TC + 4], F32, "uL"); hst = k.sb([128, 2], F32, "hst")
    WA = k.sb([128, 2, 128], BF16, "WA"); WX = k.sb([128, 2, 128], BF16, "WX")
    cl = k.sb([128, 2], F32, "cl"); cl2 = k.sb([128, 2], F32, "cl2"); cltmp = k.sb([128, 2], F32, "cltmp")
    Rst = k.sb([128, 2, 128], F32, "Rst"); Rb = k.sb([128, 2, 128], BF16, "Rb")
    st4 = [k.sb([128, 4], F32, f"st4_{i}") for i in range(6)]
    yTb = nc.alloc_sbuf_tensor("s_yT", [128, 8, TC], BF16).ap()
    yTt = [T(yTb[:, i, :]) for i in range(8)]
    WB = [k.sb([128, 8, 128], BF16, f"WB{i}") for i in range(2)]
    colp = k.sb([128, 3, 16], F32, "colp"); colq = k.sb([128, 3, 16], F32, "colq")
    UN = nc.alloc_sbuf_tensor("UN", [128, 7168], F32).ap()
    def carve(off_f32, n_f32, dt, shape):
        v = UN[:, off_f32:off_f32 + n_f32]
        if dt is BF16: v = v.bitcast(BF16)
        if len(shape) == 3: v = v.rearrange("p (a b) -> p a b", a=shape[1])
        return T(v)
    LT = [carve(512 * i, 512, F32, [128, 512]) for i in range(5)]
    xcb = carve(2560, 256, BF16, [128, 512])
    qrot = carve(2816, 512, BF16, [128, 2, 512]); krot = carve(3328, 512, BF16, [128, 2, 512])
    qxi = carve(3840, 512, BF16, [128, 2, 512])
    vt = carve(4352, 512, BF16, [128, 4, 256]); kz = carve(4864, 128, BF16, [128, 256])
    PTt = carve(4992, 256, BF16, [128, 2, 256])
    osbf = carve(5248, 256, F32, [128, 256]); osq = carve(5504, 256, F32, [128, 256]); onb = carve(5760, 128, BF16, [128, 256])
    rt1 = carve(5888, 512, F32, [128, 512]); rt2 = carve(6400, 512, F32, [128, 512])
    merged = carve(0, 2048, BF16, [128, 8, 512])
    prodS = [carve(2048 + 256 * i, 256, BF16, [128, 512]) for i in range(4)]
    sg = [carve(3072, 512, F32, [128, 512]), carve(3584, 512, F32, [128, 512])]
    outT = carve(4096, 2048, F32, [128, 8, 256])
    stg = [carve(4096, 1024, F32, [128, 1024]), carve(5120, 1024, F32, [128, 1024])]
    sqb = [carve(6144, 256, BF16, [128, 512]), carve(6400, 256, BF16, [128, 512])]
    rstd = carve(6656, 512, F32, [128, 512])
    SRt = [carve(512 * i, 512, F32, [128, 512]) for i in range(14)]
    B = [k.ps([128, 512], F32, f"bank{i}") for i in range(8)]

    def mm(out_t, out_ap, lhsT, rhs, start, stop, reads):
        k.op(PE, lambda e: e.matmul(out_ap, lhsT=lhsT, rhs=rhs, start=start, stop=stop), reads=reads, writes=[out_t])
    def act(out_t, out_ap, in_ap, func, reads, bias=None, scale=None):
        kw = {}
        if bias is not None: kw['bias'] = bias
        if scale is not None: kw['scale'] = scale
        k.op(ACT, lambda e: e.activation(out=out_ap, in_=in_ap, func=func, **kw), reads=reads, writes=[out_t])
    def tt(E, out_t, out_ap, a, b, op, reads):
        k.op(E, lambda e: e.tensor_tensor(out=out_ap, in0=a, in1=b, op=op), reads=reads, writes=[out_t])
    def ts(E, out_t, out_ap, a, s1, s2, op0, op1, reads):
        if op1 is None:
            k.op(E, lambda e: e.tensor_scalar(out=out_ap, in0=a, scalar1=s1, scalar2=None, op0=op0), reads=reads, writes=[out_t])
        else:
            k.op(E, lambda e: e.tensor_scalar(out=out_ap, in0=a, scalar1=s1, scalar2=s2, op0=op0, op1=op1), reads=reads, writes=[out_t])
    def stt(out_t, out_ap, a, sc, b, op0, op1, reads):
        k.op(DVE, lambda e: e.scalar_tensor_tensor(out=out_ap, in0=a, scalar=sc, in1=b, op0=op0, op1=op1), reads=reads, writes=[out_t])
    def cp(E, out_t, out_ap, in_ap, reads):
        if E is ACT:
            k.op(E, lambda e: e.copy(out=out_ap, in_=in_ap), reads=reads, writes=[out_t])
        else:
            k.op(E, lambda e: e.tensor_copy(out=out_ap, in_=in_ap), reads=reads, writes=[out_t])
    def memset(E, t, ap, v):
        k.op(E, lambda e: e.memset(ap, v), writes=[t])
    def dbg_out(name, t, ap):
        if name in dbg_d:
            k.dma(SP, dbg_d[name], ap, reads=[t])

    def load_w(src_ap):
        w = WT[wtn[0] % 3]; wtn[0] += 1
        k.dma(POOL, w[:], src_ap, writes=[w])
        return w
    def win_tile(l, ct):
        return win_d[l].rearrange("(k p) c -> p k c", p=128)[:, :, ct * 128:(ct + 1) * 128]
    pbn = [0]
    def proj_fm(l, ct, bank=None):
        w = load_w(win_tile(l, ct))
        if bank is None:
            b = B[pbn[0] % 2]; pbn[0] += 1
        else:
            b = bank
        for kk in range(8):
            mm(b, b[:], w[:, kk, :], hT[:, kk, :], kk == 0, kk == 7, [w, hT])
        return b
    def proj_tm(l, ct0, dst_t, dst_fn, banks=None):
        w0 = load_w(win_tile(l, ct0)); w1 = load_w(win_tile(l, ct0 + 1))
        for half in range(2):
            if banks is None:
                b = B[pbn[0] % 2]; pbn[0] += 1
            else:
                b = banks[half % len(banks)]
            for bi in range(2):
                blk = half * 2 + bi
                for j, w in enumerate((w0, w1)):
                    o = bi * 256 + j * 128
                    for kk in range(8):
                        mm(b, b[:, o:o + 128], hT[:, kk, blk * 128:(blk + 1) * 128], w[:, kk, :], kk == 0, kk == 7, [w, hT])
            for bi in range(2):
                blk = half * 2 + bi
                cp(ACT, dst_t, dst_fn(blk), b[:, bi * 256:(bi + 1) * 256], [b])

    def sincos(ang, vw, tmps):
        ki, kr, sn, cs = tmps
        kiv = vw(ki).bitcast(I32)
        ts(DVE, kr, vw(kr), vw(ang), 1.0 / (2 * PI), None, ALU.mult, None, [ang])
        cp(DVE, ki, kiv, vw(kr), [kr])
        cp(DVE, kr, vw(kr), kiv, [ki])
        C1 = 6.28125; C2 = 2 * PI - 6.28125
        stt(sn, vw(sn), vw(kr), -C1, vw(ang), ALU.mult, ALU.add, [kr, ang])
        stt(sn, vw(sn), vw(kr), -C2, vw(sn), ALU.mult, ALU.add, [kr, sn])
        ts(DVE, sn, vw(sn), vw(sn), -PI, PI, ALU.max, ALU.min, [sn])
        ts(DVE, cs, vw(cs), vw(sn), PI / 2, -2 * PI, ALU.is_gt, ALU.mult, [sn])
        stt(cs, vw(cs), vw(sn), PI / 2, vw(cs), ALU.add, ALU.add, [sn, cs])
        ts(DVE, cs, vw(cs), vw(cs), -PI, PI, ALU.max, ALU.min, [cs])
        act(cs, vw(cs), vw(cs), AF.Sin, [cs])
        act(sn, vw(sn), vw(sn), AF.Sin, [sn])

    k.dma(POOL, cmats[:], cm_d, writes=[cmats])
    k.dma(SP, misc[:], misc_d, writes=[misc])
    k.dma(POOL, dtab[:], dtab_d, writes=[dtab]); k.dma(POOL, ztab[:], ztab_d, writes=[ztab]); k.dma(POOL, xitab[:], xitab_d, writes=[xitab])
    k.dma(SP, gdec[:], gdec_d, writes=[gdec])
    memset(POOL, zrow, zrow[:], 0.0)
    for i in range(2):
        cp(POOL, mask2, mask2[:, i * 128:(i + 1) * 128], cmats[:, 4, :], [cmats])
    cp(DVE, identf, identf[:], cmats[:, 0, :], [cmats])
    tauc = misc[:, 0:1]; sgn1 = misc[:, 1:2]; eps_ap = misc[:, 2:3]; one_ap = misc[:, 3:4]; taurow = misc[:, 4:132]


    flat = lambda t: t.ap
    v3 = lambda t: t.ap.rearrange("p (g n) -> p g n", g=8)
    v4 = lambda t: t.ap.rearrange("p (g n) -> p g n", g=4)
    sm = lambda t: t.ap[:, 0:16]
    MUL, ADD, SUB = ALU.mult, ALU.add, ALU.subtract

    def recip(out_t, out_ap, in_ap, reads):
        k.op(DVE, lambda e: e.reciprocal(out=out_ap, in_=in_ap), reads=reads, writes=[out_t])
    def transpose(out_t, out_ap, in_ap, ident_ap, reads):
        k.op(PE, lambda e: e.transpose(out=out_ap, in_=in_ap, identity=ident_ap), reads=reads, writes=[out_t])

    def layer_setup(l):
        k.dma(SP, pp[:], pp_d[l], writes=[pp])
        k.dma(POOL, WA[:], lruw_d[l, 0], writes=[WA]); k.dma(POOL, WX[:], lruw_d[l, 1], writes=[WX])
        k.dma(POOL, wglu[:], wglu_d[l].rearrange("(k p) c -> p k c", p=128), writes=[wglu])
        k.dma(POOL, Cm[:], cab_d[l], writes=[Cm])
        k.dma(SP, colp[:], s5col_d[l], writes=[colp])
        act(cltmp, cltmp[:], pp[:, 34:36], AF.Exp, [pp], scale=-1.0)
        act(cltmp, cltmp[:], cltmp[:], AF.Ln, [cltmp, misc], bias=one_ap)
        ts(DVE, cl, cl[:], cltmp[:], -8.0, None, MUL, None, [cltmp])
        ts(DVE, cl2, cl2[:], cltmp[:], -16.0, None, MUL, None, [cltmp])
        memset(POOL, uL, uL[:, :, 0:4], 0.0); memset(POOL, hst, hst[:], 0.0)
        memset(POOL, Rst, Rst[:], 0.0); memset(POOL, Rb, Rb[:], 0.0)
        for q in range(4): memset(POOL, carry[q], carry[q][:], 0.0)
        act(colq, colq[:, 0, :], colp[:, 2, :], AF.Exp, [colp])
        tt(DVE, colq, colq[:, 1, :], colq[:, 0, :], colp[:, 0, :], MUL, [colq, colp])
        tt(DVE, colq, colq[:, 2, :], colq[:, 0, :], colp[:, 1, :], MUL, [colq, colp])
        for kh in range(2):
            are, aim, dt_, dre, dim_, mag, ki, kr, sn, cs, fr, fi, bre, bim = SRt
            for i, t in enumerate((are, aim, dt_)):
                k.dma(SP, t[:], bass.AP(s5row_d.tensor, (l * 3 + i) * 1024 + kh * 512, [[0, 128], [1, 512]]), writes=[t])
            k.dma(SP, bre[:], bblk_d[l, 0][:, kh, :], writes=[bre]); k.dma(SP, bim[:], bblk_d[l, 1][:, kh, :], writes=[bim])
            act(dt_, dt_[:], dt_[:], AF.Exp, [dt_])
            tt(DVE, dre, dre[:], dt_[:], are[:], MUL, [dt_, are]); tt(DVE, dim_, dim_[:], dt_[:], aim[:], MUL, [dt_, aim])
            act(mag, mag[:], dre[:], AF.Exp, [dre])
            sincos(dim_, flat, (ki, kr, sn, cs))
            tt(DVE, cs, cs[:], mag[:], cs[:], MUL, [mag, cs]); tt(DVE, sn, sn[:], mag[:], sn[:], MUL, [mag, sn])
            ts(DVE, cs, cs[:], cs[:], -1.0, None, ADD, None, [cs])
            tt(DVE, mag, mag[:], are[:], are[:], MUL, [are]); tt(DVE, ki, ki[:], aim[:], aim[:], MUL, [aim])
            tt(DVE, mag, mag[:], mag[:], ki[:], ADD, [mag, ki]); recip(mag, mag[:], mag[:], [mag])
            tt(DVE, fr, fr[:], cs[:], are[:], MUL, [cs, are]); tt(DVE, ki, ki[:], sn[:], aim[:], MUL, [sn, aim])
            tt(DVE, fr, fr[:], fr[:], ki[:], ADD, [fr, ki]); tt(DVE, fr, fr[:], fr[:], mag[:], MUL, [fr, mag])
            tt(DVE, fi, fi[:], sn[:], are[:], MUL, [sn, are]); tt(DVE, ki, ki[:], cs[:], aim[:], MUL, [cs, aim])
            tt(DVE, fi, fi[:], fi[:], ki[:], SUB, [fi, ki]); tt(DVE, fi, fi[:], fi[:], mag[:], MUL, [fi, mag])
            tt(DVE, kr, kr[:], fr[:], bre[:], MUL, [fr, bre]); tt(DVE, ki, ki[:], fi[:], bim[:], MUL, [fi, bim])
            tt(DVE, sn, sn[:], fr[:], bim[:], MUL, [fr, bim]); tt(DVE, cs, cs[:], fi[:], bre[:], MUL, [fi, bre])
            bxv = Bx[kh].ap.rearrange("p (g a n) -> p g a n", g=8, a=2)
            bsv = Bsw[kh].ap.rearrange("p (g a n) -> p g a n", g=8, a=2)
            tt(DVE, Bx[kh], bxv[:, :, 0, :], v3(kr), v3(ki), SUB, [kr, ki])
            tt(DVE, Bsw[kh], bsv[:, :, 1, :], v3(kr), v3(ki), SUB, [kr, ki])
            tt(DVE, Bx[kh], bxv[:, :, 1, :], v3(sn), v3(cs), ADD, [sn, cs])
            stt(Bsw[kh], bsv[:, :, 0, :], v3(sn), -1.0, v3(cs), MUL, SUB, [sn, cs])
            ts(DVE, mag, mag[:], dim_[:], tauc, None, MUL, None, [dim_, misc])
            sincos(mag, flat, (ki, kr, sn, cs))
            ts(DVE, fr, fr[:], dre[:], tauc, None, MUL, None, [dre, misc]); act(fr, fr[:], fr[:], AF.Exp, [fr], scale=-1.0)
            tt(DVE, Pr, Pr[:, 8 * kh:8 * kh + 8, :], v3(fr), v3(cs), MUL, [fr, cs])
            stt(Pi, Pi[:, 8 * kh:8 * kh + 8, :], v3(fr), -1.0, v3(sn), MUL, MUL, [fr, sn])
        for q in range(4):
            ang, mexp, ki, kr, sn, cs = SRt[0:6]
            taub = taurow.unsqueeze(1).to_broadcast([128, 4, 128])
            dimb = colq[:, 2, 4 * q:4 * q + 4].unsqueeze(2).to_broadcast([128, 4, 128])
            dreb = colq[:, 1, 4 * q:4 * q + 4].unsqueeze(2).to_broadcast([128, 4, 128])
            tt(DVE, ang, v4(ang), dimb, taub, MUL, [colq, misc])
            tt(DVE, mexp, v4(mexp), dreb, taub, MUL, [colq, misc]); act(mexp, mexp[:], mexp[:], AF.Exp, [mexp])
            sincos(ang, flat, (ki, kr, sn, cs))
            stt(T1, T1[:, 4 * q:4 * q + 4, :], v4(mexp), sgn1, v4(cs), MUL, MUL, [mexp, misc, cs])
            stt(T2, T2[:, 4 * q:4 * q + 4, :], v4(mexp), -1.0, v4(sn), MUL, MUL, [mexp, sn])
        angw, mw, ki, kr, sn, cs = SRt[6:12]
        ts(DVE, angw, sm(angw), colq[:, 2, :], 128.0, None, MUL, None, [colq])
        ts(DVE, mw, sm(mw), colq[:, 1, :], 128.0, None, MUL, None, [colq]); act(mw, sm(mw), sm(mw), AF.Exp, [mw])
        sincos(angw, sm, (ki, kr, sn, cs))
        tt(DVE, W1, W1[:], sm(mw), sm(cs), MUL, [mw, cs]); tt(DVE, W2, W2[:], sm(mw), sm(sn), MUL, [mw, sn])

    def load_seq(sq):
        for blk in range(NBLK):
            st = stg[blk % 2]; c = blk // 4; o = (blk % 4) * 128
            k.dma(SP, st[:], x_d[sq, blk * 128:(blk + 1) * 128, :], writes=[st])
            for half in range(2):
                b = B[2 + half]
                for j in range(4):
                    f = half * 4 + j
                    transpose(b, b[:, j * 128:(j + 1) * 128], st[:, f * 128:(f + 1) * 128], identf[:], [st, identf])
                cp(DVE if half == 0 else ACT, xTc[c], xTc[c][:, half * 4:(half + 1) * 4, o:o + 128],
                   b[:].rearrange("p (a b) -> p a b", a=4), [b])

    def store_seq(sq):
        for blk in range(NBLK):
            st = stg[blk % 2]; c = blk // 4; o = (blk % 4) * 128
            for half in range(2):
                b = B[2 + half]
                for j in range(4):
                    f = half * 4 + j
                    transpose(b, b[:, j * 128:(j + 1) * 128], xTc[c][:, f, o:o + 128], identf[:], [xTc[c], identf])
                cp(DVE if half == 0 else ACT, st, st[:, half * 512:(half + 1) * 512], b[:], [b])
            k.dma(SP, y_d[sq, blk * 128:(blk + 1) * 128, :], st[:], reads=[st])

    def prenorm(l, c):
        xc_ = xTc[c]; ssb = B[5]
        for f in range(8):
            s_ = sqb[f % 2]
            act(s_, s_[:], xc_[:, f, :], AF.Square, [xc_])
            mm(ssb, ssb[:], onesb, s_[:], f == 0, f == 7, [cmats, s_])
        act(sg[0], sg[0][:], ssb[:], AF.Sqrt, [ssb, misc], bias=eps_ap, scale=1.0 / D)
        recip(rstd, rstd[:], sg[0][:], [sg[0]])
        for f in range(8):
            stt(hT, hT[:, f, :], xc_[:, f, :], pp[:, f:f + 1], rstd[:], MUL, MUL, [xc_, pp, rstd])

    def phaseM(l, c):
        t0 = c * TC
        k.dma(POOL, ropeC[:], ropeC_d[:, t0:t0 + TC], writes=[ropeC]); k.dma(POOL, ropeS[:], ropeS_d[:, t0:t0 + TC], writes=[ropeS])
        def sec_sb():
            Z = [B[2], B[3]]; PO = B[4]; PTb = B[2]; pv = PTb.ap.bitcast(BF16)
            for i in range(2):
                b = proj_fm(l, i, B[2 + i]); act(QT, QT[:, i, :], b[:], AF.Copy, [b], scale=0.125)
            for i in range(2):
                b = proj_fm(l, 2 + i, B[2 + i]); cp(DVE, KT, KT[:, i, t0:t0 + TC], b[:], [b])
            yield
            proj_tm(l, 4, Vc, lambda blk: Vc[:, c * 4 + blk, :], [B[2], B[3]])
            yield
            for qi in range(4):
                qb = c * 4 + qi
                mm(PO, PO[:, 0:256], zrow[0:1, 0:128], zrow[0:1, 0:256], True, False, [zrow])
                for a in range(qb, -1, -1):
                    diag = (a == qb)
                    for par in range(2):
                        z = Z[par]
                        mm(z, z[:, 0:256], zrow[0:1, 0:128], zrow[0:1, 0:256], True, False, [zrow])
                        for ti in range(2):
                            mm(z, z[:, ti * 128:(ti + 1) * 128], KT[64 * par:64 * par + 64, ti, a * 128:(a + 1) * 128],
                               QT[64 * par:64 * par + 64, ti, qi * 128:(qi + 1) * 128], False, False, [KT, QT])
                    for par in range(2):
                        z = Z[par]; e_ = sbE[par]; sp_ = sbSP[par]; w_ = sbW[par]; c_ = sbC[par]
                        act(e_, e_[:], z[:, 0:256], AF.Exp, [z])
                        act(sp_, sp_[:], e_[:], AF.Ln, [e_, misc], bias=one_ap)
                        if diag: tt(POOL, sp_, sp_[:], sp_[:], mask2[:], MUL, [sp_, mask2])
                        mm(z, z[:, 0:256], ntri, sp_[:], False, diag, [cmats, sp_])
                        if not diag: mm(z, z[:, 0:256], nones, c_[:], False, True, [cmats, c_])
                        act(w_, w_[:], z[:, 0:256], AF.Exp, [z])
                        if diag: tt(POOL, w_, w_[:], w_[:], mask2[:], MUL, [w_, mask2])
                        if a > 0:
                            if diag: cp(POOL, c_, c_[:], sp_[:], [sp_])
                            else: tt(POOL, c_, c_[:], c_[:], sp_[:], ADD, [c_, sp_])
                        for ti in range(2):
                            h = 2 * ti + par
                            mm(PO, PO[:, h * 64:(h + 1) * 64], w_[:, ti * 128:(ti + 1) * 128], Vc[:, a, h * 64:(h + 1) * 64],
                               False, (a == 0 and par == 1 and ti == 1), [w_, Vc])
                    yield
                cp(ACT, osb, osb[:], PO[:, 0:256], [PO])
                for ti in range(2):
                    transpose(PTb, pv[:, ti * 128:(ti + 1) * 128], osb[:, ti * 128:(ti + 1) * 128], identb, [osb, cmats])
                cp(DVE, yTt[0], yTb[:, 0:2, qi * 128:(qi + 1) * 128], pv[:, 0:256].rearrange("p (a b) -> p a b", a=2), [PTb])
                yTt[1].w = yTt[0].w; yTt[1].r = {}


        def sec_s5():
            Yb = B[1]
            for i in range(2):
                b = proj_fm(l, 6 + i, B[6 + i]); cp(ACT, uT, uT[:, i, :], b[:], [b])
            yield
            x4 = lambda b: b.ap.rearrange("p (g a n) -> p g a n", g=4, a=2)
            g4 = lambda t: t.ap.rearrange("p g (a n) -> p g a n", a=2)
            for sc in range(4):
                for q in range(4):
                    kh = q // 2; hq = q % 2; st_ = (sc * 4 + q) % 2
                    mm(B[6], B[6][:], uT[:, kh, sc * 128:(sc + 1) * 128], Bx[kh][:, hq * 512:(hq + 1) * 512], True, True, [uT, Bx[kh]])
                    mm(B[7], B[7][:], uT[:, kh, sc * 128:(sc + 1) * 128], Bsw[kh][:, hq * 512:(hq + 1) * 512], True, True, [uT, Bsw[kh]])
                    prb = Pr[:, 4 * q:4 * q + 4, :].unsqueeze(2).to_broadcast([128, 4, 2, 64])
                    pib = Pi[:, 4 * q:4 * q + 4, :].unsqueeze(2).to_broadcast([128, 4, 2, 64])
                    tt(DVE, G1[st_], g4(G1[st_]), x4(B[6]), prb, MUL, [B[6], Pr])
                    tt(DVE, G2[st_], g4(G2[st_]), x4(B[7]), pib, MUL, [B[7], Pi])
                    for gl in range(4):
                        mm(B[0], B[0][:, gl * 128:(gl + 1) * 128], G1[st_][:, gl, :], tri, True, False, [G1[st_], cmats])
                        mm(B[0], B[0][:, gl * 128:(gl + 1) * 128], G2[st_][:, gl, :], tri, False, True, [G2[st_], cmats])
                    for gl in range(4):
                        g = 4 * q + gl
                        stt(A1[st_], A1[st_][:, gl, :], B[0][:, gl * 128:(gl + 1) * 128], carry[q][:, gl:gl + 1], T1[:, g, :], ADD, MUL, [B[0], carry[q], T1])
                        stt(A2[st_], A2[st_][:, gl, :], B[0][:, gl * 128:(gl + 1) * 128], carry[q][:, gl:gl + 1], T2[:, g, :], ADD, MUL, [B[0], carry[q], T2])
                    yb = Yb
                    for gl in range(4):
                        g = 4 * q + gl
                        mm(yb, yb[:, kh * 128:(kh + 1) * 128], Cm[:, 2 * g, :], A1[st_][:, gl, :], (hq == 0 and gl == 0), False, [Cm, A1[st_]])
                        mm(yb, yb[:, kh * 128:(kh + 1) * 128], Cm[:, 2 * g + 1, :], A2[st_][:, gl, :], False, (hq == 1 and gl == 3), [Cm, A2[st_]])
                    if hq == 1:
                        cp(ACT, LT[kh], LT[kh][:, sc * 128:(sc + 1) * 128], yb[:, kh * 128:(kh + 1) * 128], [yb])
                    slast = B[0][:].rearrange("p (g t) -> p g t", g=4)[:, :, 127:128].rearrange("p g o -> p (g o)")
                    tt(DVE, sfull, sfull[:], slast, carry[q][:], ADD, [B[0], carry[q]])
                    tt(DVE, U1, U1[:], sfull[:], W1[:, 4 * q:4 * q + 4], MUL, [sfull, W1])
                    tt(DVE, U2, U2[:], sfull[:], W2[:, 4 * q:4 * q + 4], MUL, [sfull, W2])
                    cb = B[6]
                    mm(cb, cb[:, 0:4], identb, U1[:], True, False, [cmats, U1]); mm(cb, cb[:, 0:4], Jm, U2[:], False, True, [cmats, U2])
                    cp(ACT, carry[q], carry[q][:], cb[:, 0:4], [cb])
                    yield
            for kh in range(2):
                yv = LT[kh]; x2 = LT[2]
                stt(yv, yv[:], uT[:, kh, :], pp[:, 16 + kh:17 + kh], yv[:], MUL, ADD, [uT, pp, yv])
                act(x2, x2[:], yv[:], AF.Square, [yv])
                ts(DVE, x2, x2[:], x2[:], 0.044715, 1.0, MUL, ADD, [x2])
                tt(DVE, x2, x2[:], x2[:], yv[:], MUL, [x2, yv])
                act(x2, x2[:], x2[:], AF.Sigmoid, [x2], scale=1.5957691216057308)
                tt(DVE, yv, yv[:], yv[:], x2[:], MUL, [yv, x2])
                cp(POOL, ygb, ygb[:, kh, :], yv[:], [yv])
            for e in range(2):
                b = B[6 + e]
                for kh in range(2):
                    mm(b, b[:], wglu[:, kh, e * 128:(e + 1) * 128], ygb[:, kh, :], kh == 0, kh == 1, [wglu, ygb])
                act(LT[2 + e], LT[2 + e][:], b[:], AF.Sigmoid, [b, pp], bias=pp[:, 18 + e:19 + e])
                tt(DVE, yTt[2 + e], yTb[:, 2 + e, :], LT[e][:], LT[2 + e][:], MUL, [LT[e], LT[2 + e]])


        def sec_ret():
            for ti in range(2):
                tt(DVE, qxi, qxi[:, ti, :].rearrange("p (n i) -> p n i", n=4), qrot[:, ti, :].rearrange("p (n i) -> p n i", n=4),
                   xitab[:, ti, :].unsqueeze(1).to_broadcast([128, 4, 128]), MUL, [qrot, xitab])
            pv2 = B[6].ap.bitcast(BF16); pv = B[5].ap.bitcast(BF16)
            for n in range(4):
                SX = [B[2], B[3]]
                for par in range(2):
                    for ti in range(2):
                        mm(SX[par], SX[par][:, ti * 128:(ti + 1) * 128], krot[64 * par:64 * par + 64, ti, n * 128:(n + 1) * 128],
                           qrot[64 * par:64 * par + 64, ti, n * 128:(n + 1) * 128], True, True, [krot, qrot])
                    tt(DVE, PTt, PTt[:, par, :], SX[par][:, 0:256], dtab[:, par, :], MUL, [SX[par], dtab])
                po = B[4]
                mm(po, po[:, 0:256], zrow[0:1, 0:128], zrow[0:1, 0:256], True, False, [zrow])
                for par in range(2):
                    for ti in range(2):
                        h = 2 * ti + par
                        mm(po, po[:, h * 64:(h + 1) * 64], PTt[:, par, ti * 128:(ti + 1) * 128], vt[:, n, h * 64:(h + 1) * 64], False, False, [PTt, vt])
                for ti in range(2):
                    mm(po, po[:, ti * 128:(ti + 1) * 128], qxi[:, ti, n * 128:(n + 1) * 128], Rb[:, ti, :], False, ti == 1, [qxi, Rb])
                if 'ret1' in parts: continue
                cp(ACT, osbf, osbf[:], po[:, 0:256], [po])
                o3 = osbf.ap.rearrange("p (h e) -> p h e", h=4); q3 = osq.ap.rearrange("p (h e) -> p h e", h=4)
                k.op(DVE, lambda e: e.tensor_reduce(out=st4[0][:], in_=o3, axis=AX.X, op=ADD), reads=[osbf], writes=[st4[0]])
                act(osq, osq[:], osbf[:], AF.Square, [osbf])
                k.op(DVE, lambda e: e.tensor_reduce(out=st4[1][:], in_=q3, axis=AX.X, op=ADD), reads=[osq], writes=[st4[1]])
                ts(DVE, st4[2], st4[2][:], st4[0][:], 1.0 / 64, None, MUL, None, [st4[0]])
                tt(DVE, st4[3], st4[3][:], st4[2][:], st4[2][:], MUL, [st4[2]])
                stt(st4[3], st4[3][:], st4[1][:], 1.0 / 64, st4[3][:], MUL, SUB, [st4[1], st4[3]])
                act(st4[4], st4[4][:], st4[3][:], AF.Sqrt, [st4[3], misc], bias=eps_ap)
                recip(st4[5], st4[5][:], st4[4][:], [st4[4]])
                for h in range(4):
                    ts(DVE, onb, onb[:, h * 64:(h + 1) * 64], osbf[:, h * 64:(h + 1) * 64], st4[2][:, h:h + 1], st4[5][:, h:h + 1], SUB, MUL, [osbf, st4[2], st4[5]])
                if 'ret2' in parts: continue
                for ti in range(2):
                    transpose(B[5], pv[:, ti * 128:(ti + 1) * 128], onb[:, ti * 128:(ti + 1) * 128], identb, [onb, cmats])
                cp(ACT, yTt[4], yTb[:, 4:6, n * 128:(n + 1) * 128], pv[:, 0:256].rearrange("p (a b) -> p a b", a=2), [B[5]])
                yTt[5].w = yTt[4].w; yTt[5].r = {}
                if 'ret3' in parts: continue
                for ti in range(2):
                    transpose(B[6], pv2[:, ti * 128:(ti + 1) * 128], krot[:, ti, n * 128:(n + 1) * 128], identb, [krot, cmats])
                tt(DVE, kz, kz[:], pv2[:, 0:256], ztab[:], MUL, [B[6], ztab])
                kvb = B[7]
                for ti in range(2):
                    mm(kvb, kvb[:, ti * 128:(ti + 1) * 128], kz[:, ti * 128:(ti + 1) * 128], vt[:, n, ti * 128:(ti + 1) * 128], True, True, [kz, vt])
                tt(DVE, osq, osq.ap.rearrange("p (a b) -> p a b", a=2), kvb[:, 0:256].rearrange("p (a b) -> p a b", a=2),
                   cmats[:, 7:8, :].to_broadcast([128, 2, 128]), MUL, [kvb, cmats])
                for ti in range(2):
                    stt(Rst, Rst[:, ti, :], Rst[:, ti, :], gdec[:, ti:ti + 1], osq[:, ti * 128:(ti + 1) * 128], MUL, ADD, [Rst, gdec, osq])
                cp(POOL, Rb, Rb[:], Rst[:], [Rst])
                yield

        def sec_lru():
            for i in range(2):
                xc = LT[0]; r = LT[1]; ig = LT[2]; a_ = LT[3]; h_ = LT[4]
                ts(DVE, xc, xc[:], uL[:, i, 0:TC], pp[:, 20 + 4 * i:21 + 4 * i], pp[:, 28 + i:29 + i], MUL, ADD, [uL, pp])
                for kk in range(1, 4):
                    stt(xc, xc[:], uL[:, i, kk:kk + TC], pp[:, 20 + 4 * i + kk:21 + 4 * i + kk], xc[:], MUL, ADD, [uL, pp, xc])
                cp(POOL, xcb, xcb[:], xc[:], [xc])
                mm(B[0], B[0][:], WA[:, i, :], xcb[:], True, True, [WA, xcb]); mm(B[1], B[1][:], WX[:, i, :], xcb[:], True, True, [WX, xcb])
                act(r, r[:], B[0][:], AF.Sigmoid, [B[0], pp], bias=pp[:, 30 + i:31 + i])
                act(ig, ig[:], B[1][:], AF.Sigmoid, [B[1], pp], bias=pp[:, 32 + i:33 + i])
                act(a_, a_[:], r[:], AF.Exp, [r, cl], scale=cl[:, i:i + 1])
                act(r, r[:], r[:], AF.Exp, [r, cl2], scale=cl2[:, i:i + 1])
                act(r, r[:], r[:], AF.Sqrt, [r, misc], bias=one_ap, scale=-1.0)
                tt(POOL, ig, ig[:], ig[:], xc[:], MUL, [ig, xc]); tt(DVE, r, r[:], r[:], ig[:], MUL, [r, ig])
                k.op(DVE, lambda e: e.tensor_tensor_scan(out=h_[:], data0=a_[:], data1=r[:], initial=hst[:, i:i + 1], op0=MUL, op1=ADD),
                     reads=[a_, r, hst], writes=[h_])
                cp(ACT, hst, hst[:, i:i + 1], h_[:, TC - 1:TC], [h_])
                cp(POOL, yTt[6 + i], yTb[:, 6 + i, :], h_[:], [h_])
                cp(POOL, uL, uL[:, i, 0:3], uL[:, i, TC:TC + 3], [uL])
                yield

        def sec_p():
            bk = B[5]
            for (ct, ctsw, dst) in ((8, 56, qrot), (10, 58, krot)):
                for i in range(2):
                    b = proj_fm(l, ct + i, bk); tt(DVE, rt1, rt1[:], b[:], ropeC[:], MUL, [b, ropeC])
                    b2 = proj_fm(l, ctsw + i, bk); tt(DVE, rt2, rt2[:], b2[:], ropeS[:], MUL, [b2, ropeS])
                    tt(POOL, dst, dst[:, i, :], rt1[:], rt2[:], ADD, [rt1, rt2])
                    yield
            proj_tm(l, 12, vt, lambda blk: vt[:, blk, :], [bk])
            yield
            for i in range(2):
                b = proj_fm(l, 14 + i, bk); cp(DVE, uL, uL[:, i, 3:3 + TC], b[:], [b])
                yield

        def run_wave(gens):
            st = [[g, n, 0, True] for g, n in gens]
            while any(x[3] for x in st):
                cand = [x for x in st if x[3]]
                x = min(cand, key=lambda y: y[2] / max(y[1], 1))
                try:
                    next(x[0]); x[2] += 1
                except StopIteration:
                    x[3] = False
        npairs = sum(c * 4 + qi + 1 for qi in range(4))
        wave_a = []
        if 'sb' in parts: wave_a.append((sec_sb(), npairs + 2))
        if 's5' in parts: wave_a.append((sec_s5(), 18))
        wave_a.append((sec_p(), 8))
        run_wave(wave_a)
        wave_b = []
        if 'ret' in parts: wave_b.append((sec_ret(), 4))
        if 'lru' in parts: wave_b.append((sec_lru(), 2))
        run_wave(wave_b)

    def phaseO(l, c):
        xc_ = xTc[c]
        for ct in range(8):
            b = proj_fm(l, 16 + ct)
            act(prodS[ct % 4], prodS[ct % 4][:], b[:], AF.Silu, [b])
            tt(DVE, yTt[ct], yTb[:, ct, :], yTb[:, ct, :], prodS[ct % 4][:], MUL, [yTt[ct], prodS[ct % 4]])
        if 'O1' in parts: return
        for f in range(8):
            wb = WB[f % 2]
            k.dma(POOL, wb[:], wbr_d[l].rearrange("n (k p) d -> p (n k) d", p=128)[:, :, f * 128:(f + 1) * 128], writes=[wb])
            for n in range(4):
                bm = proj_fm(l, 24 + n * 8 + f)
                by = B[2 + n % 2]
                for kk in range(2):
                    mm(by, by[:], wb[:, n * 2 + kk, :], yTb[:, 2 * n + kk, :], kk == 0, kk == 1, [wb, yTt[2 * n + kk]])
                s_ = sg[n % 2]
                act(s_, s_[:], bm[:], AF.Sigmoid, [bm])
                tt(DVE, prodS[n], prodS[n][:], s_[:], by[:], MUL, [s_, by])
            bs = B[4]
            for n in range(4):
                mm(bs, bs[:], identb, prodS[n][:], n == 0, n == 3, [cmats, prodS[n]])
            cp(ACT, merged, merged[:, f, :], bs[:], [bs])
        if 'O2' in parts: return
        for half in range(2):
            cs_ = slice(half * 256, (half + 1) * 256)
            ssb = B[5]
            for e in range(8):
                w = load_w(wout_d[l].rearrange("(k p) c -> p k c", p=128)[:, :, e * 128:(e + 1) * 128])
                b = B[pbn[0] % 2]; pbn[0] += 1
                for d in range(8):
                    mm(b, b[:, 0:256], w[:, d, :], merged[:, d, cs_], d == 0, d == 7, [w, merged])
                cp(DVE, outT, outT[:, e, :], b[:, 0:256], [b])
                if 'x1' in parts: continue
                act(sqb[e % 2], sqb[e % 2][:, 0:256], outT[:, e, :], AF.Square, [outT])
                mm(ssb, ssb[:, 0:256], onesb, sqb[e % 2][:, 0:256], e == 0, e == 7, [cmats, sqb[e % 2]])
            if 'x1' in parts or 'x2' in parts: continue
            act(sg[0], sg[0][:, 0:256], ssb[:, 0:256], AF.Sqrt, [ssb, misc], bias=eps_ap, scale=1.0 / D)
            recip(rstd, rstd[:, 0:256], sg[0][:, 0:256], [sg[0]])
            for e in range(8):
                stt(sg[1], sg[1][:, 0:256], outT[:, e, :], pp[:, 8 + e:9 + e], rstd[:, 0:256], MUL, MUL, [outT, pp, rstd])
                tt(DVE, xc_, xc_[:, e, cs_], xc_[:, e, cs_], sg[1][:, 0:256], ADD, [xc_, sg[1]])

    for sq in range(NSEQ):
        k.barrier(); load_seq(sq)
        for l in range(L):
            k.barrier()
            if 'setup' in parts: layer_setup(l)
            k.barrier()
            if 'prenorm' in parts: prenorm(l, 0)
            k.barrier()
            for c in range(NCH):
                if 'proj' in parts: phaseM(l, c)
                if "yT" in dbg_d and sq == 0 and l == 0:
                    k.dma(POOL, dbg_d["yT"][:, :, c * TC:(c + 1) * TC], yTb[:], reads=yTt)
                k.barrier()
                if 'O' in parts: phaseO(l, c)
                if c + 1 < NCH and 'prenorm' in parts: prenorm(l, c + 1)
                k.barrier()
        store_seq(sq)
    k.barrier()
    import os
    if os.environ.get('KSTAT'): print('ENGINE COUNTS', {E.name: E.cnt for E in k.engs}, 'dma', {E.name: sum(E.dvals)//16 for E in k.engs})
    return nc


def _prep_shared(inp, S, L):
    f32 = np.float32
    g = lambda n: np.asarray(inp[n], dtype=f32)[:L]
    w_in = g('w_in')
    perm = []
    for base in (1024, 1280):
        for h in range(4):
            b0 = base + 64 * h
            perm += list(range(b0 + 32, b0 + 64)) + list(range(b0, b0 + 32))
    win = np.ascontiguousarray(np.concatenate([w_in, w_in[:, :, perm]], axis=2))
    pre_g, post_g = g('pre_norm_g'), g('post_norm_g')
    pp = np.zeros((L, 128, 40), f32)
    for l in range(L):
        pp[l, :, 0:8] = pre_g[l].reshape(8, 128).T
        pp[l, :, 8:16] = post_g[l].reshape(8, 128).T
        pp[l, :, 16:18] = g('ssm_d')[l].reshape(2, 128).T
        pp[l, :, 18:20] = g('ssm_b_glu')[l].reshape(2, 128).T
        cw = g('lru_conv_w')[l]
        for i in range(2):
            pp[l, :, 20 + 4 * i:24 + 4 * i] = cw[:, 128 * i:128 * (i + 1)].T
        pp[l, :, 28:30] = g('lru_conv_b')[l].reshape(2, 128).T
        pp[l, :, 30:32] = g('lru_b_a')[l].reshape(2, 128).T
        pp[l, :, 32:34] = g('lru_b_x')[l].reshape(2, 128).T
        pp[l, :, 34:36] = g('lru_lambda')[l].reshape(2, 128).T
    lruw = np.zeros((L, 2, 128, 2, 128), f32)
    for l in range(L):
        for ax, nm in enumerate(('lru_w_a', 'lru_w_x')):
            w = g(nm)[l]
            for i in range(2):
                for bl in range(2):
                    lruw[l, ax, 64 * bl:64 * bl + 64, i, 64 * bl:64 * bl + 64] = w[2 * i + bl]
    a_re, a_im, ldt = g('ssm_a_re'), g('ssm_a_im'), g('ssm_log_dt')
    s5row = np.stack([a_re.reshape(L, 1024), a_im.reshape(L, 1024), np.repeat(ldt, 64, axis=1)], axis=1)
    s5col = np.zeros((L, 128, 3, 16), f32)
    for l in range(L):
        s5col[l, :, 0, :] = np.concatenate([a_re[l].T, a_re[l].T], axis=0)
        s5col[l, :, 1, :] = np.concatenate([a_im[l].T, a_im[l].T], axis=0)
        s5col[l, :, 2, :] = np.broadcast_to(ldt[l][None, :], (128, 16))
    bblk = np.zeros((L, 2, 128, 2, 512), f32)
    for l in range(L):
        for ri, nm in enumerate(('ssm_b_re', 'ssm_b_im')):
            bb = g(nm)[l]
            for gg in range(16):
                kh, gl = gg // 8, gg % 8
                bblk[l, ri, gl * 16:(gl + 1) * 16, kh, gl * 64:(gl + 1) * 64] = bb[gg].T
    cab = np.zeros((L, 128, 32, 128), f32)
    c_re, c_im = g('ssm_c_re'), g('ssm_c_im')
    for l in range(L):
        for gg in range(16):
            co = 16 * (gg % 8)
            cab[l, 0:64, 2 * gg, co:co + 16] = c_re[l, gg].T
            cab[l, 64:128, 2 * gg, co:co + 16] = c_im[l, gg].T
            cab[l, 0:64, 2 * gg + 1, co:co + 16] = c_im[l, gg].T
            cab[l, 64:128, 2 * gg + 1, co:co + 16] = c_re[l, gg].T
    p = np.arange(128)
    invf = (np.float32(10000.0) ** (-(np.arange(32, dtype=f32) / np.float32(32)))).astype(f32)
    ang = (np.arange(S, dtype=f32)[None, :] * invf[(p % 64) % 32][:, None]).astype(f32)
    ropeC = np.cos(ang).astype(f32)
    ropeS = (np.sin(ang) * np.where((p % 64) < 32, -1.0, 1.0)[:, None]).astype(f32)
    gam = 1.0 - 2.0 ** (-5.0 - np.arange(4))
    ii = np.arange(128)
    dtab = np.zeros((128, 2, 256), f32)
    for par in range(2):
        for ti in range(2):
            h = 2 * ti + par
            rel = ii[None, :] - ii[:, None]
            dtab[:, par, ti * 128:(ti + 1) * 128] = np.where(rel >= 0, gam[h] ** np.maximum(rel, 0), 0.0) / 8.0
    ztab = np.zeros((128, 256), f32)
    for h in range(4):
        ztab[:, h * 64:(h + 1) * 64] = (gam[h] ** (127 - ii) / 8.0)[:, None]
    xitab = np.zeros((128, 2, 128), f32); gdec = np.zeros((128, 2), f32)
    for ti in range(2):
        for hl in range(2):
            h = 2 * ti + hl
            xitab[64 * hl:64 * hl + 64, ti, :] = (gam[h] ** (ii + 1.0))[None, :]
            gdec[64 * hl:64 * hl + 64, ti] = gam[h] ** 128
    cm = np.zeros((128, 8, 128), f32)
    cm[:, 0, :] = np.eye(128)
    cm[:, 1, :] = -1.0 * (ii[:, None] >= ii[None, :])
    cm[:, 2, :] = -1.0
    cm[:, 3, :] = (ii[:, None] <= ii[None, :])
    cm[:, 4, :] = (ii[:, None] < ii[None, :])
    for m in range(64):
        cm[m + 64, 5, m] = -1.0
        cm[m, 5, m + 64] = 1.0
    cm[:, 6, :] = 1.0
    cm[0:64, 7, 0:64] = 1.0; cm[64:128, 7, 64:128] = 1.0
    misc = np.zeros((128, 132), f32)
    misc[:, 0] = p; misc[:, 1] = np.where(p < 64, 1.0, -1.0); misc[:, 2] = EPS; misc[:, 3] = 1.0
    misc[:, 4:132] = np.arange(128)[None, :]
    return dict(win=win, wbr=g('w_branch'), wout=g('w_out'), wglu=g('ssm_w_glu'), pp=pp, lruw=lruw, s5row=np.ascontiguousarray(s5row),
                s5col=s5col, bblk=bblk, cab=cab, ropeC=ropeC, ropeS=ropeS, dtab=dtab, ztab=ztab, xitab=xitab, gdec=gdec,
                cmats=cm, misc=misc)


def run(inp, S, NSEQ, DEPTH, ncores, dbg=None, parts=None):
    shared = _prep_shared(inp, S, DEPTH)
    x = np.asarray(inp['x'], dtype=np.float32)
    nc = build(S, NSEQ, DEPTH, dbg, parts)
    in_maps = []
    for i in range(ncores):
        m = dict(shared); m['x'] = np.ascontiguousarray(x[i * NSEQ:(i + 1) * NSEQ]); in_maps.append(m)
    res = run_bass_kernel_spmd(nc, in_maps, core_ids=list(range(ncores)))
    return res


def kernel(**inputs):
    x = np.asarray(inputs['x'])
    Bn, S, _ = x.shape
    ncores = 8
    NSEQ = Bn // ncores
    res = run(inputs, S, NSEQ, 2, ncores)
    return np.concatenate([r["y"] for r in res.results], axis=0).astype(np.float32)
```

```python
import math
import numpy as np
import concourse.bass as bass
import concourse.mybir as mybir
from concourse.bass_utils import run_bass_kernel_spmd

F32 = mybir.dt.float32; BF16 = mybir.dt.bfloat16; I32 = mybir.dt.int32
AF = mybir.ActivationFunctionType; ALU = mybir.AluOpType; AX = mybir.AxisListType
D = 1024; TC = 512; NBK = 4; EPS = 1e-6
NCOLT = 60
PI = math.pi


class T:
    __slots__ = ('ap', 'w', 'r')
    def __init__(s, ap): s.ap = ap; s.w = None; s.r = {}
    def __getitem__(s, idx): return s.ap[idx]


class Eng:
    def __init__(s, nc, name, eng, is_pe=False):
        s.name = name; s.eng = eng; s.sem = nc.alloc_semaphore("sem_" + name); s.cnt = 0; s.known = {}; s.is_pe = is_pe
        s.dsems = []; s.dvals = []; s.dnext = 0


class K:
    def __init__(s, nc, ndma=(16, 16, 4)):
        s.nc = nc
        s.PE = Eng(nc, "pe", nc.tensor, True); s.ACT = Eng(nc, "act", nc.scalar); s.DVE = Eng(nc, "dve", nc.vector)
        s.POOL = Eng(nc, "pool", nc.gpsimd); s.SP = Eng(nc, "sp", nc.sync)
        s.engs = [s.PE, s.ACT, s.DVE, s.POOL, s.SP]
        for E, n in ((s.SP, ndma[0]), (s.POOL, ndma[1]), (s.ACT, ndma[2])):
            E.dsems = [nc.alloc_semaphore(f"d_{E.name}_{i}") for i in range(n)]; E.dvals = [0] * n
        s.nt = 0
    def sb(s, shape, dt, name=None):
        s.nt += 1
        return T(s.nc.alloc_sbuf_tensor("s_" + (name or f"t{s.nt}"), list(shape), dt).ap())
    def ps(s, shape, dt=F32, name=None):
        s.nt += 1
        return T(s.nc.alloc_psum_tensor("ps_" + (name or f"p{s.nt}"), list(shape), dt).ap())
    def _waits(s, E, reads, writes):
        deps = {}
        def add(ev, war=False):
            if ev is None: return
            key, sem, val, who = ev
            if who is E and (E.is_pe or war): return
            if deps.get(key, (None, 0))[1] < val: deps[key] = (sem, val)
        for t in reads: add(t.w)
        for t in writes:
            add(t.w)
            for ev in t.r.values(): add(ev, True)
        for key, (sem, val) in deps.items():
            if E.known.get(key, 0) >= val: continue
            E.eng.wait_ge(sem, val); E.known[key] = val
    def _post(s, ev, reads, writes):
        for t in writes: t.w = ev; t.r = {}
        for t in reads: t.r[ev[0]] = ev
    def op(s, E, fn, reads=(), writes=()):
        s._waits(E, reads, writes)
        ins = fn(E.eng); E.cnt += 1; ins.then_inc(E.sem, 1)
        s._post((E.name, E.sem, E.cnt, E), reads, writes)
        return ins
    def dma(s, Q, out, in_, reads=(), writes=(), **kw):
        i = Q.dnext; Q.dnext = (i + 1) % len(Q.dsems); sem = Q.dsems[i]; key = f"d_{Q.name}_{i}"
        if Q.dvals[i] > 0 and Q.known.get(key, 0) < Q.dvals[i]:
            Q.eng.wait_ge(sem, Q.dvals[i]); Q.known[key] = Q.dvals[i]
        s._waits(Q, reads, writes)
        ins = Q.eng.dma_start(out=out, in_=in_, **kw); Q.dvals[i] += 16; ins.then_inc(sem, 16)
        ev = (key, sem, Q.dvals[i], None)
        s._post(ev, reads, writes)
        return ev
    def barrier(s):
        for E in s.engs:
            for Fg in s.engs:
                if Fg is E or Fg.cnt == 0: continue
                if E.known.get(Fg.name, 0) < Fg.cnt:
                    E.eng.wait_ge(Fg.sem, Fg.cnt); E.known[Fg.name] = Fg.cnt
            for Q in s.engs:
                for i, sem in enumerate(Q.dsems):
                    key = f"d_{Q.name}_{i}"
                    if Q.dvals[i] > 0 and E.known.get(key, 0) < Q.dvals[i]:
                        E.eng.wait_ge(sem, Q.dvals[i]); E.known[key] = Q.dvals[i]


def build(S, NSEQ, DEPTH, dbg=None, parts=None):
    nc = bass.Bass("TRN2", target_bir_lowering=False)
    k = K(nc)
    if parts is None: parts = {"setup", "prenorm", "proj", "sb", "s5", "ret", "lru", "O"}
    PE, ACT, DVE, POOL, SP = k.PE, k.ACT, k.DVE, k.POOL, k.SP
    NCH = S // TC
    NBLK = S // 128
    L = DEPTH
    def din(name, shape): return nc.dram_tensor(name, list(shape), F32, kind="ExternalInput").ap()
    x_d = din("x", [NSEQ, S, D]); win_d = din("win", [L, D, 7680]); wbr_d = din("wbr", [L, 4, 256, D])
    wout_d = din("wout", [L, D, D]); wglu_d = din("wglu", [L, 256, 256]); pp_d = din("pp", [L, 128, 40])
    lruw_d = din("lruw", [L, 2, 128, 2, 128]); s5row_d = din("s5row", [L, 3, 1024]); s5col_d = din("s5col", [L, 128, 3, 16])
    bblk_d = din("bblk", [L, 2, 128, 2, 512]); cab_d = din("cab", [L, 128, 32, 128])
    ropeC_d = din("ropeC", [128, S]); ropeS_d = din("ropeS", [128, S])
    dtab_d = din("dtab", [128, 2, 256]); ztab_d = din("ztab", [128, 256]); xitab_d = din("xitab", [128, 2, 128])
    gdec_d = din("gdec", [128, 2]); cm_d = din("cmats", [128, 8, 128]); misc_d = din("misc", [128, 132])
    y_d = nc.dram_tensor("y", [NSEQ, S, D], F32, kind="ExternalOutput").ap()
    dbg_d = {}
    if dbg:
        for nm, shp in dbg.items():
            dbg_d[nm] = nc.dram_tensor(nm, list(shp), F32, kind="ExternalOutput").ap()

    xTb = nc.alloc_sbuf_tensor("s_xT", [128, 8, S], F32).ap()
    xTc = [T(xTb[:, :, c * TC:(c + 1) * TC]) for c in range(S // TC)]
    hT = k.sb([128, 8, TC], BF16, "hT")
    WT = [k.sb([128, 8, 128], BF16, f"WT{i}") for i in range(5)]
    wtn = [0]
    pp = k.sb([128, 40], F32, "pp")
    cmats = k.sb([128, 8, 128], BF16, "cmats")
    identb = cmats[:, 0, :]; ntri = cmats[:, 1, :]; nones = cmats[:, 2, :]; tri = cmats[:, 3, :]
    Jm = cmats[:, 5, :]; onesb = cmats[:, 6, :]
    identf = k.sb([128, 128], F32, "identf")
    misc = k.sb([128, 132], F32, "misc")
    mask2 = k.sb([128, 256], BF16, "mask2")
    zrow = k.sb([1, 256], BF16, "zrow")
    ropeC = k.sb([128, TC], BF16, "ropeC"); ropeS = k.sb([128, TC], BF16, "ropeS")
    dtab = k.sb([128, 2, 256], BF16, "dtab"); ztab = k.sb([128, 256], BF16, "ztab"); xitab = k.sb([128, 2, 128], BF16, "xitab")
    gdec = k.sb([128, 2], F32, "gdec")
    KT = k.sb([128, 2, S], BF16, "KT"); Vc = k.sb([128, NBLK, 256], BF16, "Vc"); QT = k.sb([128, 2, TC], BF16, "QT")
    sbE = [k.sb([128, 256], F32, f"sbE{i}") for i in range(2)]
    sbSP = [k.sb([128, 256], BF16, f"sbSP{i}") for i in range(2)]
    sbW = [k.sb([128, 256], BF16, f"sbW{i}") for i in range(2)]
    sbC = [k.sb([128, 256], BF16, f"sbC{i}") for i in range(2)]
    osb = k.sb([128, 256], BF16, "osb")
    uT = k.sb([128, 2, TC], BF16, "uT")
    Bx = [k.sb([128, 1024], BF16, f"Bx{i}") for i in range(2)]; Bsw = [k.sb([128, 1024], BF16, f"Bsw{i}") for i in range(2)]
    Pr = k.sb([128, 16, 64], BF16, "Pr"); Pi = k.sb([128, 16, 64], BF16, "Pi")
    T1 = k.sb([128, 16, 128], BF16, "T1"); T2 = k.sb([128, 16, 128], BF16, "T2")
    W1 = k.sb([128, 16], F32, "W1"); W2 = k.sb([128, 16], F32, "W2")
    Cm = k.sb([128, 32, 128], BF16, "Cm")
    G1 = [k.sb([128, 4, 128], BF16, f"G1_{i}") for i in range(2)]; G2 = [k.sb([128, 4, 128], BF16, f"G2_{i}") for i in range(2)]
    A1 = [k.sb([128, 4, 128], BF16, f"A1_{i}") for i in range(2)]; A2 = [k.sb([128, 4, 128], BF16, f"A2_{i}") for i in range(2)]
    carry = [k.sb([128, 4], F32, f"carry{i}") for i in range(4)]
    sfull = k.sb([128, 4], F32, "sfull"); U1 = k.sb([128, 4], BF16, "U1"); U2 = k.sb([128, 4], BF16, "U2")
    ygb = k.sb([128, 2, TC], BF16, "ygb"); wglu = k.sb([128, 2, 256], BF16, "wglu")
    uL = k.sb([128, 2, TC + 4], F32, "uL"); hst = k.sb([128, 2], F32, "hst")
    WA = k.sb([128, 2, 128], BF16, "WA"); WX = k.sb([128, 2, 128], BF16, "WX")
    cl = k.sb([128, 2], F32, "cl"); cl2 = k.sb([128, 2], F32, "cl2"); cltmp = k.sb([128, 2], F32, "cltmp")
    Rst = k.sb([128, 2, 128], F32, "Rst"); Rb = k.sb([128, 2, 128], BF16, "Rb")
    st4 = [k.sb([128, 4], F32, f"st4_{i}") for i in range(6)]
    yTb = nc.alloc_sbuf_tensor("s_yT", [128, 8, TC], BF16).ap()
    yTt = [T(yTb[:, i, :]) for i in range(8)]
    WB = [k.sb([128, 8, 128], BF16, f"WB{i}") for i in range(2)]
    colp = k.sb([128, 3, 16], F32, "colp"); colq = k.sb([128, 3, 16], F32, "colq")
    UN = nc.alloc_sbuf_tensor("UN", [128, 7168], F32).ap()
    def carve(off_f32, n_f32, dt, shape):
        v = UN[:, off_f32:off_f32 + n_f32]
        if dt is BF16: v = v.bitcast(BF16)
        if len(shape) == 3: v = v.rearrange("p (a b) -> p a b", a=shape[1])
        return T(v)
    LT = [carve(512 * i, 512, F32, [128, 512]) for i in range(5)]
    xcb = carve(2560, 256, BF16, [128, 512])
    qrot = carve(2816, 512, BF16, [128, 2, 512]); krot = carve(3328, 512, BF16, [128, 2, 512])
    qxi = carve(3840, 512, BF16, [128, 2, 512])
    vt = carve(4352, 512, BF16, [128, 4, 256]); kz = carve(4864, 128, BF16, [128, 256])
    PTt = carve(4992, 256, BF16, [128, 2, 256])
    osbf = carve(5248, 256, F32, [128, 256]); osq = carve(5504, 256, F32, [128, 256]); onb = carve(5760, 128, BF16, [128, 256])
    rt1 = carve(5888, 512, F32, [128, 512]); rt2 = carve(6400, 512, F32, [128, 512])
    merged = carve(0, 2048, BF16, [128, 8, 512])
    prodS = [carve(2048 + 256 * i, 256, BF16, [128, 512]) for i in range(4)]
    sg = [carve(3072, 512, F32, [128, 512]), carve(3584, 512, F32, [128, 512])]
    outT = carve(4096, 2048, F32, [128, 8, 256])
    stg = [carve(4096, 1024, F32, [128, 1024]), carve(5120, 1024, F32, [128, 1024])]
    sqb = [carve(6144, 256, BF16, [128, 512]), carve(6400, 256, BF16, [128, 512])]
    rstd = carve(6656, 512, F32, [128, 512])
    SRt = [carve(512 * i, 512, F32, [128, 512]) for i in range(14)]
    B = [k.ps([128, 512], F32, f"bank{i}") for i in range(8)]

    def mm(out_t, out_ap, lhsT, rhs, start, stop, reads):
        k.op(PE, lambda e: e.matmul(out_ap, lhsT=lhsT, rhs=rhs, start=start, stop=stop), reads=reads, writes=[out_t])
    def act(out_t, out_ap, in_ap, func, reads, bias=None, scale=None):
        kw = {}
        if bias is not None: kw['bias'] = bias
        if scale is not None: kw['scale'] = scale
        k.op(ACT, lambda e: e.activation(out=out_ap, in_=in_ap, func=func, **kw), reads=reads, writes=[out_t])
    def tt(E, out_t, out_ap, a, b, op, reads):
        k.op(E, lambda e: e.tensor_tensor(out=out_ap, in0=a, in1=b, op=op), reads=reads, writes=[out_t])
    def ts(E, out_t, out_ap, a, s1, s2, op0, op1, reads):
        if op1 is None:
            k.op(E, lambda e: e.tensor_scalar(out=out_ap, in0=a, scalar1=s1, scalar2=None, op0=op0), reads=reads, writes=[out_t])
        else:
            k.op(E, lambda e: e.tensor_scalar(out=out_ap, in0=a, scalar1=s1, scalar2=s2, op0=op0, op1=op1), reads=reads, writes=[out_t])
    def stt(out_t, out_ap, a, sc, b, op0, op1, reads):
        k.op(DVE, lambda e: e.scalar_tensor_tensor(out=out_ap, in0=a, scalar=sc, in1=b, op0=op0, op1=op1), reads=reads, writes=[out_t])
    def cp(E, out_t, out_ap, in_ap, reads):
        if E is ACT:
            k.op(E, lambda e: e.copy(out=out_ap, in_=in_ap), reads=reads, writes=[out_t])
        else:
            k.op(E, lambda e: e.tensor_copy(out=out_ap, in_=in_ap), reads=reads, writes=[out_t])
    def memset(E, t, ap, v):
        k.op(E, lambda e: e.memset(ap, v), writes=[t])
    def dbg_out(name, t, ap):
        if name in dbg_d:
            k.dma(SP, dbg_d[name], ap, reads=[t])

    def load_w(src_ap):
        w = WT[wtn[0] % 5]; wtn[0] += 1
        k.dma(POOL, w[:], src_ap, writes=[w])
        return w
    def win_tile(l, ct):
        return win_d[l].rearrange("(k p) c -> p k c", p=128)[:, :, ct * 128:(ct + 1) * 128]
    pbn = [0]
    def proj_fm(l, ct, bank=None):
        w = load_w(win_tile(l, ct))
        if bank is None:
            b = B[pbn[0] % 2]; pbn[0] += 1
        else:
            b = bank
        for kk in range(8):
            mm(b, b[:], w[:, kk, :], hT[:, kk, :], kk == 0, kk == 7, [w, hT])
        return b
    def proj_tm(l, ct0, dst_t, dst_fn, banks=None):
        w0 = load_w(win_tile(l, ct0)); w1 = load_w(win_tile(l, ct0 + 1))
        for half in range(2):
            if banks is None:
                b = B[pbn[0] % 2]; pbn[0] += 1
            else:
                b = banks[half % len(banks)]
            for bi in range(2):
                blk = half * 2 + bi
                for j, w in enumerate((w0, w1)):
                    o = bi * 256 + j * 128
                    for kk in range(8):
                        mm(b, b[:, o:o + 128], hT[:, kk, blk * 128:(blk + 1) * 128], w[:, kk, :], kk == 0, kk == 7, [w, hT])
            for bi in range(2):
                blk = half * 2 + bi
                cp(ACT, dst_t, dst_fn(blk), b[:, bi * 256:(bi + 1) * 256], [b])

    def sincos(ang, vw, tmps):
        ki, kr, sn, cs = tmps
        kiv = vw(ki).bitcast(I32)
        ts(DVE, kr, vw(kr), vw(ang), 1.0 / (2 * PI), None, ALU.mult, None, [ang])
        cp(DVE, ki, kiv, vw(kr), [kr])
        cp(DVE, kr, vw(kr), kiv, [ki])
        C1 = 6.28125; C2 = 2 * PI - 6.28125
        stt(sn, vw(sn), vw(kr), -C1, vw(ang), ALU.mult, ALU.add, [kr, ang])
        stt(sn, vw(sn), vw(kr), -C2, vw(sn), ALU.mult, ALU.add, [kr, sn])
        ts(DVE, sn, vw(sn), vw(sn), -PI, PI, ALU.max, ALU.min, [sn])
        ts(DVE, cs, vw(cs), vw(sn), PI / 2, -2 * PI, ALU.is_gt, ALU.mult, [sn])
        stt(cs, vw(cs), vw(sn), PI / 2, vw(cs), ALU.add, ALU.add, [sn, cs])
        ts(DVE, cs, vw(cs), vw(cs), -PI, PI, ALU.max, ALU.min, [cs])
        act(cs, vw(cs), vw(cs), AF.Sin, [cs])
        act(sn, vw(sn), vw(sn), AF.Sin, [sn])

    k.dma(POOL, cmats[:], cm_d, writes=[cmats])
    k.dma(SP, misc[:], misc_d, writes=[misc])
    k.dma(POOL, dtab[:], dtab_d, writes=[dtab]); k.dma(POOL, ztab[:], ztab_d, writes=[ztab]); k.dma(POOL, xitab[:], xitab_d, writes=[xitab])
    k.dma(SP, gdec[:], gdec_d, writes=[gdec])
    memset(POOL, zrow, zrow[:], 0.0)
    for i in range(2):
        cp(POOL, mask2, mask2[:, i * 128:(i + 1) * 128], cmats[:, 4, :], [cmats])
    cp(DVE, identf, identf[:], cmats[:, 0, :], [cmats])
    tauc = misc[:, 0:1]; sgn1 = misc[:, 1:2]; eps_ap = misc[:, 2:3]; one_ap = misc[:, 3:4]; taurow = misc[:, 4:132]


    flat = lambda t: t.ap
    v3 = lambda t: t.ap.rearrange("p (g n) -> p g n", g=8)
    v4 = lambda t: t.ap.rearrange("p (g n) -> p g n", g=4)
    sm = lambda t: t.ap[:, 0:16]
    MUL, ADD, SUB = ALU.mult, ALU.add, ALU.subtract

    def recip(out_t, out_ap, in_ap, reads):
        k.op(DVE, lambda e: e.reciprocal(out=out_ap, in_=in_ap), reads=reads, writes=[out_t])
    def transpose(out_t, out_ap, in_ap, ident_ap, reads):
        k.op(PE, lambda e: e.transpose(out=out_ap, in_=in_ap, identity=ident_ap), reads=reads, writes=[out_t])

    def layer_setup(l):
        k.dma(SP, pp[:], pp_d[l], writes=[pp])
        k.dma(POOL, WA[:], lruw_d[l, 0], writes=[WA]); k.dma(POOL, WX[:], lruw_d[l, 1], writes=[WX])
        k.dma(POOL, wglu[:], wglu_d[l].rearrange("(k p) c -> p k c", p=128), writes=[wglu])
        k.dma(POOL, Cm[:], cab_d[l], writes=[Cm])
        k.dma(SP, colp[:], s5col_d[l], writes=[colp])
        act(cltmp, cltmp[:], pp[:, 34:36], AF.Exp, [pp], scale=-1.0)
        act(cltmp, cltmp[:], cltmp[:], AF.Ln, [cltmp, misc], bias=one_ap)
        ts(DVE, cl, cl[:], cltmp[:], -8.0, None, MUL, None, [cltmp])
        ts(DVE, cl2, cl2[:], cltmp[:], -16.0, None, MUL, None, [cltmp])
        memset(POOL, uL, uL[:, :, 0:4], 0.0); memset(POOL, hst, hst[:], 0.0)
        memset(POOL, Rst, Rst[:], 0.0); memset(POOL, Rb, Rb[:], 0.0)
        for q in range(4): memset(POOL, carry[q], carry[q][:], 0.0)
        act(colq, colq[:, 0, :], colp[:, 2, :], AF.Exp, [colp])
        tt(DVE, colq, colq[:, 1, :], colq[:, 0, :], colp[:, 0, :], MUL, [colq, colp])
        tt(DVE, colq, colq[:, 2, :], colq[:, 0, :], colp[:, 1, :], MUL, [colq, colp])
        for kh in range(2):
            are, aim, dt_, dre, dim_, mag, ki, kr, sn, cs, fr, fi, bre, bim = SRt
            for i, t in enumerate((are, aim, dt_)):
                k.dma(SP, t[:], bass.AP(s5row_d.tensor, (l * 3 + i) * 1024 + kh * 512, [[0, 128], [1, 512]]), writes=[t])
            k.dma(SP, bre[:], bblk_d[l, 0][:, kh, :], writes=[bre]); k.dma(SP, bim[:], bblk_d[l, 1][:, kh, :], writes=[bim])
            act(dt_, dt_[:], dt_[:], AF.Exp, [dt_])
            tt(DVE, dre, dre[:], dt_[:], are[:], MUL, [dt_, are]); tt(DVE, dim_, dim_[:], dt_[:], aim[:], MUL, [dt_, aim])
            act(mag, mag[:], dre[:], AF.Exp, [dre])
            sincos(dim_, flat, (ki, kr, sn, cs))
            tt(DVE, cs, cs[:], mag[:], cs[:], MUL, [mag, cs]); tt(DVE, sn, sn[:], mag[:], sn[:], MUL, [mag, sn])
            ts(DVE, cs, cs[:], cs[:], -1.0, None, ADD, None, [cs])
            tt(DVE, mag, mag[:], are[:], are[:], MUL, [are]); tt(DVE, ki, ki[:], aim[:], aim[:], MUL, [aim])
            tt(DVE, mag, mag[:], mag[:], ki[:], ADD, [mag, ki]); recip(mag, mag[:], mag[:], [mag])
            tt(DVE, fr, fr[:], cs[:], are[:], MUL, [cs, are]); tt(DVE, ki, ki[:], sn[:], aim[:], MUL, [sn, aim])
            tt(DVE, fr, fr[:], fr[:], ki[:], ADD, [fr, ki]); tt(DVE, fr, fr[:], fr[:], mag[:], MUL, [fr, mag])
            tt(DVE, fi, fi[:], sn[:], are[:], MUL, [sn, are]); tt(DVE, ki, ki[:], cs[:], aim[:], MUL, [cs, aim])
            tt(DVE, fi, fi[:], fi[:], ki[:], SUB, [fi, ki]); tt(DVE, fi, fi[:], fi[:], mag[:], MUL, [fi, mag])
            tt(DVE, kr, kr[:], fr[:], bre[:], MUL, [fr, bre]); tt(DVE, ki, ki[:], fi[:], bim[:], MUL, [fi, bim])
            tt(DVE, sn, sn[:], fr[:], bim[:], MUL, [fr, bim]); tt(DVE, cs, cs[:], fi[:], bre[:], MUL, [fi, bre])
            bxv = Bx[kh].ap.rearrange("p (g a n) -> p g a n", g=8, a=2)
            bsv = Bsw[kh].ap.rearrange("p (g a n) -> p g a n", g=8, a=2)
            tt(DVE, Bx[kh], bxv[:, :, 0, :], v3(kr), v3(ki), SUB, [kr, ki])
            tt(DVE, Bsw[kh], bsv[:, :, 1, :], v3(kr), v3(ki), SUB, [kr, ki])
            tt(DVE, Bx[kh], bxv[:, :, 1, :], v3(sn), v3(cs), ADD, [sn, cs])
            stt(Bsw[kh], bsv[:, :, 0, :], v3(sn), -1.0, v3(cs), MUL, SUB, [sn, cs])
            ts(DVE, mag, mag[:], dim_[:], tauc, None, MUL, None, [dim_, misc])
            sincos(mag, flat, (ki, kr, sn, cs))
            ts(DVE, fr, fr[:], dre[:], tauc, None, MUL, None, [dre, misc]); act(fr, fr[:], fr[:], AF.Exp, [fr], scale=-1.0)
            tt(DVE, Pr, Pr[:, 8 * kh:8 * kh + 8, :], v3(fr), v3(cs), MUL, [fr, cs])
            stt(Pi, Pi[:, 8 * kh:8 * kh + 8, :], v3(fr), -1.0, v3(sn), MUL, MUL, [fr, sn])
        for q in range(4):
            ang, mexp, ki, kr, sn, cs = SRt[0:6]
            taub = taurow.unsqueeze(1).to_broadcast([128, 4, 128])
            dimb = colq[:, 2, 4 * q:4 * q + 4].unsqueeze(2).to_broadcast([128, 4, 128])
            dreb = colq[:, 1, 4 * q:4 * q + 4].unsqueeze(2).to_broadcast([128, 4, 128])
            tt(DVE, ang, v4(ang), dimb, taub, MUL, [colq, misc])
            tt(DVE, mexp, v4(mexp), dreb, taub, MUL, [colq, misc]); act(mexp, mexp[:], mexp[:], AF.Exp, [mexp])
            sincos(ang, flat, (ki, kr, sn, cs))
            stt(T1, T1[:, 4 * q:4 * q + 4, :], v4(mexp), sgn1, v4(cs), MUL, MUL, [mexp, misc, cs])
            stt(T2, T2[:, 4 * q:4 * q + 4, :], v4(mexp), -1.0, v4(sn), MUL, MUL, [mexp, sn])
        angw, mw, ki, kr, sn, cs = SRt[6:12]
        ts(DVE, angw, sm(angw), colq[:, 2, :], 128.0, None, MUL, None, [colq])
        ts(DVE, mw, sm(mw), colq[:, 1, :], 128.0, None, MUL, None, [colq]); act(mw, sm(mw), sm(mw), AF.Exp, [mw])
        sincos(angw, sm, (ki, kr, sn, cs))
        tt(DVE, W1, W1[:], sm(mw), sm(cs), MUL, [mw, cs]); tt(DVE, W2, W2[:], sm(mw), sm(sn), MUL, [mw, sn])

    def load_seq(sq):
        for blk in range(NBLK):
            st = stg[blk % 2]; c = blk // 4; o = (blk % 4) * 128
            k.dma(SP, st[:], x_d[sq, blk * 128:(blk + 1) * 128, :], writes=[st])
            for half in range(2):
                b = B[2 + half]
                for j in range(4):
                    f = half * 4 + j
                    transpose(b, b[:, j * 128:(j + 1) * 128], st[:, f * 128:(f + 1) * 128], identf[:], [st, identf])
                cp(DVE if half == 0 else ACT, xTc[c], xTc[c][:, half * 4:(half + 1) * 4, o:o + 128],
                   b[:].rearrange("p (a b) -> p a b", a=4), [b])

    def store_seq(sq):
        for blk in range(NBLK):
            st = stg[blk % 2]; c = blk // 4; o = (blk % 4) * 128
            for half in range(2):
                b = B[2 + half]
                for j in range(4):
                    f = half * 4 + j
                    transpose(b, b[:, j * 128:(j + 1) * 128], xTc[c][:, f, o:o + 128], identf[:], [xTc[c], identf])
                cp(DVE if half == 0 else ACT, st, st[:, half * 512:(half + 1) * 512], b[:], [b])
            k.dma(SP, y_d[sq, blk * 128:(blk + 1) * 128, :], st[:], reads=[st])

    def prenorm(l, c):
        xc_ = xTc[c]; ssb = B[5]
        for f in range(8):
            s_ = sqb[f % 2]
            act(s_, s_[:], xc_[:, f, :], AF.Square, [xc_])
            mm(ssb, ssb[:], onesb, s_[:], f == 0, f == 7, [cmats, s_])
        act(sg[0], sg[0][:], ssb[:], AF.Sqrt, [ssb, misc], bias=eps_ap, scale=1.0 / D)
        recip(rstd, rstd[:], sg[0][:], [sg[0]])
        for f in range(8):
            stt(hT, hT[:, f, :], xc_[:, f, :], pp[:, f:f + 1], rstd[:], MUL, MUL, [xc_, pp, rstd])

    def phaseM(l, c):
        t0 = c * TC
        k.dma(POOL, ropeC[:], ropeC_d[:, t0:t0 + TC], writes=[ropeC]); k.dma(POOL, ropeS[:], ropeS_d[:, t0:t0 + TC], writes=[ropeS])
        def sec_sb():
            Z = [B[2], B[3]]; PO = B[4]; PTb = B[2]; pv = PTb.ap.bitcast(BF16)
            for i in range(2):
                b = proj_fm(l, i, B[2 + i]); act(QT, QT[:, i, :], b[:], AF.Copy, [b], scale=0.125)
            for i in range(2):
                b = proj_fm(l, 2 + i, B[2 + i]); cp(DVE, KT, KT[:, i, t0:t0 + TC], b[:], [b])
            yield
            proj_tm(l, 4, Vc, lambda blk: Vc[:, c * 4 + blk, :], [B[2], B[3]])
            yield
            for qi in range(4):
                qb = c * 4 + qi
                mm(PO, PO[:, 0:256], zrow[0:1, 0:128], zrow[0:1, 0:256], True, False, [zrow])
                for a in range(qb, -1, -1):
                    diag = (a == qb)
                    for par in range(2):
                        z = Z[par]
                        mm(z, z[:, 0:256], zrow[0:1, 0:128], zrow[0:1, 0:256], True, False, [zrow])
                        for ti in range(2):
                            mm(z, z[:, ti * 128:(ti + 1) * 128], KT[64 * par:64 * par + 64, ti, a * 128:(a + 1) * 128],
                               QT[64 * par:64 * par + 64, ti, qi * 128:(qi + 1) * 128], False, False, [KT, QT])
                    yield
                    for par in range(2):
                        z = Z[par]; e_ = sbE[par]; sp_ = sbSP[par]
                        act(e_, e_[:], z[:, 0:256], AF.Exp, [z])
                        act(sp_, sp_[:], e_[:], AF.Ln, [e_, misc], bias=one_ap)
                        if diag: tt(POOL, sp_, sp_[:], sp_[:], mask2[:], MUL, [sp_, mask2])
                    yield
                    for par in range(2):
                        z = Z[par]; sp_ = sbSP[par]; c_ = sbC[par]
                        mm(z, z[:, 0:256], ntri, sp_[:], False, diag, [cmats, sp_])
                        if not diag: mm(z, z[:, 0:256], nones, c_[:], False, True, [cmats, c_])
                    yield
                    for par in range(2):
                        z = Z[par]; sp_ = sbSP[par]; w_ = sbW[par]; c_ = sbC[par]
                        act(w_, w_[:], z[:, 0:256], AF.Exp, [z])
                        if diag: tt(POOL, w_, w_[:], w_[:], mask2[:], MUL, [w_, mask2])
                        if a > 0:
                            if diag: cp(POOL, c_, c_[:], sp_[:], [sp_])
                            else: tt(POOL, c_, c_[:], c_[:], sp_[:], ADD, [c_, sp_])
                    yield
                    for par in range(2):
                        w_ = sbW[par]
                        for ti in range(2):
                            h = 2 * ti + par
                            mm(PO, PO[:, h * 64:(h + 1) * 64], w_[:, ti * 128:(ti + 1) * 128], Vc[:, a, h * 64:(h + 1) * 64],
                               False, (a == 0 and par == 1 and ti == 1), [w_, Vc])
                    yield
                cp(ACT, osb, osb[:], PO[:, 0:256], [PO])
                for ti in range(2):
                    transpose(PTb, pv[:, ti * 128:(ti + 1) * 128], osb[:, ti * 128:(ti + 1) * 128], identb, [osb, cmats])
                cp(DVE, yTt[0], yTb[:, 0:2, qi * 128:(qi + 1) * 128], pv[:, 0:256].rearrange("p (a b) -> p a b", a=2), [PTb])
                yTt[1].w = yTt[0].w; yTt[1].r = {}


        def sec_s5():
            Yb = B[1]
            for i in range(2):
                b = proj_fm(l, 6 + i, B[6 + i]); cp(ACT, uT, uT[:, i, :], b[:], [b])
            yield
            x4 = lambda b: b.ap.rearrange("p (g a n) -> p g a n", g=4, a=2)
            g4 = lambda t: t.ap.rearrange("p g (a n) -> p g a n", a=2)
            for sc in range(4):
                for q in range(4):
                    kh = q // 2; hq = q % 2; st_ = (sc * 4 + q) % 2
                    mm(B[6], B[6][:], uT[:, kh, sc * 128:(sc + 1) * 128], Bx[kh][:, hq * 512:(hq + 1) * 512], True, True, [uT, Bx[kh]])
                    mm(B[7], B[7][:], uT[:, kh, sc * 128:(sc + 1) * 128], Bsw[kh][:, hq * 512:(hq + 1) * 512], True, True, [uT, Bsw[kh]])
                    yield
                    prb = Pr[:, 4 * q:4 * q + 4, :].unsqueeze(2).to_broadcast([128, 4, 2, 64])
                    pib = Pi[:, 4 * q:4 * q + 4, :].unsqueeze(2).to_broadcast([128, 4, 2, 64])
                    tt(DVE, G1[st_], g4(G1[st_]), x4(B[6]), prb, MUL, [B[6], Pr])
                    tt(DVE, G2[st_], g4(G2[st_]), x4(B[7]), pib, MUL, [B[7], Pi])
                    yield
                    for gl in range(4):
                        mm(B[0], B[0][:, gl * 128:(gl + 1) * 128], G1[st_][:, gl, :], tri, True, False, [G1[st_], cmats])
                        mm(B[0], B[0][:, gl * 128:(gl + 1) * 128], G2[st_][:, gl, :], tri, False, True, [G2[st_], cmats])
                    yield
                    for gl in range(4):
                        g = 4 * q + gl
                        stt(A1[st_], A1[st_][:, gl, :], B[0][:, gl * 128:(gl + 1) * 128], carry[q][:, gl:gl + 1], T1[:, g, :], ADD, MUL, [B[0], carry[q], T1])
                        stt(A2[st_], A2[st_][:, gl, :], B[0][:, gl * 128:(gl + 1) * 128], carry[q][:, gl:gl + 1], T2[:, g, :], ADD, MUL, [B[0], carry[q], T2])
                    slast = B[0][:].rearrange("p (g t) -> p g t", g=4)[:, :, 127:128].rearrange("p g o -> p (g o)")
                    tt(DVE, sfull, sfull[:], slast, carry[q][:], ADD, [B[0], carry[q]])
                    tt(DVE, U1, U1[:], sfull[:], W1[:, 4 * q:4 * q + 4], MUL, [sfull, W1])
                    tt(DVE, U2, U2[:], sfull[:], W2[:, 4 * q:4 * q + 4], MUL, [sfull, W2])
                    yield
                    yb = Yb
                    for gl in range(4):
                        g = 4 * q + gl
                        mm(yb, yb[:, kh * 128:(kh + 1) * 128], Cm[:, 2 * g, :], A1[st_][:, gl, :], (hq == 0 and gl == 0), False, [Cm, A1[st_]])
                        mm(yb, yb[:, kh * 128:(kh + 1) * 128], Cm[:, 2 * g + 1, :], A2[st_][:, gl, :], False, (hq == 1 and gl == 3), [Cm, A2[st_]])
                    if hq == 1:
                        cp(ACT, LT[kh], LT[kh][:, sc * 128:(sc + 1) * 128], yb[:, kh * 128:(kh + 1) * 128], [yb])
                    cb = B[6]
                    mm(cb, cb[:, 0:4], identb, U1[:], True, False, [cmats, U1]); mm(cb, cb[:, 0:4], Jm, U2[:], False, True, [cmats, U2])
                    yield
                    cp(ACT, carry[q], carry[q][:], cb[:, 0:4], [cb])
                    yield
            for kh in range(2):
                yv = LT[kh]; x2 = LT[2]
                stt(yv, yv[:], uT[:, kh, :], pp[:, 16 + kh:17 + kh], yv[:], MUL, ADD, [uT, pp, yv])
                act(x2, x2[:], yv[:], AF.Square, [yv])
                ts(DVE, x2, x2[:], x2[:], 0.044715, 1.0, MUL, ADD, [x2])
                tt(DVE, x2, x2[:], x2[:], yv[:], MUL, [x2, yv])
                act(x2, x2[:], x2[:], AF.Sigmoid, [x2], scale=1.5957691216057308)
                tt(DVE, yv, yv[:], yv[:], x2[:], MUL, [yv, x2])
                cp(POOL, ygb, ygb[:, kh, :], yv[:], [yv])
            for e in range(2):
                b = B[6 + e]
                for kh in range(2):
                    mm(b, b[:], wglu[:, kh, e * 128:(e + 1) * 128], ygb[:, kh, :], kh == 0, kh == 1, [wglu, ygb])
                act(LT[2 + e], LT[2 + e][:], b[:], AF.Sigmoid, [b, pp], bias=pp[:, 18 + e:19 + e])
                tt(DVE, yTt[2 + e], yTb[:, 2 + e, :], LT[e][:], LT[2 + e][:], MUL, [LT[e], LT[2 + e]])


        def sec_ret():
            for ti in range(2):
                tt(DVE, qxi, qxi[:, ti, :].rearrange("p (n i) -> p n i", n=4), qrot[:, ti, :].rearrange("p (n i) -> p n i", n=4),
                   xitab[:, ti, :].unsqueeze(1).to_broadcast([128, 4, 128]), MUL, [qrot, xitab])
            pv2 = B[6].ap.bitcast(BF16); pv = B[5].ap.bitcast(BF16)
            for n in range(4):
                SX = [B[2], B[3]]
                for par in range(2):
                    for ti in range(2):
                        mm(SX[par], SX[par][:, ti * 128:(ti + 1) * 128], krot[64 * par:64 * par + 64, ti, n * 128:(n + 1) * 128],
                           qrot[64 * par:64 * par + 64, ti, n * 128:(n + 1) * 128], True, True, [krot, qrot])
                    tt(DVE, PTt, PTt[:, par, :], SX[par][:, 0:256], dtab[:, par, :], MUL, [SX[par], dtab])
                po = B[4]
                mm(po, po[:, 0:256], zrow[0:1, 0:128], zrow[0:1, 0:256], True, False, [zrow])
                for par in range(2):
                    for ti in range(2):
                        h = 2 * ti + par
                        mm(po, po[:, h * 64:(h + 1) * 64], PTt[:, par, ti * 128:(ti + 1) * 128], vt[:, n, h * 64:(h + 1) * 64], False, False, [PTt, vt])
                for ti in range(2):
                    mm(po, po[:, ti * 128:(ti + 1) * 128], qxi[:, ti, n * 128:(n + 1) * 128], Rb[:, ti, :], False, ti == 1, [qxi, Rb])
                if 'ret1' in parts: continue
                cp(ACT, osbf, osbf[:], po[:, 0:256], [po])
                o3 = osbf.ap.rearrange("p (h e) -> p h e", h=4); q3 = osq.ap.rearrange("p (h e) -> p h e", h=4)
                k.op(DVE, lambda e: e.tensor_reduce(out=st4[0][:], in_=o3, axis=AX.X, op=ADD), reads=[osbf], writes=[st4[0]])
                act(osq, osq[:], osbf[:], AF.Square, [osbf])
                k.op(DVE, lambda e: e.tensor_reduce(out=st4[1][:], in_=q3, axis=AX.X, op=ADD), reads=[osq], writes=[st4[1]])
                ts(DVE, st4[2], st4[2][:], st4[0][:], 1.0 / 64, None, MUL, None, [st4[0]])
                tt(DVE, st4[3], st4[3][:], st4[2][:], st4[2][:], MUL, [st4[2]])
                stt(st4[3], st4[3][:], st4[1][:], 1.0 / 64, st4[3][:], MUL, SUB, [st4[1], st4[3]])
                act(st4[4], st4[4][:], st4[3][:], AF.Sqrt, [st4[3], misc], bias=eps_ap)
                recip(st4[5], st4[5][:], st4[4][:], [st4[4]])
                for h in range(4):
                    ts(DVE, onb, onb[:, h * 64:(h + 1) * 64], osbf[:, h * 64:(h + 1) * 64], st4[2][:, h:h + 1], st4[5][:, h:h + 1], SUB, MUL, [osbf, st4[2], st4[5]])
                if 'ret2' in parts: continue
                for ti in range(2):
                    transpose(B[5], pv[:, ti * 128:(ti + 1) * 128], onb[:, ti * 128:(ti + 1) * 128], identb, [onb, cmats])
                cp(ACT, yTt[4], yTb[:, 4:6, n * 128:(n + 1) * 128], pv[:, 0:256].rearrange("p (a b) -> p a b", a=2), [B[5]])
                yTt[5].w = yTt[4].w; yTt[5].r = {}
                if 'ret3' in parts: continue
                for ti in range(2):
                    transpose(B[6], pv2[:, ti * 128:(ti + 1) * 128], krot[:, ti, n * 128:(n + 1) * 128], identb, [krot, cmats])
                tt(DVE, kz, kz[:], pv2[:, 0:256], ztab[:], MUL, [B[6], ztab])
                kvb = B[7]
                for ti in range(2):
                    mm(kvb, kvb[:, ti * 128:(ti + 1) * 128], kz[:, ti * 128:(ti + 1) * 128], vt[:, n, ti * 128:(ti + 1) * 128], True, True, [kz, vt])
                tt(DVE, osq, osq.ap.rearrange("p (a b) -> p a b", a=2), kvb[:, 0:256].rearrange("p (a b) -> p a b", a=2),
                   cmats[:, 7:8, :].to_broadcast([128, 2, 128]), MUL, [kvb, cmats])
                for ti in range(2):
                    stt(Rst, Rst[:, ti, :], Rst[:, ti, :], gdec[:, ti:ti + 1], osq[:, ti * 128:(ti + 1) * 128], MUL, ADD, [Rst, gdec, osq])
                cp(POOL, Rb, Rb[:], Rst[:], [Rst])
                yield

        def sec_lru():
            for i in range(2):
                xc = LT[0]; r = LT[1]; ig = LT[2]; a_ = LT[3]; h_ = LT[4]
                ts(DVE, xc, xc[:], uL[:, i, 0:TC], pp[:, 20 + 4 * i:21 + 4 * i], pp[:, 28 + i:29 + i], MUL, ADD, [uL, pp])
                for kk in range(1, 4):
                    stt(xc, xc[:], uL[:, i, kk:kk + TC], pp[:, 20 + 4 * i + kk:21 + 4 * i + kk], xc[:], MUL, ADD, [uL, pp, xc])
                cp(POOL, xcb, xcb[:], xc[:], [xc])
                mm(B[0], B[0][:], WA[:, i, :], xcb[:], True, True, [WA, xcb]); mm(B[1], B[1][:], WX[:, i, :], xcb[:], True, True, [WX, xcb])
                act(r, r[:], B[0][:], AF.Sigmoid, [B[0], pp], bias=pp[:, 30 + i:31 + i])
                act(ig, ig[:], B[1][:], AF.Sigmoid, [B[1], pp], bias=pp[:, 32 + i:33 + i])
                act(a_, a_[:], r[:], AF.Exp, [r, cl], scale=cl[:, i:i + 1])
                act(r, r[:], r[:], AF.Exp, [r, cl2], scale=cl2[:, i:i + 1])
                act(r, r[:], r[:], AF.Sqrt, [r, misc], bias=one_ap, scale=-1.0)
                tt(POOL, ig, ig[:], ig[:], xc[:], MUL, [ig, xc]); tt(DVE, r, r[:], r[:], ig[:], MUL, [r, ig])
                k.op(DVE, lambda e: e.tensor_tensor_scan(out=h_[:], data0=a_[:], data1=r[:], initial=hst[:, i:i + 1], op0=MUL, op1=ADD),
                     reads=[a_, r, hst], writes=[h_])
                cp(ACT, hst, hst[:, i:i + 1], h_[:, TC - 1:TC], [h_])
                cp(POOL, yTt[6 + i], yTb[:, 6 + i, :], h_[:], [h_])
                cp(POOL, uL, uL[:, i, 0:3], uL[:, i, TC:TC + 3], [uL])
                yield

        def sec_p():
            bk = B[5]
            for (ct, ctsw, dst) in ((8, 56, qrot), (10, 58, krot)):
                for i in range(2):
                    b = proj_fm(l, ct + i, bk); tt(DVE, rt1, rt1[:], b[:], ropeC[:], MUL, [b, ropeC])
                    b2 = proj_fm(l, ctsw + i, bk); tt(DVE, rt2, rt2[:], b2[:], ropeS[:], MUL, [b2, ropeS])
                    tt(POOL, dst, dst[:, i, :], rt1[:], rt2[:], ADD, [rt1, rt2])
                    yield
            proj_tm(l, 12, vt, lambda blk: vt[:, blk, :], [bk])
            yield
            for i in range(2):
                b = proj_fm(l, 14 + i, bk); cp(DVE, uL, uL[:, i, 3:3 + TC], b[:], [b])
                yield

        def run_wave(gens):
            st = [[g, n, 0, True] for g, n in gens]
            while any(x[3] for x in st):
                cand = [x for x in st if x[3]]
                x = min(cand, key=lambda y: y[2] / max(y[1], 1))
                try:
                    next(x[0]); x[2] += 1
                except StopIteration:
                    x[3] = False
        npairs = sum(c * 4 + qi + 1 for qi in range(4))
        wave_a = []
        if 'sb' in parts: wave_a.append((sec_sb(), 5 * npairs + 2))
        if 's5' in parts: wave_a.append((sec_s5(), 16 * 6 + 1))
        wave_a.append((sec_p(), 8))
        run_wave(wave_a)
        wave_b = []
        if 'ret' in parts: wave_b.append((sec_ret(), 4))
        if 'lru' in parts: wave_b.append((sec_lru(), 2))
        run_wave(wave_b)

    def phaseO(l, c):
        xc_ = xTc[c]
        for ct in range(8):
            b = proj_fm(l, 16 + ct)
            act(prodS[ct % 4], prodS[ct % 4][:], b[:], AF.Silu, [b])
            tt(DVE, yTt[ct], yTb[:, ct, :], yTb[:, ct, :], prodS[ct % 4][:], MUL, [yTt[ct], prodS[ct % 4]])
        if 'O1' in parts: return
        for f in range(8):
            wb = WB[f % 2]
            k.dma(POOL, wb[:], wbr_d[l].rearrange("n (k p) d -> p (n k) d", p=128)[:, :, f * 128:(f + 1) * 128], writes=[wb])
            for n in range(4):
                bm = proj_fm(l, 24 + n * 8 + f)
                by = B[2 + n % 2]
                for kk in range(2):
                    mm(by, by[:], wb[:, n * 2 + kk, :], yTb[:, 2 * n + kk, :], kk == 0, kk == 1, [wb, yTt[2 * n + kk]])
                s_ = sg[n % 2]
                act(s_, s_[:], bm[:], AF.Sigmoid, [bm])
                tt(DVE, prodS[n], prodS[n][:], s_[:], by[:], MUL, [s_, by])
            bs = B[4]
            for n in range(4):
                mm(bs, bs[:], identb, prodS[n][:], n == 0, n == 3, [cmats, prodS[n]])
            cp(ACT, merged, merged[:, f, :], bs[:], [bs])
        if 'O2' in parts: return
        for half in range(2):
            cs_ = slice(half * 256, (half + 1) * 256)
            ssb = B[5]
            pend = None
            for e in range(8):
                w = load_w(wout_d[l].rearrange("(k p) c -> p k c", p=128)[:, :, e * 128:(e + 1) * 128])
                b = B[pbn[0] % 2]; pbn[0] += 1
                for d in range(8):
                    mm(b, b[:, 0:256], w[:, d, :], merged[:, d, cs_], d == 0, d == 7, [w, merged])
                if pend is not None:
                    pe_ = pend
                    mm(ssb, ssb[:, 0:256], onesb, sqb[pe_ % 2][:, 0:256], pe_ == 0, pe_ == 7, [cmats, sqb[pe_ % 2]])
                cp(DVE, outT, outT[:, e, :], b[:, 0:256], [b])
                act(sqb[e % 2], sqb[e % 2][:, 0:256], outT[:, e, :], AF.Square, [outT])
                pend = e
            mm(ssb, ssb[:, 0:256], onesb, sqb[pend % 2][:, 0:256], pend == 0, pend == 7, [cmats, sqb[pend % 2]])
            if 'x1' in parts or 'x2' in parts: continue
            act(sg[0], sg[0][:, 0:256], ssb[:, 0:256], AF.Sqrt, [ssb, misc], bias=eps_ap, scale=1.0 / D)
            recip(rstd, rstd[:, 0:256], sg[0][:, 0:256], [sg[0]])
            for e in range(8):
                tb_ = sg[e % 2]
                stt(tb_, tb_[:, 0:256], outT[:, e, :], pp[:, 8 + e:9 + e], rstd[:, 0:256], MUL, MUL, [outT, pp, rstd])
                tt(POOL, xc_, xc_[:, e, cs_], xc_[:, e, cs_], tb_[:, 0:256], ADD, [xc_, tb_])

    for sq in range(NSEQ):
        k.barrier(); load_seq(sq)
        for l in range(L):
            k.barrier()
            if 'setup' in parts: layer_setup(l)
            k.barrier()
            if 'prenorm' in parts: prenorm(l, 0)
            k.barrier()
            for c in range(NCH):
                if 'proj' in parts: phaseM(l, c)
                if "yT" in dbg_d and sq == 0 and l == 0:
                    k.dma(POOL, dbg_d["yT"][:, :, c * TC:(c + 1) * TC], yTb[:], reads=yTt)
                k.barrier()
                if 'O' in parts: phaseO(l, c)
                if c + 1 < NCH and 'prenorm' in parts: prenorm(l, c + 1)
                k.barrier()
        store_seq(sq)
    k.barrier()
    import os
    if os.environ.get('KSTAT'): print('ENGINE COUNTS', {E.name: E.cnt for E in k.engs}, 'dma', {E.name: sum(E.dvals)//16 for E in k.engs})
    return nc


def _prep_shared(inp, S, L):
    f32 = np.float32
    g = lambda n: np.asarray(inp[n], dtype=f32)[:L]
    w_in = g('w_in')
    perm = []
    for base in (1024, 1280):
        for h in range(4):
            b0 = base + 64 * h
            perm += list(range(b0 + 32, b0 + 64)) + list(range(b0, b0 + 32))
    win = np.ascontiguousarray(np.concatenate([w_in, w_in[:, :, perm]], axis=2))
    pre_g, post_g = g('pre_norm_g'), g('post_norm_g')
    pp = np.zeros((L, 128, 40), f32)
    for l in range(L):
        pp[l, :, 0:8] = pre_g[l].reshape(8, 128).T
        pp[l, :, 8:16] = post_g[l].reshape(8, 128).T
        pp[l, :, 16:18] = g('ssm_d')[l].reshape(2, 128).T
        pp[l, :, 18:20] = g('ssm_b_glu')[l].reshape(2, 128).T
        cw = g('lru_conv_w')[l]
        for i in range(2):
            pp[l, :, 20 + 4 * i:24 + 4 * i] = cw[:, 128 * i:128 * (i + 1)].T
        pp[l, :, 28:30] = g('lru_conv_b')[l].reshape(2, 128).T
        pp[l, :, 30:32] = g('lru_b_a')[l].reshape(2, 128).T
        pp[l, :, 32:34] = g('lru_b_x')[l].reshape(2, 128).T
        pp[l, :, 34:36] = g('lru_lambda')[l].reshape(2, 128).T
    lruw = np.zeros((L, 2, 128, 2, 128), f32)
    for l in range(L):
        for ax, nm in enumerate(('lru_w_a', 'lru_w_x')):
            w = g(nm)[l]
            for i in range(2):
                for bl in range(2):
                    lruw[l, ax, 64 * bl:64 * bl + 64, i, 64 * bl:64 * bl + 64] = w[2 * i + bl]
    a_re, a_im, ldt = g('ssm_a_re'), g('ssm_a_im'), g('ssm_log_dt')
    s5row = np.stack([a_re.reshape(L, 1024), a_im.reshape(L, 1024), np.repeat(ldt, 64, axis=1)], axis=1)
    s5col = np.zeros((L, 128, 3, 16), f32)
    for l in range(L):
        s5col[l, :, 0, :] = np.concatenate([a_re[l].T, a_re[l].T], axis=0)
        s5col[l, :, 1, :] = np.concatenate([a_im[l].T, a_im[l].T], axis=0)
        s5col[l, :, 2, :] = np.broadcast_to(ldt[l][None, :], (128, 16))
    bblk = np.zeros((L, 2, 128, 2, 512), f32)
    for l in range(L):
        for ri, nm in enumerate(('ssm_b_re', 'ssm_b_im')):
            bb = g(nm)[l]
            for gg in range(16):
                kh, gl = gg // 8, gg % 8
                bblk[l, ri, gl * 16:(gl + 1) * 16, kh, gl * 64:(gl + 1) * 64] = bb[gg].T
    cab = np.zeros((L, 128, 32, 128), f32)
    c_re, c_im = g('ssm_c_re'), g('ssm_c_im')
    for l in range(L):
        for gg in range(16):
            co = 16 * (gg % 8)
            cab[l, 0:64, 2 * gg, co:co + 16] = c_re[l, gg].T
            cab[l, 64:128, 2 * gg, co:co + 16] = c_im[l, gg].T
            cab[l, 0:64, 2 * gg + 1, co:co + 16] = c_im[l, gg].T
            cab[l, 64:128, 2 * gg + 1, co:co + 16] = c_re[l, gg].T
    p = np.arange(128)
    invf = (np.float32(10000.0) ** (-(np.arange(32, dtype=f32) / np.float32(32)))).astype(f32)
    ang = (np.arange(S, dtype=f32)[None, :] * invf[(p % 64) % 32][:, None]).astype(f32)
    ropeC = np.cos(ang).astype(f32)
    ropeS = (np.sin(ang) * np.where((p % 64) < 32, -1.0, 1.0)[:, None]).astype(f32)
    gam = 1.0 - 2.0 ** (-5.0 - np.arange(4))
    ii = np.arange(128)
    dtab = np.zeros((128, 2, 256), f32)
    for par in range(2):
        for ti in range(2):
            h = 2 * ti + par
            rel = ii[None, :] - ii[:, None]
            dtab[:, par, ti * 128:(ti + 1) * 128] = np.where(rel >= 0, gam[h] ** np.maximum(rel, 0), 0.0) / 8.0
    ztab = np.zeros((128, 256), f32)
    for h in range(4):
        ztab[:, h * 64:(h + 1) * 64] = (gam[h] ** (127 - ii) / 8.0)[:, None]
    xitab = np.zeros((128, 2, 128), f32); gdec = np.zeros((128, 2), f32)
    for ti in range(2):
        for hl in range(2):
            h = 2 * ti + hl
            xitab[64 * hl:64 * hl + 64, ti, :] = (gam[h] ** (ii + 1.0))[None, :]
            gdec[64 * hl:64 * hl + 64, ti] = gam[h] ** 128
    cm = np.zeros((128, 8, 128), f32)
    cm[:, 0, :] = np.eye(128)
    cm[:, 1, :] = -1.0 * (ii[:, None] >= ii[None, :])
    cm[:, 2, :] = -1.0
    cm[:, 3, :] = (ii[:, None] <= ii[None, :])
    cm[:, 4, :] = (ii[:, None] < ii[None, :])
    for m in range(64):
        cm[m + 64, 5, m] = -1.0
        cm[m, 5, m + 64] = 1.0
    cm[:, 6, :] = 1.0
    cm[0:64, 7, 0:64] = 1.0; cm[64:128, 7, 64:128] = 1.0
    misc = np.zeros((128, 132), f32)
    misc[:, 0] = p; misc[:, 1] = np.where(p < 64, 1.0, -1.0); misc[:, 2] = EPS; misc[:, 3] = 1.0
    misc[:, 4:132] = np.arange(128)[None, :]
    return dict(win=win, wbr=g('w_branch'), wout=g('w_out'), wglu=g('ssm_w_glu'), pp=pp, lruw=lruw, s5row=np.ascontiguousarray(s5row),
                s5col=s5col, bblk=bblk, cab=cab, ropeC=ropeC, ropeS=ropeS, dtab=dtab, ztab=ztab, xitab=xitab, gdec=gdec,
                cmats=cm, misc=misc)


def run(inp, S, NSEQ, DEPTH, ncores, dbg=None, parts=None):
    shared = _prep_shared(inp, S, DEPTH)
    x = np.asarray(inp['x'], dtype=np.float32)
    nc = build(S, NSEQ, DEPTH, dbg, parts)
    in_maps = []
    for i in range(ncores):
        m = dict(shared); m['x'] = np.ascontiguousarray(x[i * NSEQ:(i + 1) * NSEQ]); in_maps.append(m)
    res = run_bass_kernel_spmd(nc, in_maps, core_ids=list(range(ncores)))
    return res


def kernel(**inputs):
    x = np.asarray(inputs['x'])
    Bn, S, _ = x.shape
    ncores = 8
    NSEQ = Bn // ncores
    res = run(inputs, S, NSEQ, 2, ncores)
    return np.concatenate([r["y"] for r in res.results], axis=0).astype(np.float32)
```

```python
import math
import numpy as np
import concourse.bass as bass
import concourse.mybir as mybir
from concourse.bass_utils import run_bass_kernel_spmd

F32 = mybir.dt.float32; BF16 = mybir.dt.bfloat16; I32 = mybir.dt.int32
AF = mybir.ActivationFunctionType; ALU = mybir.AluOpType; AX = mybir.AxisListType
D = 1024; TC = 512; NBK = 4; EPS = 1e-6
NCOLT = 60
PI = math.pi


class T:
    __slots__ = ('ap', 'w', 'r')
    def __init__(s, ap): s.ap = ap; s.w = None; s.r = {}
    def __getitem__(s, idx): return s.ap[idx]


class Eng:
    def __init__(s, nc, name, eng, is_pe=False):
        s.name = name; s.eng = eng; s.sem = nc.alloc_semaphore("sem_" + name); s.cnt = 0; s.real = 0; s.known = {}; s.is_pe = is_pe
        s.dsems = []; s.dvals = []; s.dnext = 0


class K:
    def __init__(s, nc, ndma=(16, 16, 4), needed=None):
        s.nc = nc; s.needed = needed; s.waited = set(); s.rmap = {}
        s.PE = Eng(nc, "pe", nc.tensor, True); s.ACT = Eng(nc, "act", nc.scalar); s.DVE = Eng(nc, "dve", nc.vector)
        s.POOL = Eng(nc, "pool", nc.gpsimd); s.SP = Eng(nc, "sp", nc.sync)
        s.engs = [s.PE, s.ACT, s.DVE, s.POOL, s.SP]
        for E, n in ((s.SP, ndma[0]), (s.POOL, ndma[1]), (s.ACT, ndma[2])):
            E.dsems = [nc.alloc_semaphore(f"d_{E.name}_{i}") for i in range(n)]; E.dvals = [0] * n
        s.nt = 0
    def sb(s, shape, dt, name=None):
        s.nt += 1
        return T(s.nc.alloc_sbuf_tensor("s_" + (name or f"t{s.nt}"), list(shape), dt).ap())
    def ps(s, shape, dt=F32, name=None):
        s.nt += 1
        return T(s.nc.alloc_psum_tensor("ps_" + (name or f"p{s.nt}"), list(shape), dt).ap())
    def _waits(s, E, reads, writes):
        deps = {}
        def add(ev, war=False):
            if ev is None: return
            key, sem, val, who = ev
            if who is E and (E.is_pe or war): return
            if deps.get(key, (None, 0))[1] < val: deps[key] = (sem, val)
        for t in reads: add(t.w)
        for t in writes:
            add(t.w)
            for ev in t.r.values(): add(ev, True)
        for key, (sem, val) in deps.items():
            if E.known.get(key, 0) >= val: continue
            s._wait(E, key, sem, val)
    def _wait(s, E, key, sem, val):
        rv = val
        if not key.startswith("d_"):
            s.waited.add((key, val))
            if s.needed is not None: rv = s.rmap[(key, val)]
        E.eng.wait_ge(sem, rv); E.known[key] = val
    def _post(s, ev, reads, writes):
        for t in writes: t.w = ev; t.r = {}
        for t in reads: t.r[ev[0]] = ev
    def op(s, E, fn, reads=(), writes=()):
        s._waits(E, reads, writes)
        ins = fn(E.eng); E.cnt += 1
        if s.needed is None or (E.name, E.cnt) in s.needed:
            E.real += 1; ins.then_inc(E.sem, 1); s.rmap[(E.name, E.cnt)] = E.real
        s._post((E.name, E.sem, E.cnt, E), reads, writes)
        return ins
    def dma(s, Q, out, in_, reads=(), writes=(), **kw):
        i = Q.dnext; Q.dnext = (i + 1) % len(Q.dsems); sem = Q.dsems[i]; key = f"d_{Q.name}_{i}"
        if Q.dvals[i] > 0 and Q.known.get(key, 0) < Q.dvals[i]:
            Q.eng.wait_ge(sem, Q.dvals[i]); Q.known[key] = Q.dvals[i]
        s._waits(Q, reads, writes)
        ins = Q.eng.dma_start(out=out, in_=in_, **kw); Q.dvals[i] += 16; ins.then_inc(sem, 16)
        ev = (key, sem, Q.dvals[i], None)
        s._post(ev, reads, writes)
        return ev
    def barrier(s):
        for E in s.engs:
            for Fg in s.engs:
                if Fg is E or Fg.cnt == 0: continue
                if E.known.get(Fg.name, 0) < Fg.cnt:
                    s._wait(E, Fg.name, Fg.sem, Fg.cnt)
            for Q in s.engs:
                for i, sem in enumerate(Q.dsems):
                    key = f"d_{Q.name}_{i}"
                    if Q.dvals[i] > 0 and E.known.get(key, 0) < Q.dvals[i]:
                        E.eng.wait_ge(sem, Q.dvals[i]); E.known[key] = Q.dvals[i]


def build(S, NSEQ, DEPTH, dbg=None, parts=None):
    _, k1 = _build(S, NSEQ, DEPTH, dbg, parts, None)
    nc, k2 = _build(S, NSEQ, DEPTH, dbg, parts, set(k1.waited))
    return nc


def _build(S, NSEQ, DEPTH, dbg, parts, needed):
    nc = bass.Bass("TRN2", target_bir_lowering=False)
    k = K(nc, needed=needed)
    if parts is None: parts = {"setup", "prenorm", "proj", "sb", "s5", "ret", "lru", "O"}
    PE, ACT, DVE, POOL, SP = k.PE, k.ACT, k.DVE, k.POOL, k.SP
    NCH = S // TC
    NBLK = S // 128
    L = DEPTH
    def din(name, shape): return nc.dram_tensor(name, list(shape), F32, kind="ExternalInput").ap()
    x_d = din("x", [NSEQ, S, D]); win_d = din("win", [L, D, 7680]); wbr_d = din("wbr", [L, 4, 256, D])
    wout_d = din("wout", [L, D, D]); wglu_d = din("wglu", [L, 256, 256]); pp_d = din("pp", [L, 128, 40])
    lruw_d = din("lruw", [L, 2, 128, 2, 128]); s5row_d = din("s5row", [L, 3, 1024]); s5col_d = din("s5col", [L, 128, 3, 16])
    bblk_d = din("bblk", [L, 2, 128, 2, 512]); cab_d = din("cab", [L, 128, 32, 128])
    ropeC_d = din("ropeC", [128, S]); ropeS_d = din("ropeS", [128, S])
    dtab_d = din("dtab", [128, 2, 256]); ztab_d = din("ztab", [128, 256]); xitab_d = din("xitab", [128, 2, 128])
    gdec_d = din("gdec", [128, 2]); cm_d = din("cmats", [128, 8, 128]); misc_d = din("misc", [128, 132])
    y_d = nc.dram_tensor("y", [NSEQ, S, D], F32, kind="ExternalOutput").ap()
    dbg_d = {}
    if dbg:
        for nm, shp in dbg.items():
            dbg_d[nm] = nc.dram_tensor(nm, list(shp), F32, kind="ExternalOutput").ap()

    xTb = nc.alloc_sbuf_tensor("s_xT", [128, 8, S], F32).ap()
    xTc = [T(xTb[:, :, c * TC:(c + 1) * TC]) for c in range(S // TC)]
    hT = k.sb([128, 8, TC], BF16, "hT")
    WT = [k.sb([128, 8, 128], BF16, f"WT{i}") for i in range(5)]
    wtn = [0]
    pp = k.sb([128, 40], F32, "pp")
    cmats = k.sb([128, 8, 128], BF16, "cmats")
    identb = cmats[:, 0, :]; ntri = cmats[:, 1, :]; nones = cmats[:, 2, :]; tri = cmats[:, 3, :]
    Jm = cmats[:, 5, :]; onesb = cmats[:, 6, :]
    identf = k.sb([128, 128], F32, "identf")
    misc = k.sb([128, 132], F32, "misc")
    mask2 = k.sb([128, 256], BF16, "mask2")
    zrow = k.sb([1, 256], BF16, "zrow")
    ropeC = k.sb([128, TC], BF16, "ropeC"); ropeS = k.sb([128, TC], BF16, "ropeS")
    dtab = k.sb([128, 2, 256], BF16, "dtab"); ztab = k.sb([128, 256], BF16, "ztab"); xitab = k.sb([128, 2, 128], BF16, "xitab")
    gdec = k.sb([128, 2], F32, "gdec")
    KT = k.sb([128, 2, S], BF16, "KT"); Vc = k.sb([128, NBLK, 256], BF16, "Vc"); QT = k.sb([128, 2, TC], BF16, "QT")
    sbE = [k.sb([128, 256], F32, f"sbE{i}") for i in range(2)]
    sbSP = [k.sb([128, 256], BF16, f"sbSP{i}") for i in range(2)]
    sbW = [k.sb([128, 256], BF16, f"sbW{i}") for i in range(2)]
    sbC = [k.sb([128, 256], BF16, f"sbC{i}") for i in range(2)]
    osb = k.sb([128, 256], BF16, "osb")
    uT = k.sb([128, 2, TC], BF16, "uT")
    Bx = [k.sb([128, 1024], BF16, f"Bx{i}") for i in range(2)]; Bsw = [k.sb([128, 1024], BF16, f"Bsw{i}") for i in range(2)]
    Pr = k.sb([128, 16, 64], BF16, "Pr"); Pi = k.sb([128, 16, 64], BF16, "Pi")
    T1 = k.sb([128, 16, 128], BF16, "T1"); T2 = k.sb([128, 16, 128], BF16, "T2")
    W1 = k.sb([128, 16], F32, "W1"); W2 = k.sb([128, 16], F32, "W2")
    Cm = k.sb([128, 32, 128], BF16, "Cm")
    G1 = [k.sb([128, 4, 128], BF16, f"G1_{i}") for i in range(2)]; G2 = [k.sb([128, 4, 128], BF16, f"G2_{i}") for i in range(2)]
    A1 = [k.sb([128, 4, 128], BF16, f"A1_{i}") for i in range(2)]; A2 = [k.sb([128, 4, 128], BF16, f"A2_{i}") for i in range(2)]
    carry = [k.sb([128, 4], F32, f"carry{i}") for i in range(4)]
    sfull = k.sb([128, 4], F32, "sfull"); U1 = k.sb([128, 4], BF16, "U1"); U2 = k.sb([128, 4], BF16, "U2")
    ygb = k.sb([128, 2, TC], BF16, "ygb"); wglu = k.sb([128, 2, 256], BF16, "wglu")
    uL = k.sb([128, 2, TC + 4], F32, "uL"); hst = k.sb([128, 2], F32, "hst")
    WA = k.sb([128, 2, 128], BF16, "WA"); WX = k.sb([128, 2, 128], BF16, "WX")
    cl = k.sb([128, 2], F32, "cl"); cl2 = k.sb([128, 2], F32, "cl2"); cltmp = k.sb([128, 2], F32, "cltmp")
    Rst = k.sb([128, 2, 128], F32, "Rst"); Rb = k.sb([128, 2, 128], BF16, "Rb")
    st4 = [k.sb([128, 4], F32, f"st4_{i}") for i in range(6)]
    yTb = nc.alloc_sbuf_tensor("s_yT", [128, 8, TC], BF16).ap()
    yTt = [T(yTb[:, i, :]) for i in range(8)]
    WB = [k.sb([128, 8, 128], BF16, f"WB{i}") for i in range(2)]
    colp = k.sb([128, 3, 16], F32, "colp"); colq = k.sb([128, 3, 16], F32, "colq")
    UN = nc.alloc_sbuf_tensor("UN", [128, 7168], F32).ap()
    def carve(off_f32, n_f32, dt, shape):
        v = UN[:, off_f32:off_f32 + n_f32]
        if dt is BF16: v = v.bitcast(BF16)
        if len(shape) == 3: v = v.rearrange("p (a b) -> p a b", a=shape[1])
        return T(v)
    LT = [carve(512 * i, 512, F32, [128, 512]) for i in range(5)]
    xcb = carve(2560, 256, BF16, [128, 512])
    qrot = carve(2816, 512, BF16, [128, 2, 512]); krot = carve(3328, 512, BF16, [128, 2, 512])
    qxi = carve(3840, 512, BF16, [128, 2, 512])
    vt = carve(4352, 512, BF16, [128, 4, 256]); kz = carve(4864, 128, BF16, [128, 256])
    PTt = carve(4992, 256, BF16, [128, 2, 256])
    osbf = carve(5248, 256, F32, [128, 256]); osq = carve(5504, 256, F32, [128, 256]); onb = carve(5760, 128, BF16, [128, 256])
    rt1 = carve(5888, 512, F32, [128, 512]); rt2 = carve(6400, 512, F32, [128, 512])
    merged = carve(0, 2048, BF16, [128, 8, 512])
    prodS = [carve(2048 + 256 * i, 256, BF16, [128, 512]) for i in range(4)]
    sg = [carve(3072, 512, F32, [128, 512]), carve(3584, 512, F32, [128, 512])]
    outT = carve(4096, 2048, F32, [128, 8, 256])
    stg = [carve(4096, 1024, F32, [128, 1024]), carve(5120, 1024, F32, [128, 1024])]
    sqb = [carve(6144, 256, BF16, [128, 512]), carve(6400, 256, BF16, [128, 512])]
    rstd = carve(6656, 512, F32, [128, 512])
    SRt = [carve(512 * i, 512, F32, [128, 512]) for i in range(14)]
    B = [k.ps([128, 512], F32, f"bank{i}") for i in range(8)]

    def mm(out_t, out_ap, lhsT, rhs, start, stop, reads):
        k.op(PE, lambda e: e.matmul(out_ap, lhsT=lhsT, rhs=rhs, start=start, stop=stop), reads=reads, writes=[out_t])
    def act(out_t, out_ap, in_ap, func, reads, bias=None, scale=None):
        kw = {}
        if bias is not None: kw['bias'] = bias
        if scale is not None: kw['scale'] = scale
        k.op(ACT, lambda e: e.activation(out=out_ap, in_=in_ap, func=func, **kw), reads=reads, writes=[out_t])
    def tt(E, out_t, out_ap, a, b, op, reads):
        k.op(E, lambda e: e.tensor_tensor(out=out_ap, in0=a, in1=b, op=op), reads=reads, writes=[out_t])
    def ts(E, out_t, out_ap, a, s1, s2, op0, op1, reads):
        if op1 is None:
            k.op(E, lambda e: e.tensor_scalar(out=out_ap, in0=a, scalar1=s1, scalar2=None, op0=op0), reads=reads, writes=[out_t])
        else:
            k.op(E, lambda e: e.tensor_scalar(out=out_ap, in0=a, scalar1=s1, scalar2=s2, op0=op0, op1=op1), reads=reads, writes=[out_t])
    def stt(out_t, out_ap, a, sc, b, op0, op1, reads):
        k.op(DVE, lambda e: e.scalar_tensor_tensor(out=out_ap, in0=a, scalar=sc, in1=b, op0=op0, op1=op1), reads=reads, writes=[out_t])
    def cp(E, out_t, out_ap, in_ap, reads):
        if E is ACT:
            k.op(E, lambda e: e.copy(out=out_ap, in_=in_ap), reads=reads, writes=[out_t])
        else:
            k.op(E, lambda e: e.tensor_copy(out=out_ap, in_=in_ap), reads=reads, writes=[out_t])
    def memset(E, t, ap, v):
        k.op(E, lambda e: e.memset(ap, v), writes=[t])
    def dbg_out(name, t, ap):
        if name in dbg_d:
            k.dma(SP, dbg_d[name], ap, reads=[t])

    def load_w(src_ap):
        w = WT[wtn[0] % 5]; wtn[0] += 1
        k.dma(POOL, w[:], src_ap, writes=[w])
        return w
    def win_tile(l, ct):
        return win_d[l].rearrange("(k p) c -> p k c", p=128)[:, :, ct * 128:(ct + 1) * 128]
    pbn = [0]
    def proj_fm(l, ct, bank=None):
        w = load_w(win_tile(l, ct))
        if bank is None:
            b = B[pbn[0] % 2]; pbn[0] += 1
        else:
            b = bank
        for kk in range(8):
            mm(b, b[:], w[:, kk, :], hT[:, kk, :], kk == 0, kk == 7, [w, hT])
        return b
    def proj_tm(l, ct0, dst_t, dst_fn, banks=None):
        w0 = load_w(win_tile(l, ct0)); w1 = load_w(win_tile(l, ct0 + 1))
        for half in range(2):
            if banks is None:
                b = B[pbn[0] % 2]; pbn[0] += 1
            else:
                b = banks[half % len(banks)]
            for bi in range(2):
                blk = half * 2 + bi
                for j, w in enumerate((w0, w1)):
                    o = bi * 256 + j * 128
                    for kk in range(8):
                        mm(b, b[:, o:o + 128], hT[:, kk, blk * 128:(blk + 1) * 128], w[:, kk, :], kk == 0, kk == 7, [w, hT])
            for bi in range(2):
                blk = half * 2 + bi
                cp(ACT, dst_t, dst_fn(blk), b[:, bi * 256:(bi + 1) * 256], [b])

    def sincos(ang, vw, tmps):
        ki, kr, sn, cs = tmps
        kiv = vw(ki).bitcast(I32)
        ts(DVE, kr, vw(kr), vw(ang), 1.0 / (2 * PI), None, ALU.mult, None, [ang])
        cp(DVE, ki, kiv, vw(kr), [kr])
        cp(DVE, kr, vw(kr), kiv, [ki])
        C1 = 6.28125; C2 = 2 * PI - 6.28125
        stt(sn, vw(sn), vw(kr), -C1, vw(ang), ALU.mult, ALU.add, [kr, ang])
        stt(sn, vw(sn), vw(kr), -C2, vw(sn), ALU.mult, ALU.add, [kr, sn])
        ts(DVE, sn, vw(sn), vw(sn), -PI, PI, ALU.max, ALU.min, [sn])
        ts(DVE, cs, vw(cs), vw(sn), PI / 2, -2 * PI, ALU.is_gt, ALU.mult, [sn])
        stt(cs, vw(cs), vw(sn), PI / 2, vw(cs), ALU.add, ALU.add, [sn, cs])
        ts(DVE, cs, vw(cs), vw(cs), -PI, PI, ALU.max, ALU.min, [cs])
        act(cs, vw(cs), vw(cs), AF.Sin, [cs])
        act(sn, vw(sn), vw(sn), AF.Sin, [sn])

    k.dma(POOL, cmats[:], cm_d, writes=[cmats])
    k.dma(SP, misc[:], misc_d, writes=[misc])
    k.dma(POOL, dtab[:], dtab_d, writes=[dtab]); k.dma(POOL, ztab[:], ztab_d, writes=[ztab]); k.dma(POOL, xitab[:], xitab_d, writes=[xitab])
    k.dma(SP, gdec[:], gdec_d, writes=[gdec])
    memset(POOL, zrow, zrow[:], 0.0)
    for i in range(2):
        cp(POOL, mask2, mask2[:, i * 128:(i + 1) * 128], cmats[:, 4, :], [cmats])
    cp(DVE, identf, identf[:], cmats[:, 0, :], [cmats])
    tauc = misc[:, 0:1]; sgn1 = misc[:, 1:2]; eps_ap = misc[:, 2:3]; one_ap = misc[:, 3:4]; taurow = misc[:, 4:132]


    flat = lambda t: t.ap
    v3 = lambda t: t.ap.rearrange("p (g n) -> p g n", g=8)
    v4 = lambda t: t.ap.rearrange("p (g n) -> p g n", g=4)
    sm = lambda t: t.ap[:, 0:16]
    MUL, ADD, SUB = ALU.mult, ALU.add, ALU.subtract

    def recip(out_t, out_ap, in_ap, reads):
        k.op(DVE, lambda e: e.reciprocal(out=out_ap, in_=in_ap), reads=reads, writes=[out_t])
    def transpose(out_t, out_ap, in_ap, ident_ap, reads):
        k.op(PE, lambda e: e.transpose(out=out_ap, in_=in_ap, identity=ident_ap), reads=reads, writes=[out_t])

    def layer_setup(l):
        k.dma(SP, pp[:], pp_d[l], writes=[pp])
        k.dma(POOL, WA[:], lruw_d[l, 0], writes=[WA]); k.dma(POOL, WX[:], lruw_d[l, 1], writes=[WX])
        k.dma(POOL, wglu[:], wglu_d[l].rearrange("(k p) c -> p k c", p=128), writes=[wglu])
        k.dma(POOL, Cm[:], cab_d[l], writes=[Cm])
        k.dma(SP, colp[:], s5col_d[l], writes=[colp])
        act(cltmp, cltmp[:], pp[:, 34:36], AF.Exp, [pp], scale=-1.0)
        act(cltmp, cltmp[:], cltmp[:], AF.Ln, [cltmp, misc], bias=one_ap)
        ts(DVE, cl, cl[:], cltmp[:], -8.0, None, MUL, None, [cltmp])
        ts(DVE, cl2, cl2[:], cltmp[:], -16.0, None, MUL, None, [cltmp])
        memset(POOL, uL, uL[:, :, 0:4], 0.0); memset(POOL, hst, hst[:], 0.0)
        memset(POOL, Rst, Rst[:], 0.0); memset(POOL, Rb, Rb[:], 0.0)
        for q in range(4): memset(POOL, carry[q], carry[q][:], 0.0)
        act(colq, colq[:, 0, :], colp[:, 2, :], AF.Exp, [colp])
        tt(DVE, colq, colq[:, 1, :], colq[:, 0, :], colp[:, 0, :], MUL, [colq, colp])
        tt(DVE, colq, colq[:, 2, :], colq[:, 0, :], colp[:, 1, :], MUL, [colq, colp])
        for kh in range(2):
            are, aim, dt_, dre, dim_, mag, ki, kr, sn, cs, fr, fi, bre, bim = SRt
            for i, t in enumerate((are, aim, dt_)):
                k.dma(SP, t[:], bass.AP(s5row_d.tensor, (l * 3 + i) * 1024 + kh * 512, [[0, 128], [1, 512]]), writes=[t])
            k.dma(SP, bre[:], bblk_d[l, 0][:, kh, :], writes=[bre]); k.dma(SP, bim[:], bblk_d[l, 1][:, kh, :], writes=[bim])
            act(dt_, dt_[:], dt_[:], AF.Exp, [dt_])
            tt(DVE, dre, dre[:], dt_[:], are[:], MUL, [dt_, are]); tt(DVE, dim_, dim_[:], dt_[:], aim[:], MUL, [dt_, aim])
            act(mag, mag[:], dre[:], AF.Exp, [dre])
            sincos(dim_, flat, (ki, kr, sn, cs))
            tt(DVE, cs, cs[:], mag[:], cs[:], MUL, [mag, cs]); tt(DVE, sn, sn[:], mag[:], sn[:], MUL, [mag, sn])
            ts(DVE, cs, cs[:], cs[:], -1.0, None, ADD, None, [cs])
            tt(DVE, mag, mag[:], are[:], are[:], MUL, [are]); tt(DVE, ki, ki[:], aim[:], aim[:], MUL, [aim])
            tt(DVE, mag, mag[:], mag[:], ki[:], ADD, [mag, ki]); recip(mag, mag[:], mag[:], [mag])
            tt(DVE, fr, fr[:], cs[:], are[:], MUL, [cs, are]); tt(DVE, ki, ki[:], sn[:], aim[:], MUL, [sn, aim])
            tt(DVE, fr, fr[:], fr[:], ki[:], ADD, [fr, ki]); tt(DVE, fr, fr[:], fr[:], mag[:], MUL, [fr, mag])
            tt(DVE, fi, fi[:], sn[:], are[:], MUL, [sn, are]); tt(DVE, ki, ki[:], cs[:], aim[:], MUL, [cs, aim])
            tt(DVE, fi, fi[:], fi[:], ki[:], SUB, [fi, ki]); tt(DVE, fi, fi[:], fi[:], mag[:], MUL, [fi, mag])
            tt(DVE, kr, kr[:], fr[:], bre[:], MUL, [fr, bre]); tt(DVE, ki, ki[:], fi[:], bim[:], MUL, [fi, bim])
            tt(DVE, sn, sn[:], fr[:], bim[:], MUL, [fr, bim]); tt(DVE, cs, cs[:], fi[:], bre[:], MUL, [fi, bre])
            bxv = Bx[kh].ap.rearrange("p (g a n) -> p g a n", g=8, a=2)
            bsv = Bsw[kh].ap.rearrange("p (g a n) -> p g a n", g=8, a=2)
            tt(DVE, Bx[kh], bxv[:, :, 0, :], v3(kr), v3(ki), SUB, [kr, ki])
            tt(DVE, Bsw[kh], bsv[:, :, 1, :], v3(kr), v3(ki), SUB, [kr, ki])
            tt(DVE, Bx[kh], bxv[:, :, 1, :], v3(sn), v3(cs), ADD, [sn, cs])
            stt(Bsw[kh], bsv[:, :, 0, :], v3(sn), -1.0, v3(cs), MUL, SUB, [sn, cs])
            ts(DVE, mag, mag[:], dim_[:], tauc, None, MUL, None, [dim_, misc])
            sincos(mag, flat, (ki, kr, sn, cs))
            ts(DVE, fr, fr[:], dre[:], tauc, None, MUL, None, [dre, misc]); act(fr, fr[:], fr[:], AF.Exp, [fr], scale=-1.0)
            tt(DVE, Pr, Pr[:, 8 * kh:8 * kh + 8, :], v3(fr), v3(cs), MUL, [fr, cs])
            stt(Pi, Pi[:, 8 * kh:8 * kh + 8, :], v3(fr), -1.0, v3(sn), MUL, MUL, [fr, sn])
        for q in range(4):
            ang, mexp, ki, kr, sn, cs = SRt[0:6]
            taub = taurow.unsqueeze(1).to_broadcast([128, 4, 128])
            dimb = colq[:, 2, 4 * q:4 * q + 4].unsqueeze(2).to_broadcast([128, 4, 128])
            dreb = colq[:, 1, 4 * q:4 * q + 4].unsqueeze(2).to_broadcast([128, 4, 128])
            tt(DVE, ang, v4(ang), dimb, taub, MUL, [colq, misc])
            tt(DVE, mexp, v4(mexp), dreb, taub, MUL, [colq, misc]); act(mexp, mexp[:], mexp[:], AF.Exp, [mexp])
            sincos(ang, flat, (ki, kr, sn, cs))
            stt(T1, T1[:, 4 * q:4 * q + 4, :], v4(mexp), sgn1, v4(cs), MUL, MUL, [mexp, misc, cs])
            stt(T2, T2[:, 4 * q:4 * q + 4, :], v4(mexp), -1.0, v4(sn), MUL, MUL, [mexp, sn])
        angw, mw, ki, kr, sn, cs = SRt[6:12]
        ts(DVE, angw, sm(angw), colq[:, 2, :], 128.0, None, MUL, None, [colq])
        ts(DVE, mw, sm(mw), colq[:, 1, :], 128.0, None, MUL, None, [colq]); act(mw, sm(mw), sm(mw), AF.Exp, [mw])
        sincos(angw, sm, (ki, kr, sn, cs))
        tt(DVE, W1, W1[:], sm(mw), sm(cs), MUL, [mw, cs]); tt(DVE, W2, W2[:], sm(mw), sm(sn), MUL, [mw, sn])

    def load_seq(sq):
        for blk in range(NBLK):
            st = stg[blk % 2]; c = blk // 4; o = (blk % 4) * 128
            k.dma(SP, st[:], x_d[sq, blk * 128:(blk + 1) * 128, :], writes=[st])
            for half in range(2):
                b = B[2 + half]
                for j in range(4):
                    f = half * 4 + j
                    transpose(b, b[:, j * 128:(j + 1) * 128], st[:, f * 128:(f + 1) * 128], identf[:], [st, identf])
                cp(DVE if half == 0 else ACT, xTc[c], xTc[c][:, half * 4:(half + 1) * 4, o:o + 128],
                   b[:].rearrange("p (a b) -> p a b", a=4), [b])

    def store_seq(sq):
        for blk in range(NBLK):
            st = stg[blk % 2]; c = blk // 4; o = (blk % 4) * 128
            for half in range(2):
                b = B[2 + half]
                for j in range(4):
                    f = half * 4 + j
                    transpose(b, b[:, j * 128:(j + 1) * 128], xTc[c][:, f, o:o + 128], identf[:], [xTc[c], identf])
                cp(DVE if half == 0 else ACT, st, st[:, half * 512:(half + 1) * 512], b[:], [b])
            k.dma(SP, y_d[sq, blk * 128:(blk + 1) * 128, :], st[:], reads=[st])

    def prenorm(l, c):
        xc_ = xTc[c]; ssb = B[5]
        for f in range(8):
            s_ = sqb[f % 2]
            act(s_, s_[:], xc_[:, f, :], AF.Square, [xc_])
            mm(ssb, ssb[:], onesb, s_[:], f == 0, f == 7, [cmats, s_])
        act(sg[0], sg[0][:], ssb[:], AF.Sqrt, [ssb, misc], bias=eps_ap, scale=1.0 / D)
        recip(rstd, rstd[:], sg[0][:], [sg[0]])
        for f in range(8):
            stt(hT, hT[:, f, :], xc_[:, f, :], pp[:, f:f + 1], rstd[:], MUL, MUL, [xc_, pp, rstd])

    def phaseM(l, c):
        t0 = c * TC
        k.dma(POOL, ropeC[:], ropeC_d[:, t0:t0 + TC], writes=[ropeC]); k.dma(POOL, ropeS[:], ropeS_d[:, t0:t0 + TC], writes=[ropeS])
        def sec_sb():
            Z = [B[2], B[3]]; PO = B[4]; PTb = B[2]; pv = PTb.ap.bitcast(BF16)
            for i in range(2):
                b = proj_fm(l, i, B[2 + i]); act(QT, QT[:, i, :], b[:], AF.Copy, [b], scale=0.125)
            for i in range(2):
                b = proj_fm(l, 2 + i, B[2 + i]); cp(DVE, KT, KT[:, i, t0:t0 + TC], b[:], [b])
            yield
            proj_tm(l, 4, Vc, lambda blk: Vc[:, c * 4 + blk, :], [B[2], B[3]])
            yield
            for qi in range(4):
                qb = c * 4 + qi
                mm(PO, PO[:, 0:256], zrow[0:1, 0:128], zrow[0:1, 0:256], True, False, [zrow])
                for a in range(qb, -1, -1):
                    diag = (a == qb)
                    for par in range(2):
                        z = Z[par]
                        mm(z, z[:, 0:256], zrow[0:1, 0:128], zrow[0:1, 0:256], True, False, [zrow])
                        for ti in range(2):
                            mm(z, z[:, ti * 128:(ti + 1) * 128], KT[64 * par:64 * par + 64, ti, a * 128:(a + 1) * 128],
                               QT[64 * par:64 * par + 64, ti, qi * 128:(qi + 1) * 128], False, False, [KT, QT])
                    yield
                    for par in range(2):
                        z = Z[par]; e_ = sbE[par]; sp_ = sbSP[par]
                        act(e_, e_[:], z[:, 0:256], AF.Exp, [z])
                        act(sp_, sp_[:], e_[:], AF.Ln, [e_, misc], bias=one_ap)
                        if diag: tt(POOL, sp_, sp_[:], sp_[:], mask2[:], MUL, [sp_, mask2])
                    yield
                    for par in range(2):
                        z = Z[par]; sp_ = sbSP[par]; c_ = sbC[par]
                        mm(z, z[:, 0:256], ntri, sp_[:], False, diag, [cmats, sp_])
                        if not diag: mm(z, z[:, 0:256], nones, c_[:], False, True, [cmats, c_])
                    yield
                    for par in range(2):
                        z = Z[par]; sp_ = sbSP[par]; w_ = sbW[par]; c_ = sbC[par]
                        act(w_, w_[:], z[:, 0:256], AF.Exp, [z])
                        if diag: tt(POOL, w_, w_[:], w_[:], mask2[:], MUL, [w_, mask2])
                        if a > 0:
                            if diag: cp(POOL, c_, c_[:], sp_[:], [sp_])
                            else: tt(POOL, c_, c_[:], c_[:], sp_[:], ADD, [c_, sp_])
                    yield
                    for par in range(2):
                        w_ = sbW[par]
                        for ti in range(2):
                            h = 2 * ti + par
                            mm(PO, PO[:, h * 64:(h + 1) * 64], w_[:, ti * 128:(ti + 1) * 128], Vc[:, a, h * 64:(h + 1) * 64],
                               False, (a == 0 and par == 1 and ti == 1), [w_, Vc])
                    yield
                cp(ACT, osb, osb[:], PO[:, 0:256], [PO])
                for ti in range(2):
                    transpose(PTb, pv[:, ti * 128:(ti + 1) * 128], osb[:, ti * 128:(ti + 1) * 128], identb, [osb, cmats])
                cp(DVE, yTt[0], yTb[:, 0:2, qi * 128:(qi + 1) * 128], pv[:, 0:256].rearrange("p (a b) -> p a b", a=2), [PTb])
                yTt[1].w = yTt[0].w; yTt[1].r = {}


        def sec_s5():
            Yb = B[1]
            for i in range(2):
                b = proj_fm(l, 6 + i, B[6 + i]); cp(ACT, uT, uT[:, i, :], b[:], [b])
            yield
            x4 = lambda b: b.ap.rearrange("p (g a n) -> p g a n", g=4, a=2)
            g4 = lambda t: t.ap.rearrange("p g (a n) -> p g a n", a=2)
            for sc in range(4):
                for q in range(4):
                    kh = q // 2; hq = q % 2; st_ = (sc * 4 + q) % 2
                    mm(B[6], B[6][:], uT[:, kh, sc * 128:(sc + 1) * 128], Bx[kh][:, hq * 512:(hq + 1) * 512], True, True, [uT, Bx[kh]])
                    mm(B[7], B[7][:], uT[:, kh, sc * 128:(sc + 1) * 128], Bsw[kh][:, hq * 512:(hq + 1) * 512], True, True, [uT, Bsw[kh]])
                    yield
                    prb = Pr[:, 4 * q:4 * q + 4, :].unsqueeze(2).to_broadcast([128, 4, 2, 64])
                    pib = Pi[:, 4 * q:4 * q + 4, :].unsqueeze(2).to_broadcast([128, 4, 2, 64])
                    tt(DVE, G1[st_], g4(G1[st_]), x4(B[6]), prb, MUL, [B[6], Pr])
                    tt(DVE, G2[st_], g4(G2[st_]), x4(B[7]), pib, MUL, [B[7], Pi])
                    yield
                    for gl in range(4):
                        mm(B[0], B[0][:, gl * 128:(gl + 1) * 128], G1[st_][:, gl, :], tri, True, False, [G1[st_], cmats])
                        mm(B[0], B[0][:, gl * 128:(gl + 1) * 128], G2[st_][:, gl, :], tri, False, True, [G2[st_], cmats])
                    yield
                    for gl in range(4):
                        g = 4 * q + gl
                        stt(A1[st_], A1[st_][:, gl, :], B[0][:, gl * 128:(gl + 1) * 128], carry[q][:, gl:gl + 1], T1[:, g, :], ADD, MUL, [B[0], carry[q], T1])
                        stt(A2[st_], A2[st_][:, gl, :], B[0][:, gl * 128:(gl + 1) * 128], carry[q][:, gl:gl + 1], T2[:, g, :], ADD, MUL, [B[0], carry[q], T2])
                    slast = B[0][:].rearrange("p (g t) -> p g t", g=4)[:, :, 127:128].rearrange("p g o -> p (g o)")
                    tt(DVE, sfull, sfull[:], slast, carry[q][:], ADD, [B[0], carry[q]])
                    tt(DVE, U1, U1[:], sfull[:], W1[:, 4 * q:4 * q + 4], MUL, [sfull, W1])
                    tt(DVE, U2, U2[:], sfull[:], W2[:, 4 * q:4 * q + 4], MUL, [sfull, W2])
                    yield
                    yb = Yb
                    for gl in range(4):
                        g = 4 * q + gl
                        mm(yb, yb[:, kh * 128:(kh + 1) * 128], Cm[:, 2 * g, :], A1[st_][:, gl, :], (hq == 0 and gl == 0), False, [Cm, A1[st_]])
                        mm(yb, yb[:, kh * 128:(kh + 1) * 128], Cm[:, 2 * g + 1, :], A2[st_][:, gl, :], False, (hq == 1 and gl == 3), [Cm, A2[st_]])
                    if hq == 1:
                        cp(ACT, LT[kh], LT[kh][:, sc * 128:(sc + 1) * 128], yb[:, kh * 128:(kh + 1) * 128], [yb])
                    cb = B[6]
                    mm(cb, cb[:, 0:4], identb, U1[:], True, False, [cmats, U1]); mm(cb, cb[:, 0:4], Jm, U2[:], False, True, [cmats, U2])
                    yield
                    cp(ACT, carry[q], carry[q][:], cb[:, 0:4], [cb])
                    yield
            for kh in range(2):
                yv = LT[kh]; x2 = LT[2]
                stt(yv, yv[:], uT[:, kh, :], pp[:, 16 + kh:17 + kh], yv[:], MUL, ADD, [uT, pp, yv])
                act(x2, x2[:], yv[:], AF.Square, [yv])
                ts(DVE, x2, x2[:], x2[:], 0.044715, 1.0, MUL, ADD, [x2])
                tt(DVE, x2, x2[:], x2[:], yv[:], MUL, [x2, yv])
                act(x2, x2[:], x2[:], AF.Sigmoid, [x2], scale=1.5957691216057308)
                tt(DVE, yv, yv[:], yv[:], x2[:], MUL, [yv, x2])
                cp(POOL, ygb, ygb[:, kh, :], yv[:], [yv])
            for e in range(2):
                b = B[6 + e]
                for kh in range(2):
                    mm(b, b[:], wglu[:, kh, e * 128:(e + 1) * 128], ygb[:, kh, :], kh == 0, kh == 1, [wglu, ygb])
                act(LT[2 + e], LT[2 + e][:], b[:], AF.Sigmoid, [b, pp], bias=pp[:, 18 + e:19 + e])
                tt(DVE, yTt[2 + e], yTb[:, 2 + e, :], LT[e][:], LT[2 + e][:], MUL, [LT[e], LT[2 + e]])


        def sec_ret():
            for ti in range(2):
                tt(DVE, qxi, qxi[:, ti, :].rearrange("p (n i) -> p n i", n=4), qrot[:, ti, :].rearrange("p (n i) -> p n i", n=4),
                   xitab[:, ti, :].unsqueeze(1).to_broadcast([128, 4, 128]), MUL, [qrot, xitab])
            pv2 = B[6].ap.bitcast(BF16); pv = B[5].ap.bitcast(BF16)
            for n in range(4):
                SX = [B[2], B[3]]
                for par in range(2):
                    for ti in range(2):
                        mm(SX[par], SX[par][:, ti * 128:(ti + 1) * 128], krot[64 * par:64 * par + 64, ti, n * 128:(n + 1) * 128],
                           qrot[64 * par:64 * par + 64, ti, n * 128:(n + 1) * 128], True, True, [krot, qrot])
                    tt(DVE, PTt, PTt[:, par, :], SX[par][:, 0:256], dtab[:, par, :], MUL, [SX[par], dtab])
                po = B[4]
                mm(po, po[:, 0:256], zrow[0:1, 0:128], zrow[0:1, 0:256], True, False, [zrow])
                for par in range(2):
                    for ti in range(2):
                        h = 2 * ti + par
                        mm(po, po[:, h * 64:(h + 1) * 64], PTt[:, par, ti * 128:(ti + 1) * 128], vt[:, n, h * 64:(h + 1) * 64], False, False, [PTt, vt])
                for ti in range(2):
                    mm(po, po[:, ti * 128:(ti + 1) * 128], qxi[:, ti, n * 128:(n + 1) * 128], Rb[:, ti, :], False, ti == 1, [qxi, Rb])
                if 'ret1' in parts: continue
                cp(ACT, osbf, osbf[:], po[:, 0:256], [po])
                o3 = osbf.ap.rearrange("p (h e) -> p h e", h=4); q3 = osq.ap.rearrange("p (h e) -> p h e", h=4)
                k.op(DVE, lambda e: e.tensor_reduce(out=st4[0][:], in_=o3, axis=AX.X, op=ADD), reads=[osbf], writes=[st4[0]])
                act(osq, osq[:], osbf[:], AF.Square, [osbf])
                k.op(DVE, lambda e: e.tensor_reduce(out=st4[1][:], in_=q3, axis=AX.X, op=ADD), reads=[osq], writes=[st4[1]])
                ts(DVE, st4[2], st4[2][:], st4[0][:], 1.0 / 64, None, MUL, None, [st4[0]])
                tt(DVE, st4[3], st4[3][:], st4[2][:], st4[2][:], MUL, [st4[2]])
                stt(st4[3], st4[3][:], st4[1][:], 1.0 / 64, st4[3][:], MUL, SUB, [st4[1], st4[3]])
                act(st4[4], st4[4][:], st4[3][:], AF.Sqrt, [st4[3], misc], bias=eps_ap)
                recip(st4[5], st4[5][:], st4[4][:], [st4[4]])
                for h in range(4):
                    ts(DVE, onb, onb[:, h * 64:(h + 1) * 64], osbf[:, h * 64:(h + 1) * 64], st4[2][:, h:h + 1], st4[5][:, h:h + 1], SUB, MUL, [osbf, st4[2], st4[5]])
                if 'ret2' in parts: continue
                for ti in range(2):
                    transpose(B[5], pv[:, ti * 128:(ti + 1) * 128], onb[:, ti * 128:(ti + 1) * 128], identb, [onb, cmats])
                cp(ACT, yTt[4], yTb[:, 4:6, n * 128:(n + 1) * 128], pv[:, 0:256].rearrange("p (a b) -> p a b", a=2), [B[5]])
                yTt[5].w = yTt[4].w; yTt[5].r = {}
                if 'ret3' in parts: continue
                for ti in range(2):
                    transpose(B[6], pv2[:, ti * 128:(ti + 1) * 128], krot[:, ti, n * 128:(n + 1) * 128], identb, [krot, cmats])
                tt(DVE, kz, kz[:], pv2[:, 0:256], ztab[:], MUL, [B[6], ztab])
                kvb = B[7]
                for ti in range(2):
                    mm(kvb, kvb[:, ti * 128:(ti + 1) * 128], kz[:, ti * 128:(ti + 1) * 128], vt[:, n, ti * 128:(ti + 1) * 128], True, True, [kz, vt])
                tt(DVE, osq, osq.ap.rearrange("p (a b) -> p a b", a=2), kvb[:, 0:256].rearrange("p (a b) -> p a b", a=2),
                   cmats[:, 7:8, :].to_broadcast([128, 2, 128]), MUL, [kvb, cmats])
                for ti in range(2):
                    stt(Rst, Rst[:, ti, :], Rst[:, ti, :], gdec[:, ti:ti + 1], osq[:, ti * 128:(ti + 1) * 128], MUL, ADD, [Rst, gdec, osq])
                cp(POOL, Rb, Rb[:], Rst[:], [Rst])
                yield

        def sec_lru():
            for i in range(2):
                xc = LT[0]; r = LT[1]; ig = LT[2]; a_ = LT[3]; h_ = LT[4]
                ts(DVE, xc, xc[:], uL[:, i, 0:TC], pp[:, 20 + 4 * i:21 + 4 * i], pp[:, 28 + i:29 + i], MUL, ADD, [uL, pp])
                for kk in range(1, 4):
                    stt(xc, xc[:], uL[:, i, kk:kk + TC], pp[:, 20 + 4 * i + kk:21 + 4 * i + kk], xc[:], MUL, ADD, [uL, pp, xc])
                cp(POOL, xcb, xcb[:], xc[:], [xc])
                mm(B[0], B[0][:], WA[:, i, :], xcb[:], True, True, [WA, xcb]); mm(B[1], B[1][:], WX[:, i, :], xcb[:], True, True, [WX, xcb])
                act(r, r[:], B[0][:], AF.Sigmoid, [B[0], pp], bias=pp[:, 30 + i:31 + i])
                act(ig, ig[:], B[1][:], AF.Sigmoid, [B[1], pp], bias=pp[:, 32 + i:33 + i])
                act(a_, a_[:], r[:], AF.Exp, [r, cl], scale=cl[:, i:i + 1])
                act(r, r[:], r[:], AF.Exp, [r, cl2], scale=cl2[:, i:i + 1])
                act(r, r[:], r[:], AF.Sqrt, [r, misc], bias=one_ap, scale=-1.0)
                tt(POOL, ig, ig[:], ig[:], xc[:], MUL, [ig, xc]); tt(DVE, r, r[:], r[:], ig[:], MUL, [r, ig])
                k.op(DVE, lambda e: e.tensor_tensor_scan(out=h_[:], data0=a_[:], data1=r[:], initial=hst[:, i:i + 1], op0=MUL, op1=ADD),
                     reads=[a_, r, hst], writes=[h_])
                cp(ACT, hst, hst[:, i:i + 1], h_[:, TC - 1:TC], [h_])
                cp(POOL, yTt[6 + i], yTb[:, 6 + i, :], h_[:], [h_])
                cp(POOL, uL, uL[:, i, 0:3], uL[:, i, TC:TC + 3], [uL])
                yield

        def sec_p():
            bk = B[5]
            for (ct, ctsw, dst) in ((8, 56, qrot), (10, 58, krot)):
                for i in range(2):
                    b = proj_fm(l, ct + i, bk); tt(DVE, rt1, rt1[:], b[:], ropeC[:], MUL, [b, ropeC])
                    b2 = proj_fm(l, ctsw + i, bk); tt(DVE, rt2, rt2[:], b2[:], ropeS[:], MUL, [b2, ropeS])
                    tt(POOL, dst, dst[:, i, :], rt1[:], rt2[:], ADD, [rt1, rt2])
                    yield
            proj_tm(l, 12, vt, lambda blk: vt[:, blk, :], [bk])
            yield
            for i in range(2):
                b = proj_fm(l, 14 + i, bk); cp(DVE, uL, uL[:, i, 3:3 + TC], b[:], [b])
                yield

        def run_wave(gens):
            st = [[g, n, 0, True] for g, n in gens]
            while any(x[3] for x in st):
                cand = [x for x in st if x[3]]
                x = min(cand, key=lambda y: y[2] / max(y[1], 1))
                try:
                    next(x[0]); x[2] += 1
                except StopIteration:
                    x[3] = False
        npairs = sum(c * 4 + qi + 1 for qi in range(4))
        wave_a = []
        if 'sb' in parts: wave_a.append((sec_sb(), 5 * npairs + 2))
        if 's5' in parts: wave_a.append((sec_s5(), 16 * 6 + 1))
        wave_a.append((sec_p(), 8))
        run_wave(wave_a)
        wave_b = []
        if 'ret' in parts: wave_b.append((sec_ret(), 4))
        if 'lru' in parts: wave_b.append((sec_lru(), 2))
        run_wave(wave_b)

    def phaseO(l, c):
        xc_ = xTc[c]
        for ct in range(8):
            b = proj_fm(l, 16 + ct)
            act(prodS[ct % 4], prodS[ct % 4][:], b[:], AF.Silu, [b])
            tt(DVE, yTt[ct], yTb[:, ct, :], yTb[:, ct, :], prodS[ct % 4][:], MUL, [yTt[ct], prodS[ct % 4]])
        if 'O1' in parts: return
        for f in range(8):
            wb = WB[f % 2]
            k.dma(POOL, wb[:], wbr_d[l].rearrange("n (k p) d -> p (n k) d", p=128)[:, :, f * 128:(f + 1) * 128], writes=[wb])
            for n in range(4):
                bm = proj_fm(l, 24 + n * 8 + f)
                by = B[2 + n % 2]
                for kk in range(2):
                    mm(by, by[:], wb[:, n * 2 + kk, :], yTb[:, 2 * n + kk, :], kk == 0, kk == 1, [wb, yTt[2 * n + kk]])
                s_ = sg[n % 2]
                act(s_, s_[:], bm[:], AF.Sigmoid, [bm])
                tt(DVE, prodS[n], prodS[n][:], s_[:], by[:], MUL, [s_, by])
            bs = B[4]
            for n in range(4):
                mm(bs, bs[:], identb, prodS[n][:], n == 0, n == 3, [cmats, prodS[n]])
            cp(ACT, merged, merged[:, f, :], bs[:], [bs])
        if 'O2' in parts: return
        for half in range(2):
            cs_ = slice(half * 256, (half + 1) * 256)
            ssb = B[5]
            pend = None
            for e in range(8):
                w = load_w(wout_d[l].rearrange("(k p) c -> p k c", p=128)[:, :, e * 128:(e + 1) * 128])
                b = B[pbn[0] % 2]; pbn[0] += 1
                for d in range(8):
                    mm(b, b[:, 0:256], w[:, d, :], merged[:, d, cs_], d == 0, d == 7, [w, merged])
                if pend is not None:
                    pe_ = pend
                    mm(ssb, ssb[:, 0:256], onesb, sqb[pe_ % 2][:, 0:256], pe_ == 0, pe_ == 7, [cmats, sqb[pe_ % 2]])
                cp(DVE, outT, outT[:, e, :], b[:, 0:256], [b])
                act(sqb[e % 2], sqb[e % 2][:, 0:256], outT[:, e, :], AF.Square, [outT])
                pend = e
            mm(ssb, ssb[:, 0:256], onesb, sqb[pend % 2][:, 0:256], pend == 0, pend == 7, [cmats, sqb[pend % 2]])
            if 'x1' in parts or 'x2' in parts: continue
            act(sg[0], sg[0][:, 0:256], ssb[:, 0:256], AF.Sqrt, [ssb, misc], bias=eps_ap, scale=1.0 / D)
            recip(rstd, rstd[:, 0:256], sg[0][:, 0:256], [sg[0]])
            for e in range(8):
                tb_ = sg[e % 2]
                stt(tb_, tb_[:, 0:256], outT[:, e, :], pp[:, 8 + e:9 + e], rstd[:, 0:256], MUL, MUL, [outT, pp, rstd])
                tt(POOL, xc_, xc_[:, e, cs_], xc_[:, e, cs_], tb_[:, 0:256], ADD, [xc_, tb_])

    for sq in range(NSEQ):
        k.barrier(); load_seq(sq)
        for l in range(L):
            k.barrier()
            if 'setup' in parts: layer_setup(l)
            k.barrier()
            if 'prenorm' in parts: prenorm(l, 0)
            k.barrier()
            for c in range(NCH):
                if 'proj' in parts: phaseM(l, c)
                if "yT" in dbg_d and sq == 0 and l == 0:
                    k.dma(POOL, dbg_d["yT"][:, :, c * TC:(c + 1) * TC], yTb[:], reads=yTt)
                k.barrier()
                if 'O' in parts: phaseO(l, c)
                if c + 1 < NCH and 'prenorm' in parts: prenorm(l, c + 1)
                k.barrier()
        store_seq(sq)
    k.barrier()
    import os
    if os.environ.get('KSTAT'): print('ENGINE COUNTS', {E.name: (E.cnt, E.real) for E in k.engs}, 'dma', {E.name: sum(E.dvals)//16 for E in k.engs})
    return nc, k


def _prep_shared(inp, S, L):
    f32 = np.float32
    g = lambda n: np.asarray(inp[n], dtype=f32)[:L]
    w_in = g('w_in')
    perm = []
    for base in (1024, 1280):
        for h in range(4):
            b0 = base + 64 * h
            perm += list(range(b0 + 32, b0 + 64)) + list(range(b0, b0 + 32))
    win = np.ascontiguousarray(np.concatenate([w_in, w_in[:, :, perm]], axis=2))
    pre_g, post_g = g('pre_norm_g'), g('post_norm_g')
    pp = np.zeros((L, 128, 40), f32)
    for l in range(L):
        pp[l, :, 0:8] = pre_g[l].reshape(8, 128).T
        pp[l, :, 8:16] = post_g[l].reshape(8, 128).T
        pp[l, :, 16:18] = g('ssm_d')[l].reshape(2, 128).T
        pp[l, :, 18:20] = g('ssm_b_glu')[l].reshape(2, 128).T
        cw = g('lru_conv_w')[l]
        for i in range(2):
            pp[l, :, 20 + 4 * i:24 + 4 * i] = cw[:, 128 * i:128 * (i + 1)].T
        pp[l, :, 28:30] = g('lru_conv_b')[l].reshape(2, 128).T
        pp[l, :, 30:32] = g('lru_b_a')[l].reshape(2, 128).T
        pp[l, :, 32:34] = g('lru_b_x')[l].reshape(2, 128).T
        pp[l, :, 34:36] = g('lru_lambda')[l].reshape(2, 128).T
    lruw = np.zeros((L, 2, 128, 2, 128), f32)
    for l in range(L):
        for ax, nm in enumerate(('lru_w_a', 'lru_w_x')):
            w = g(nm)[l]
            for i in range(2):
                for bl in range(2):
                    lruw[l, ax, 64 * bl:64 * bl + 64, i, 64 * bl:64 * bl + 64] = w[2 * i + bl]
    a_re, a_im, ldt = g('ssm_a_re'), g('ssm_a_im'), g('ssm_log_dt')
    s5row = np.stack([a_re.reshape(L, 1024), a_im.reshape(L, 1024), np.repeat(ldt, 64, axis=1)], axis=1)
    s5col = np.zeros((L, 128, 3, 16), f32)
    for l in range(L):
        s5col[l, :, 0, :] = np.concatenate([a_re[l].T, a_re[l].T], axis=0)
        s5col[l, :, 1, :] = np.concatenate([a_im[l].T, a_im[l].T], axis=0)
        s5col[l, :, 2, :] = np.broadcast_to(ldt[l][None, :], (128, 16))
    bblk = np.zeros((L, 2, 128, 2, 512), f32)
    for l in range(L):
        for ri, nm in enumerate(('ssm_b_re', 'ssm_b_im')):
            bb = g(nm)[l]
            for gg in range(16):
                kh, gl = gg // 8, gg % 8
                bblk[l, ri, gl * 16:(gl + 1) * 16, kh, gl * 64:(gl + 1) * 64] = bb[gg].T
    cab = np.zeros((L, 128, 32, 128), f32)
    c_re, c_im = g('ssm_c_re'), g('ssm_c_im')
    for l in range(L):
        for gg in range(16):
            co = 16 * (gg % 8)
            cab[l, 0:64, 2 * gg, co:co + 16] = c_re[l, gg].T
            cab[l, 64:128, 2 * gg, co:co + 16] = c_im[l, gg].T
            cab[l, 0:64, 2 * gg + 1, co:co + 16] = c_im[l, gg].T
            cab[l, 64:128, 2 * gg + 1, co:co + 16] = c_re[l, gg].T
    p = np.arange(128)
    invf = (np.float32(10000.0) ** (-(np.arange(32, dtype=f32) / np.float32(32)))).astype(f32)
    ang = (np.arange(S, dtype=f32)[None, :] * invf[(p % 64) % 32][:, None]).astype(f32)
    ropeC = np.cos(ang).astype(f32)
    ropeS = (np.sin(ang) * np.where((p % 64) < 32, -1.0, 1.0)[:, None]).astype(f32)
    gam = 1.0 - 2.0 ** (-5.0 - np.arange(4))
    ii = np.arange(128)
    dtab = np.zeros((128, 2, 256), f32)
    for par in range(2):
        for ti in range(2):
            h = 2 * ti + par
            rel = ii[None, :] - ii[:, None]
            dtab[:, par, ti * 128:(ti + 1) * 128] = np.where(rel >= 0, gam[h] ** np.maximum(rel, 0), 0.0) / 8.0
    ztab = np.zeros((128, 256), f32)
    for h in range(4):
        ztab[:, h * 64:(h + 1) * 64] = (gam[h] ** (127 - ii) / 8.0)[:, None]
    xitab = np.zeros((128, 2, 128), f32); gdec = np.zeros((128, 2), f32)
    for ti in range(2):
        for hl in range(2):
            h = 2 * ti + hl
            xitab[64 * hl:64 * hl + 64, ti, :] = (gam[h] ** (ii + 1.0))[None, :]
            gdec[64 * hl:64 * hl + 64, ti] = gam[h] ** 128
    cm = np.zeros((128, 8, 128), f32)
    cm[:, 0, :] = np.eye(128)
    cm[:, 1, :] = -1.0 * (ii[:, None] >= ii[None, :])
    cm[:, 2, :] = -1.0
    cm[:, 3, :] = (ii[:, None] <= ii[None, :])
    cm[:, 4, :] = (ii[:, None] < ii[None, :])
    for m in range(64):
        cm[m + 64, 5, m] = -1.0
        cm[m, 5, m + 64] = 1.0
    cm[:, 6, :] = 1.0
    cm[0:64, 7, 0:64] = 1.0; cm[64:128, 7, 64:128] = 1.0
    misc = np.zeros((128, 132), f32)
    misc[:, 0] = p; misc[:, 1] = np.where(p < 64, 1.0, -1.0); misc[:, 2] = EPS; misc[:, 3] = 1.0
    misc[:, 4:132] = np.arange(128)[None, :]
    return dict(win=win, wbr=g('w_branch'), wout=g('w_out'), wglu=g('ssm_w_glu'), pp=pp, lruw=lruw, s5row=np.ascontiguousarray(s5row),
                s5col=s5col, bblk=bblk, cab=cab, ropeC=ropeC, ropeS=ropeS, dtab=dtab, ztab=ztab, xitab=xitab, gdec=gdec,
                cmats=cm, misc=misc)


def run(inp, S, NSEQ, DEPTH, ncores, dbg=None, parts=None):
    shared = _prep_shared(inp, S, DEPTH)
    x = np.asarray(inp['x'], dtype=np.float32)
    nc = build(S, NSEQ, DEPTH, dbg, parts)
    in_maps = []
    for i in range(ncores):
        m = dict(shared); m['x'] = np.ascontiguousarray(x[i * NSEQ:(i + 1) * NSEQ]); in_maps.append(m)
    res = run_bass_kernel_spmd(nc, in_maps, core_ids=list(range(ncores)))
    return res


def kernel(**inputs):
    x = np.asarray(inputs['x'])
    Bn, S, _ = x.shape
    ncores = 8
    NSEQ = Bn // ncores
    res = run(inputs, S, NSEQ, 2, ncores)
    return np.concatenate([r["y"] for r in res.results], axis=0).astype(np.float32)
```

```python
import math
import numpy as np
import concourse.bass as bass
import concourse.mybir as mybir
from concourse.bass_utils import run_bass_kernel_spmd

F32 = mybir.dt.float32; BF16 = mybir.dt.bfloat16; I32 = mybir.dt.int32
AF = mybir.ActivationFunctionType; ALU = mybir.AluOpType; AX = mybir.AxisListType
D = 1024; TC = 512; NBK = 4; EPS = 1e-6
NCOLT = 60
PI = math.pi


class T:
    __slots__ = ('ap', 'w', 'r')
    def __init__(s, ap): s.ap = ap; s.w = None; s.r = {}
    def __getitem__(s, idx): return s.ap[idx]


class Eng:
    def __init__(s, nc, name, eng, is_pe=False):
        s.name = name; s.eng = eng; s.sem = nc.alloc_semaphore("sem_" + name); s.cnt = 0; s.real = 0; s.known = {}; s.is_pe = is_pe
        s.dsems = []; s.dvals = []; s.dnext = 0


class K:
    def __init__(s, nc, ndma=(16, 16, 4), needed=None):
        s.nc = nc; s.needed = needed; s.waited = set(); s.rmap = {}
        s.PE = Eng(nc, "pe", nc.tensor, True); s.ACT = Eng(nc, "act", nc.scalar); s.DVE = Eng(nc, "dve", nc.vector)
        s.POOL = Eng(nc, "pool", nc.gpsimd); s.SP = Eng(nc, "sp", nc.sync)
        s.engs = [s.PE, s.ACT, s.DVE, s.POOL, s.SP]
        for E, n in ((s.SP, ndma[0]), (s.POOL, ndma[1]), (s.ACT, ndma[2])):
            E.dsems = [nc.alloc_semaphore(f"d_{E.name}_{i}") for i in range(n)]; E.dvals = [0] * n
        s.nt = 0
    def sb(s, shape, dt, name=None):
        s.nt += 1
        return T(s.nc.alloc_sbuf_tensor("s_" + (name or f"t{s.nt}"), list(shape), dt).ap())
    def ps(s, shape, dt=F32, name=None):
        s.nt += 1
        return T(s.nc.alloc_psum_tensor("ps_" + (name or f"p{s.nt}"), list(shape), dt).ap())
    def _waits(s, E, reads, writes):
        deps = {}
        def add(ev, war=False):
            if ev is None: return
            key, sem, val, who = ev
            if who is E and (E.is_pe or war): return
            if deps.get(key, (None, 0))[1] < val: deps[key] = (sem, val)
        for t in reads: add(t.w)
        for t in writes:
            add(t.w)
            for ev in t.r.values(): add(ev, True)
        for key, (sem, val) in deps.items():
            if E.known.get(key, 0) >= val: continue
            s._wait(E, key, sem, val)
    def _wait(s, E, key, sem, val):
        rv = val
        if not key.startswith("d_"):
            s.waited.add((key, val))
            if s.needed is not None: rv = s.rmap[(key, val)]
        E.eng.wait_ge(sem, rv); E.known[key] = val
    def _post(s, ev, reads, writes):
        for t in writes: t.w = ev; t.r = {}
        for t in reads: t.r[ev[0]] = ev
    def op(s, E, fn, reads=(), writes=()):
        s._waits(E, reads, writes)
        ins = fn(E.eng); E.cnt += 1
        if s.needed is None or (E.name, E.cnt) in s.needed:
            E.real += 1; ins.then_inc(E.sem, 1); s.rmap[(E.name, E.cnt)] = E.real
        s._post((E.name, E.sem, E.cnt, E), reads, writes)
        return ins
    def dma(s, Q, out, in_, reads=(), writes=(), **kw):
        i = Q.dnext; Q.dnext = (i + 1) % len(Q.dsems); sem = Q.dsems[i]; key = f"d_{Q.name}_{i}"
        if Q.dvals[i] > 0 and Q.known.get(key, 0) < Q.dvals[i]:
            Q.eng.wait_ge(sem, Q.dvals[i]); Q.known[key] = Q.dvals[i]
        s._waits(Q, reads, writes)
        ins = Q.eng.dma_start(out=out, in_=in_, **kw); Q.dvals[i] += 16; ins.then_inc(sem, 16)
        ev = (key, sem, Q.dvals[i], None)
        s._post(ev, reads, writes)
        return ev
    def barrier(s):
        for E in s.engs:
            for Fg in s.engs:
                if Fg is E or Fg.cnt == 0: continue
                if E.known.get(Fg.name, 0) < Fg.cnt:
                    s._wait(E, Fg.name, Fg.sem, Fg.cnt)
            for Q in s.engs:
                for i, sem in enumerate(Q.dsems):
                    key = f"d_{Q.name}_{i}"
                    if Q.dvals[i] > 0 and E.known.get(key, 0) < Q.dvals[i]:
                        E.eng.wait_ge(sem, Q.dvals[i]); E.known[key] = Q.dvals[i]


def build(S, NSEQ, DEPTH, dbg=None, parts=None):
    _, k1 = _build(S, NSEQ, DEPTH, dbg, parts, None)
    nc, k2 = _build(S, NSEQ, DEPTH, dbg, parts, set(k1.waited))
    return nc


def _build(S, NSEQ, DEPTH, dbg, parts, needed):
    nc = bass.Bass("TRN2", target_bir_lowering=False)
    k = K(nc, needed=needed)
    if parts is None: parts = {"setup", "prenorm", "proj", "sb", "s5", "ret", "lru", "O"}
    PE, ACT, DVE, POOL, SP = k.PE, k.ACT, k.DVE, k.POOL, k.SP
    NCH = S // TC
    NBLK = S // 128
    L = DEPTH
    def din(name, shape): return nc.dram_tensor(name, list(shape), F32, kind="ExternalInput").ap()
    x_d = din("x", [NSEQ, S, D]); win_d = din("win", [L, D, 7680]); wbr_d = din("wbr", [L, 4, 256, D])
    wout_d = din("wout", [L, D, D]); wglu_d = din("wglu", [L, 256, 256]); pp_d = din("pp", [L, 128, 40])
    lruw_d = din("lruw", [L, 2, 128, 2, 128]); s5row_d = din("s5row", [L, 3, 1024]); s5col_d = din("s5col", [L, 128, 3, 16])
    bblk_d = din("bblk", [L, 2, 128, 2, 512]); cab_d = din("cab", [L, 128, 32, 128])
    ropeC_d = din("ropeC", [128, S]); ropeS_d = din("ropeS", [128, S])
    dtab_d = din("dtab", [128, 2, 256]); ztab_d = din("ztab", [128, 256]); xitab_d = din("xitab", [128, 2, 128])
    gdec_d = din("gdec", [128, 2]); cm_d = din("cmats", [128, 8, 128]); misc_d = din("misc", [128, 132])
    y_d = nc.dram_tensor("y", [NSEQ, S, D], F32, kind="ExternalOutput").ap()
    dbg_d = {}
    if dbg:
        for nm, shp in dbg.items():
            dbg_d[nm] = nc.dram_tensor(nm, list(shp), F32, kind="ExternalOutput").ap()

    xTb = nc.alloc_sbuf_tensor("s_xT", [128, 8, S], F32).ap()
    xTc = [T(xTb[:, :, c * TC:(c + 1) * TC]) for c in range(S // TC)]
    hT = k.sb([128, 8, TC], BF16, "hT")
    WT = [k.sb([128, 8, 128], BF16, f"WT{i}") for i in range(5)]
    wtn = [0]
    pp = k.sb([128, 40], F32, "pp")
    cmats = k.sb([128, 8, 128], BF16, "cmats")
    identb = cmats[:, 0, :]; ntri = cmats[:, 1, :]; nones = cmats[:, 2, :]; tri = cmats[:, 3, :]
    Jm = cmats[:, 5, :]; onesb = cmats[:, 6, :]
    identf = k.sb([128, 128], F32, "identf")
    misc = k.sb([128, 132], F32, "misc")
    mask2 = k.sb([128, 256], BF16, "mask2")
    zrow = k.sb([1, 256], BF16, "zrow")
    ropeC = k.sb([128, TC], BF16, "ropeC"); ropeS = k.sb([128, TC], BF16, "ropeS")
    dtab = k.sb([128, 2, 256], BF16, "dtab"); ztab = k.sb([128, 256], BF16, "ztab"); xitab = k.sb([128, 2, 128], BF16, "xitab")
    gdec = k.sb([128, 2], F32, "gdec")
    KT = k.sb([128, 2, S], BF16, "KT"); Vc = k.sb([128, NBLK, 256], BF16, "Vc"); QT = k.sb([128, 2, TC], BF16, "QT")
    sbE = [k.sb([128, 256], F32, f"sbE{i}") for i in range(2)]
    sbSP = [k.sb([128, 256], BF16, f"sbSP{i}") for i in range(2)]
    sbW = [k.sb([128, 256], BF16, f"sbW{i}") for i in range(2)]
    sbC = [k.sb([128, 256], BF16, f"sbC{i}") for i in range(2)]
    osb = k.sb([128, 256], BF16, "osb")
    uT = k.sb([128, 2, TC], BF16, "uT")
    Bx = [k.sb([128, 1024], BF16, f"Bx{i}") for i in range(2)]; Bsw = [k.sb([128, 1024], BF16, f"Bsw{i}") for i in range(2)]
    Pr = k.sb([128, 16, 64], BF16, "Pr"); Pi = k.sb([128, 16, 64], BF16, "Pi")
    T1 = k.sb([128, 16, 128], BF16, "T1"); T2 = k.sb([128, 16, 128], BF16, "T2")
    W1 = k.sb([128, 16], F32, "W1"); W2 = k.sb([128, 16], F32, "W2")
    Cm = k.sb([128, 32, 128], BF16, "Cm")
    G1 = [k.sb([128, 4, 128], BF16, f"G1_{i}") for i in range(2)]; G2 = [k.sb([128, 4, 128], BF16, f"G2_{i}") for i in range(2)]
    A1 = [k.sb([128, 4, 128], BF16, f"A1_{i}") for i in range(2)]; A2 = [k.sb([128, 4, 128], BF16, f"A2_{i}") for i in range(2)]
    carry = [k.sb([128, 4], F32, f"carry{i}") for i in range(4)]
    sfull = k.sb([128, 4], F32, "sfull"); U1 = k.sb([128, 4], BF16, "U1"); U2 = k.sb([128, 4], BF16, "U2")
    ygb = k.sb([128, 2, TC], BF16, "ygb"); wglu = k.sb([128, 2, 256], BF16, "wglu")
    uL = k.sb([128, 2, TC + 4], F32, "uL"); hst = k.sb([128, 2], F32, "hst")
    WA = k.sb([128, 2, 128], BF16, "WA"); WX = k.sb([128, 2, 128], BF16, "WX")
    cl = k.sb([128, 2], F32, "cl"); cl2 = k.sb([128, 2], F32, "cl2"); cltmp = k.sb([128, 2], F32, "cltmp")
    Rst = k.sb([128, 2, 128], F32, "Rst"); Rb = k.sb([128, 2, 128], BF16, "Rb")
    st4 = [k.sb([128, 4], F32, f"st4_{i}") for i in range(6)]
    yTb = nc.alloc_sbuf_tensor("s_yT", [128, 8, TC], BF16).ap()
    yTt = [T(yTb[:, i, :]) for i in range(8)]
    WB = [k.sb([128, 8, 128], BF16, f"WB{i}") for i in range(2)]
    colp = k.sb([128, 3, 16], F32, "colp"); colq = k.sb([128, 3, 16], F32, "colq")
    UN = nc.alloc_sbuf_tensor("UN", [128, 7168], F32).ap()
    def carve(off_f32, n_f32, dt, shape):
        v = UN[:, off_f32:off_f32 + n_f32]
        if dt is BF16: v = v.bitcast(BF16)
        if len(shape) == 3: v = v.rearrange("p (a b) -> p a b", a=shape[1])
        return T(v)
    LT = [carve(512 * i, 512, F32, [128, 512]) for i in range(5)]
    xcb = carve(2560, 256, BF16, [128, 512])
    qrot = carve(2816, 512, BF16, [128, 2, 512]); krot = carve(3328, 512, BF16, [128, 2, 512])
    qxi = carve(3840, 512, BF16, [128, 2, 512])
    vt = carve(4352, 512, BF16, [128, 4, 256]); kz = carve(4864, 128, BF16, [128, 256])
    PTt = carve(4992, 256, BF16, [128, 2, 256])
    osbf = carve(5248, 256, F32, [128, 256]); osq = carve(5504, 256, F32, [128, 256]); onb = carve(5760, 128, BF16, [128, 256])
    rt1 = carve(5888, 512, F32, [128, 512]); rt2 = carve(6400, 512, F32, [128, 512])
    merged = carve(0, 2048, BF16, [128, 8, 512])
    prodS = [carve(2048 + 256 * i, 256, BF16, [128, 512]) for i in range(4)]
    sg = [carve(3072, 512, F32, [128, 512]), carve(3584, 512, F32, [128, 512])]
    outT = carve(4096, 2048, F32, [128, 8, 256])
    stg = [carve(4096, 1024, F32, [128, 1024]), carve(5120, 1024, F32, [128, 1024])]
    sqb = [carve(6144, 256, BF16, [128, 512]), carve(6400, 256, BF16, [128, 512])]
    rstd = carve(6656, 512, F32, [128, 512])
    SRt = [carve(512 * i, 512, F32, [128, 512]) for i in range(14)]
    B = [k.ps([128, 512], F32, f"bank{i}") for i in range(8)]

    def mm(out_t, out_ap, lhsT, rhs, start, stop, reads):
        k.op(PE, lambda e: e.matmul(out_ap, lhsT=lhsT, rhs=rhs, start=start, stop=stop), reads=reads, writes=[out_t])
    def act(out_t, out_ap, in_ap, func, reads, bias=None, scale=None):
        kw = {}
        if bias is not None: kw['bias'] = bias
        if scale is not None: kw['scale'] = scale
        k.op(ACT, lambda e: e.activation(out=out_ap, in_=in_ap, func=func, **kw), reads=reads, writes=[out_t])
    def tt(E, out_t, out_ap, a, b, op, reads):
        k.op(E, lambda e: e.tensor_tensor(out=out_ap, in0=a, in1=b, op=op), reads=reads, writes=[out_t])
    def ts(E, out_t, out_ap, a, s1, s2, op0, op1, reads):
        if op1 is None:
            k.op(E, lambda e: e.tensor_scalar(out=out_ap, in0=a, scalar1=s1, scalar2=None, op0=op0), reads=reads, writes=[out_t])
        else:
            k.op(E, lambda e: e.tensor_scalar(out=out_ap, in0=a, scalar1=s1, scalar2=s2, op0=op0, op1=op1), reads=reads, writes=[out_t])
    def stt(out_t, out_ap, a, sc, b, op0, op1, reads):
        k.op(DVE, lambda e: e.scalar_tensor_tensor(out=out_ap, in0=a, scalar=sc, in1=b, op0=op0, op1=op1), reads=reads, writes=[out_t])
    def cp(E, out_t, out_ap, in_ap, reads):
        if E is ACT:
            k.op(E, lambda e: e.copy(out=out_ap, in_=in_ap), reads=reads, writes=[out_t])
        else:
            k.op(E, lambda e: e.tensor_copy(out=out_ap, in_=in_ap), reads=reads, writes=[out_t])
    def memset(E, t, ap, v):
        k.op(E, lambda e: e.memset(ap, v), writes=[t])
    def dbg_out(name, t, ap):
        if name in dbg_d:
            k.dma(SP, dbg_d[name], ap, reads=[t])

    def load_w(src_ap):
        w = WT[wtn[0] % 5]; wtn[0] += 1
        k.dma(POOL, w[:], src_ap, writes=[w])
        return w
    def win_tile(l, ct):
        return win_d[l].rearrange("(k p) c -> p k c", p=128)[:, :, ct * 128:(ct + 1) * 128]
    pbn = [0]
    def proj_fm(l, ct, bank=None):
        w = load_w(win_tile(l, ct))
        if bank is None:
            b = B[pbn[0] % 2]; pbn[0] += 1
        else:
            b = bank
        for kk in range(8):
            mm(b, b[:], w[:, kk, :], hT[:, kk, :], kk == 0, kk == 7, [w, hT])
        return b
    def proj_tm(l, ct0, dst_t, dst_fn, banks=None):
        w0 = load_w(win_tile(l, ct0)); w1 = load_w(win_tile(l, ct0 + 1))
        for half in range(2):
            if banks is None:
                b = B[pbn[0] % 2]; pbn[0] += 1
            else:
                b = banks[half % len(banks)]
            for bi in range(2):
                blk = half * 2 + bi
                for j, w in enumerate((w0, w1)):
                    o = bi * 256 + j * 128
                    for kk in range(8):
                        mm(b, b[:, o:o + 128], hT[:, kk, blk * 128:(blk + 1) * 128], w[:, kk, :], kk == 0, kk == 7, [w, hT])
            for bi in range(2):
                blk = half * 2 + bi
                cp(ACT, dst_t, dst_fn(blk), b[:, bi * 256:(bi + 1) * 256], [b])

    def sincos(ang, vw, tmps):
        ki, kr, sn, cs = tmps
        kiv = vw(ki).bitcast(I32)
        ts(DVE, kr, vw(kr), vw(ang), 1.0 / (2 * PI), None, ALU.mult, None, [ang])
        cp(DVE, ki, kiv, vw(kr), [kr])
        cp(DVE, kr, vw(kr), kiv, [ki])
        C1 = 6.28125; C2 = 2 * PI - 6.28125
        stt(sn, vw(sn), vw(kr), -C1, vw(ang), ALU.mult, ALU.add, [kr, ang])
        stt(sn, vw(sn), vw(kr), -C2, vw(sn), ALU.mult, ALU.add, [kr, sn])
        ts(DVE, sn, vw(sn), vw(sn), -PI, PI, ALU.max, ALU.min, [sn])
        ts(DVE, cs, vw(cs), vw(sn), PI / 2, -2 * PI, ALU.is_gt, ALU.mult, [sn])
        stt(cs, vw(cs), vw(sn), PI / 2, vw(cs), ALU.add, ALU.add, [sn, cs])
        ts(DVE, cs, vw(cs), vw(cs), -PI, PI, ALU.max, ALU.min, [cs])
        act(cs, vw(cs), vw(cs), AF.Sin, [cs])
        act(sn, vw(sn), vw(sn), AF.Sin, [sn])

    k.dma(POOL, cmats[:], cm_d, writes=[cmats])
    k.dma(SP, misc[:], misc_d, writes=[misc])
    k.dma(POOL, dtab[:], dtab_d, writes=[dtab]); k.dma(POOL, ztab[:], ztab_d, writes=[ztab]); k.dma(POOL, xitab[:], xitab_d, writes=[xitab])
    k.dma(SP, gdec[:], gdec_d, writes=[gdec])
    memset(POOL, zrow, zrow[:], 0.0)
    for i in range(2):
        cp(POOL, mask2, mask2[:, i * 128:(i + 1) * 128], cmats[:, 4, :], [cmats])
    cp(DVE, identf, identf[:], cmats[:, 0, :], [cmats])
    tauc = misc[:, 0:1]; sgn1 = misc[:, 1:2]; eps_ap = misc[:, 2:3]; one_ap = misc[:, 3:4]; taurow = misc[:, 4:132]


    flat = lambda t: t.ap
    v3 = lambda t: t.ap.rearrange("p (g n) -> p g n", g=8)
    v4 = lambda t: t.ap.rearrange("p (g n) -> p g n", g=4)
    sm = lambda t: t.ap[:, 0:16]
    MUL, ADD, SUB = ALU.mult, ALU.add, ALU.subtract

    def recip(out_t, out_ap, in_ap, reads):
        k.op(DVE, lambda e: e.reciprocal(out=out_ap, in_=in_ap), reads=reads, writes=[out_t])
    def transpose(out_t, out_ap, in_ap, ident_ap, reads):
        k.op(PE, lambda e: e.transpose(out=out_ap, in_=in_ap, identity=ident_ap), reads=reads, writes=[out_t])

    def layer_setup(l):
        k.dma(SP, pp[:], pp_d[l], writes=[pp])
        k.dma(POOL, WA[:], lruw_d[l, 0], writes=[WA]); k.dma(POOL, WX[:], lruw_d[l, 1], writes=[WX])
        k.dma(POOL, wglu[:], wglu_d[l].rearrange("(k p) c -> p k c", p=128), writes=[wglu])
        k.dma(POOL, Cm[:], cab_d[l], writes=[Cm])
        k.dma(SP, colp[:], s5col_d[l], writes=[colp])
        act(cltmp, cltmp[:], pp[:, 34:36], AF.Exp, [pp], scale=-1.0)
        act(cltmp, cltmp[:], cltmp[:], AF.Ln, [cltmp, misc], bias=one_ap)
        ts(DVE, cl, cl[:], cltmp[:], -8.0, None, MUL, None, [cltmp])
        ts(DVE, cl2, cl2[:], cltmp[:], -16.0, None, MUL, None, [cltmp])
        memset(POOL, uL, uL[:, :, 0:4], 0.0); memset(POOL, hst, hst[:], 0.0)
        memset(POOL, Rst, Rst[:], 0.0); memset(POOL, Rb, Rb[:], 0.0)
        for q in range(4): memset(POOL, carry[q], carry[q][:], 0.0)
        act(colq, colq[:, 0, :], colp[:, 2, :], AF.Exp, [colp])
        tt(DVE, colq, colq[:, 1, :], colq[:, 0, :], colp[:, 0, :], MUL, [colq, colp])
        tt(DVE, colq, colq[:, 2, :], colq[:, 0, :], colp[:, 1, :], MUL, [colq, colp])
        for kh in range(2):
            are, aim, dt_, dre, dim_, mag, ki, kr, sn, cs, fr, fi, bre, bim = SRt
            for i, t in enumerate((are, aim, dt_)):
                k.dma(SP, t[:], bass.AP(s5row_d.tensor, (l * 3 + i) * 1024 + kh * 512, [[0, 128], [1, 512]]), writes=[t])
            k.dma(SP, bre[:], bblk_d[l, 0][:, kh, :], writes=[bre]); k.dma(SP, bim[:], bblk_d[l, 1][:, kh, :], writes=[bim])
            act(dt_, dt_[:], dt_[:], AF.Exp, [dt_])
            tt(DVE, dre, dre[:], dt_[:], are[:], MUL, [dt_, are]); tt(DVE, dim_, dim_[:], dt_[:], aim[:], MUL, [dt_, aim])
            act(mag, mag[:], dre[:], AF.Exp, [dre])
            sincos(dim_, flat, (ki, kr, sn, cs))
            tt(DVE, cs, cs[:], mag[:], cs[:], MUL, [mag, cs]); tt(DVE, sn, sn[:], mag[:], sn[:], MUL, [mag, sn])
            ts(DVE, cs, cs[:], cs[:], -1.0, None, ADD, None, [cs])
            tt(DVE, mag, mag[:], are[:], are[:], MUL, [are]); tt(DVE, ki, ki[:], aim[:], aim[:], MUL, [aim])
            tt(DVE, mag, mag[:], mag[:], ki[:], ADD, [mag, ki]); recip(mag, mag[:], mag[:], [mag])
            tt(DVE, fr, fr[:], cs[:], are[:], MUL, [cs, are]); tt(DVE, ki, ki[:], sn[:], aim[:], MUL, [sn, aim])
            tt(DVE, fr, fr[:], fr[:], ki[:], ADD, [fr, ki]); tt(DVE, fr, fr[:], fr[:], mag[:], MUL, [fr, mag])
            tt(DVE, fi, fi[:], sn[:], are[:], MUL, [sn, are]); tt(DVE, ki, ki[:], cs[:], aim[:], MUL, [cs, aim])
            tt(DVE, fi, fi[:], fi[:], ki[:], SUB, [fi, ki]); tt(DVE, fi, fi[:], fi[:], mag[:], MUL, [fi, mag])
            tt(DVE, kr, kr[:], fr[:], bre[:], MUL, [fr, bre]); tt(DVE, ki, ki[:], fi[:], bim[:], MUL, [fi, bim])
            tt(DVE, sn, sn[:], fr[:], bim[:], MUL, [fr, bim]); tt(DVE, cs, cs[:], fi[:], bre[:], MUL, [fi, bre])
            bxv = Bx[kh].ap.rearrange("p (g a n) -> p g a n", g=8, a=2)
            bsv = Bsw[kh].ap.rearrange("p (g a n) -> p g a n", g=8, a=2)
            tt(DVE, Bx[kh], bxv[:, :, 0, :], v3(kr), v3(ki), SUB, [kr, ki])
            tt(DVE, Bsw[kh], bsv[:, :, 1, :], v3(kr), v3(ki), SUB, [kr, ki])
            tt(DVE, Bx[kh], bxv[:, :, 1, :], v3(sn), v3(cs), ADD, [sn, cs])
            stt(Bsw[kh], bsv[:, :, 0, :], v3(sn), -1.0, v3(cs), MUL, SUB, [sn, cs])
            ts(DVE, mag, mag[:], dim_[:], tauc, None, MUL, None, [dim_, misc])
            sincos(mag, flat, (ki, kr, sn, cs))
            ts(DVE, fr, fr[:], dre[:], tauc, None, MUL, None, [dre, misc]); act(fr, fr[:], fr[:], AF.Exp, [fr], scale=-1.0)
            tt(DVE, Pr, Pr[:, 8 * kh:8 * kh + 8, :], v3(fr), v3(cs), MUL, [fr, cs])
            stt(Pi, Pi[:, 8 * kh:8 * kh + 8, :], v3(fr), -1.0, v3(sn), MUL, MUL, [fr, sn])
        for q in range(4):
            ang, mexp, ki, kr, sn, cs = SRt[0:6]
            taub = taurow.unsqueeze(1).to_broadcast([128, 4, 128])
            dimb = colq[:, 2, 4 * q:4 * q + 4].unsqueeze(2).to_broadcast([128, 4, 128])
            dreb = colq[:, 1, 4 * q:4 * q + 4].unsqueeze(2).to_broadcast([128, 4, 128])
            tt(DVE, ang, v4(ang), dimb, taub, MUL, [colq, misc])
            tt(DVE, mexp, v4(mexp), dreb, taub, MUL, [colq, misc]); act(mexp, mexp[:], mexp[:], AF.Exp, [mexp])
            sincos(ang, flat, (ki, kr, sn, cs))
            stt(T1, T1[:, 4 * q:4 * q + 4, :], v4(mexp), sgn1, v4(cs), MUL, MUL, [mexp, misc, cs])
            stt(T2, T2[:, 4 * q:4 * q + 4, :], v4(mexp), -1.0, v4(sn), MUL, MUL, [mexp, sn])
        angw, mw, ki, kr, sn, cs = SRt[6:12]
        ts(DVE, angw, sm(angw), colq[:, 2, :], 128.0, None, MUL, None, [colq])
        ts(DVE, mw, sm(mw), colq[:, 1, :], 128.0, None, MUL, None, [colq]); act(mw, sm(mw), sm(mw), AF.Exp, [mw])
        sincos(angw, sm, (ki, kr, sn, cs))
        tt(DVE, W1, W1[:], sm(mw), sm(cs), MUL, [mw, cs]); tt(DVE, W2, W2[:], sm(mw), sm(sn), MUL, [mw, sn])

    def load_seq(sq):
        for blk in range(NBLK):
            st = stg[blk % 2]; c = blk // 4; o = (blk % 4) * 128
            k.dma(SP, st[:], x_d[sq, blk * 128:(blk + 1) * 128, :], writes=[st])
            for half in range(2):
                b = B[2 + half]
                for j in range(4):
                    f = half * 4 + j
                    transpose(b, b[:, j * 128:(j + 1) * 128], st[:, f * 128:(f + 1) * 128], identf[:], [st, identf])
                cp(DVE if half == 0 else ACT, xTc[c], xTc[c][:, half * 4:(half + 1) * 4, o:o + 128],
                   b[:].rearrange("p (a b) -> p a b", a=4), [b])

    def store_seq(sq):
        for blk in range(NBLK):
            st = stg[blk % 2]; c = blk // 4; o = (blk % 4) * 128
            for half in range(2):
                b = B[2 + half]
                for j in range(4):
                    f = half * 4 + j
                    transpose(b, b[:, j * 128:(j + 1) * 128], xTc[c][:, f, o:o + 128], identf[:], [xTc[c], identf])
                cp(DVE if half == 0 else ACT, st, st[:, half * 512:(half + 1) * 512], b[:], [b])
            k.dma(SP, y_d[sq, blk * 128:(blk + 1) * 128, :], st[:], reads=[st])

    def prenorm(l, c):
        xc_ = xTc[c]; ssb = B[5]
        for f in range(8):
            s_ = sqb[f % 2]
            act(s_, s_[:], xc_[:, f, :], AF.Square, [xc_])
            mm(ssb, ssb[:], onesb, s_[:], f == 0, f == 7, [cmats, s_])
        act(sg[0], sg[0][:], ssb[:], AF.Sqrt, [ssb, misc], bias=eps_ap, scale=1.0 / D)
        recip(rstd, rstd[:], sg[0][:], [sg[0]])
        for f in range(8):
            stt(hT, hT[:, f, :], xc_[:, f, :], pp[:, f:f + 1], rstd[:], MUL, MUL, [xc_, pp, rstd])

    def phaseM(l, c):
        t0 = c * TC
        k.dma(POOL, ropeC[:], ropeC_d[:, t0:t0 + TC], writes=[ropeC]); k.dma(POOL, ropeS[:], ropeS_d[:, t0:t0 + TC], writes=[ropeS])
        def sec_sb():
            Z = [B[2], B[3]]; PO = B[4]; PTb = B[5]; pv = PTb.ap.bitcast(BF16)
            for i in range(2):
                b = proj_fm(l, i, B[2 + i]); act(QT, QT[:, i, :], b[:], AF.Copy, [b], scale=0.125)
            for i in range(2):
                b = proj_fm(l, 2 + i, B[2 + i]); cp(DVE, KT, KT[:, i, t0:t0 + TC], b[:], [b])
            yield
            proj_tm(l, 4, Vc, lambda blk: Vc[:, c * 4 + blk, :], [B[2], B[3]])
            yield
            units = []
            for qi in range(4):
                qb = c * 4 + qi
                for a in range(qb, -1, -1):
                    units.append((qi, qb, a))
            n_u = len(units)
            def S1(u, par):
                qi, qb, a = u; z = Z[par]
                mm(z, z[:, 0:256], zrow[0:1, 0:128], zrow[0:1, 0:256], True, False, [zrow])
                for ti in range(2):
                    mm(z, z[:, ti * 128:(ti + 1) * 128], KT[64 * par:64 * par + 64, ti, a * 128:(a + 1) * 128],
                       QT[64 * par:64 * par + 64, ti, qi * 128:(qi + 1) * 128], False, False, [KT, QT])
            def S2(u, par):
                qi, qb, a = u; diag = (a == qb); z = Z[par]; e_ = sbE[par]; sp_ = sbSP[par]
                act(e_, e_[:], z[:, 0:256], AF.Exp, [z])
                act(sp_, sp_[:], e_[:], AF.Ln, [e_, misc], bias=one_ap)
                if diag: tt(POOL, sp_, sp_[:], sp_[:], mask2[:], MUL, [sp_, mask2])
            def S3(u, par):
                qi, qb, a = u; diag = (a == qb); z = Z[par]; sp_ = sbSP[par]; c_ = sbC[par]
                mm(z, z[:, 0:256], ntri, sp_[:], False, diag, [cmats, sp_])
                if not diag: mm(z, z[:, 0:256], nones, c_[:], False, True, [cmats, c_])
            def S4(u, par):
                qi, qb, a = u; diag = (a == qb); z = Z[par]; sp_ = sbSP[par]; w_ = sbW[par]; c_ = sbC[par]
                act(w_, w_[:], z[:, 0:256], AF.Exp, [z])
                if diag: tt(POOL, w_, w_[:], w_[:], mask2[:], MUL, [w_, mask2])
                if a > 0:
                    if diag: cp(POOL, c_, c_[:], sp_[:], [sp_])
                    else: tt(POOL, c_, c_[:], c_[:], sp_[:], ADD, [c_, sp_])
            def S5(u, par):
                qi, qb, a = u; w_ = sbW[par]
                if a == qb and par == 0:
                    mm(PO, PO[:, 0:256], zrow[0:1, 0:128], zrow[0:1, 0:256], True, False, [zrow])
                for ti in range(2):
                    h = 2 * ti + par
                    mm(PO, PO[:, h * 64:(h + 1) * 64], w_[:, ti * 128:(ti + 1) * 128], Vc[:, a, h * 64:(h + 1) * 64],
                       False, (a == 0 and par == 1 and ti == 1), [w_, Vc])
                if a == 0 and par == 1:
                    cp(ACT, osb, osb[:], PO[:, 0:256], [PO])
                    for ti in range(2):
                        transpose(PTb, pv[:, ti * 128:(ti + 1) * 128], osb[:, ti * 128:(ti + 1) * 128], identb, [osb, cmats])
                    cp(DVE, yTt[0], yTb[:, 0:2, qi * 128:(qi + 1) * 128], pv[:, 0:256].rearrange("p (a b) -> p a b", a=2), [PTb])
                    yTt[1].w = yTt[0].w; yTt[1].r = {}
            sched = {}
            for i in range(n_u):
                for par in range(2):
                    base = 4 * i + par
                    for off, fn, ispe in ((0, S1, 1), (1, S2, 0), (2, S3, 1), (3, S4, 0), (5, S5, 1)):
                        sched.setdefault(base + off, []).append((1 - ispe, i, par, fn))
            for tk in sorted(sched):
                for _, i, par, fn in sorted(sched[tk], key=lambda x: (x[0], x[1], x[2])):
                    fn(units[i], par)
                yield

        def sec_s5():
            Yb = B[1]
            for i in range(2):
                b = proj_fm(l, 6 + i, B[6 + i]); cp(ACT, uT, uT[:, i, :], b[:], [b])
            yield
            x4 = lambda b: b.ap.rearrange("p (g a n) -> p g a n", g=4, a=2)
            g4 = lambda t: t.ap.rearrange("p g (a n) -> p g a n", a=2)
            for sc in range(4):
                for q in range(4):
                    kh = q // 2; hq = q % 2; st_ = (sc * 4 + q) % 2
                    mm(B[6], B[6][:], uT[:, kh, sc * 128:(sc + 1) * 128], Bx[kh][:, hq * 512:(hq + 1) * 512], True, True, [uT, Bx[kh]])
                    mm(B[7], B[7][:], uT[:, kh, sc * 128:(sc + 1) * 128], Bsw[kh][:, hq * 512:(hq + 1) * 512], True, True, [uT, Bsw[kh]])
                    yield
                    prb = Pr[:, 4 * q:4 * q + 4, :].unsqueeze(2).to_broadcast([128, 4, 2, 64])
                    pib = Pi[:, 4 * q:4 * q + 4, :].unsqueeze(2).to_broadcast([128, 4, 2, 64])
                    tt(DVE, G1[st_], g4(G1[st_]), x4(B[6]), prb, MUL, [B[6], Pr])
                    tt(DVE, G2[st_], g4(G2[st_]), x4(B[7]), pib, MUL, [B[7], Pi])
                    yield
                    for gl in range(4):
                        mm(B[0], B[0][:, gl * 128:(gl + 1) * 128], G1[st_][:, gl, :], tri, True, False, [G1[st_], cmats])
                        mm(B[0], B[0][:, gl * 128:(gl + 1) * 128], G2[st_][:, gl, :], tri, False, True, [G2[st_], cmats])
                    yield
                    for gl in range(4):
                        g = 4 * q + gl
                        stt(A1[st_], A1[st_][:, gl, :], B[0][:, gl * 128:(gl + 1) * 128], carry[q][:, gl:gl + 1], T1[:, g, :], ADD, MUL, [B[0], carry[q], T1])
                        stt(A2[st_], A2[st_][:, gl, :], B[0][:, gl * 128:(gl + 1) * 128], carry[q][:, gl:gl + 1], T2[:, g, :], ADD, MUL, [B[0], carry[q], T2])
                    slast = B[0][:].rearrange("p (g t) -> p g t", g=4)[:, :, 127:128].rearrange("p g o -> p (g o)")
                    tt(DVE, sfull, sfull[:], slast, carry[q][:], ADD, [B[0], carry[q]])
                    tt(DVE, U1, U1[:], sfull[:], W1[:, 4 * q:4 * q + 4], MUL, [sfull, W1])
                    tt(DVE, U2, U2[:], sfull[:], W2[:, 4 * q:4 * q + 4], MUL, [sfull, W2])
                    yield
                    yb = Yb
                    for gl in range(4):
                        g = 4 * q + gl
                        mm(yb, yb[:, kh * 128:(kh + 1) * 128], Cm[:, 2 * g, :], A1[st_][:, gl, :], (hq == 0 and gl == 0), False, [Cm, A1[st_]])
                        mm(yb, yb[:, kh * 128:(kh + 1) * 128], Cm[:, 2 * g + 1, :], A2[st_][:, gl, :], False, (hq == 1 and gl == 3), [Cm, A2[st_]])
                    if hq == 1:
                        cp(ACT, LT[kh], LT[kh][:, sc * 128:(sc + 1) * 128], yb[:, kh * 128:(kh + 1) * 128], [yb])
                    cb = B[6]
                    mm(cb, cb[:, 0:4], identb, U1[:], True, False, [cmats, U1]); mm(cb, cb[:, 0:4], Jm, U2[:], False, True, [cmats, U2])
                    yield
                    cp(ACT, carry[q], carry[q][:], cb[:, 0:4], [cb])
                    yield
            for kh in range(2):
                yv = LT[kh]; x2 = LT[2]
                stt(yv, yv[:], uT[:, kh, :], pp[:, 16 + kh:17 + kh], yv[:], MUL, ADD, [uT, pp, yv])
                act(x2, x2[:], yv[:], AF.Square, [yv])
                ts(DVE, x2, x2[:], x2[:], 0.044715, 1.0, MUL, ADD, [x2])
                tt(DVE, x2, x2[:], x2[:], yv[:], MUL, [x2, yv])
                act(x2, x2[:], x2[:], AF.Sigmoid, [x2], scale=1.5957691216057308)
                tt(DVE, yv, yv[:], yv[:], x2[:], MUL, [yv, x2])
                cp(POOL, ygb, ygb[:, kh, :], yv[:], [yv])
            for e in range(2):
                b = B[6 + e]
                for kh in range(2):
                    mm(b, b[:], wglu[:, kh, e * 128:(e + 1) * 128], ygb[:, kh, :], kh == 0, kh == 1, [wglu, ygb])
                act(LT[2 + e], LT[2 + e][:], b[:], AF.Sigmoid, [b, pp], bias=pp[:, 18 + e:19 + e])
                tt(DVE, yTt[2 + e], yTb[:, 2 + e, :], LT[e][:], LT[2 + e][:], MUL, [LT[e], LT[2 + e]])


        def sec_ret():
            for ti in range(2):
                tt(DVE, qxi, qxi[:, ti, :].rearrange("p (n i) -> p n i", n=4), qrot[:, ti, :].rearrange("p (n i) -> p n i", n=4),
                   xitab[:, ti, :].unsqueeze(1).to_broadcast([128, 4, 128]), MUL, [qrot, xitab])
            pv2 = B[6].ap.bitcast(BF16); pv = B[5].ap.bitcast(BF16)
            for n in range(4):
                SX = [B[2], B[3]]
                for par in range(2):
                    for ti in range(2):
                        mm(SX[par], SX[par][:, ti * 128:(ti + 1) * 128], krot[64 * par:64 * par + 64, ti, n * 128:(n + 1) * 128],
                           qrot[64 * par:64 * par + 64, ti, n * 128:(n + 1) * 128], True, True, [krot, qrot])
                    tt(DVE, PTt, PTt[:, par, :], SX[par][:, 0:256], dtab[:, par, :], MUL, [SX[par], dtab])
                po = B[4]
                mm(po, po[:, 0:256], zrow[0:1, 0:128], zrow[0:1, 0:256], True, False, [zrow])
                for par in range(2):
                    for ti in range(2):
                        h = 2 * ti + par
                        mm(po, po[:, h * 64:(h + 1) * 64], PTt[:, par, ti * 128:(ti + 1) * 128], vt[:, n, h * 64:(h + 1) * 64], False, False, [PTt, vt])
                for ti in range(2):
                    mm(po, po[:, ti * 128:(ti + 1) * 128], qxi[:, ti, n * 128:(n + 1) * 128], Rb[:, ti, :], False, ti == 1, [qxi, Rb])
                if 'ret1' in parts: continue
                cp(ACT, osbf, osbf[:], po[:, 0:256], [po])
                o3 = osbf.ap.rearrange("p (h e) -> p h e", h=4); q3 = osq.ap.rearrange("p (h e) -> p h e", h=4)
                k.op(DVE, lambda e: e.tensor_reduce(out=st4[0][:], in_=o3, axis=AX.X, op=ADD), reads=[osbf], writes=[st4[0]])
                act(osq, osq[:], osbf[:], AF.Square, [osbf])
                k.op(DVE, lambda e: e.tensor_reduce(out=st4[1][:], in_=q3, axis=AX.X, op=ADD), reads=[osq], writes=[st4[1]])
                ts(DVE, st4[2], st4[2][:], st4[0][:], 1.0 / 64, None, MUL, None, [st4[0]])
                tt(DVE, st4[3], st4[3][:], st4[2][:], st4[2][:], MUL, [st4[2]])
                stt(st4[3], st4[3][:], st4[1][:], 1.0 / 64, st4[3][:], MUL, SUB, [st4[1], st4[3]])
                act(st4[4], st4[4][:], st4[3][:], AF.Sqrt, [st4[3], misc], bias=eps_ap)
                recip(st4[5], st4[5][:], st4[4][:], [st4[4]])
                for h in range(4):
                    ts(DVE, onb, onb[:, h * 64:(h + 1) * 64], osbf[:, h * 64:(h + 1) * 64], st4[2][:, h:h + 1], st4[5][:, h:h + 1], SUB, MUL, [osbf, st4[2], st4[5]])
                if 'ret2' in parts: continue
                for ti in range(2):
                    transpose(B[5], pv[:, ti * 128:(ti + 1) * 128], onb[:, ti * 128:(ti + 1) * 128], identb, [onb, cmats])
                cp(ACT, yTt[4], yTb[:, 4:6, n * 128:(n + 1) * 128], pv[:, 0:256].rearrange("p (a b) -> p a b", a=2), [B[5]])
                yTt[5].w = yTt[4].w; yTt[5].r = {}
                if 'ret3' in parts: continue
                for ti in range(2):
                    transpose(B[6], pv2[:, ti * 128:(ti + 1) * 128], krot[:, ti, n * 128:(n + 1) * 128], identb, [krot, cmats])
                tt(DVE, kz, kz[:], pv2[:, 0:256], ztab[:], MUL, [B[6], ztab])
                kvb = B[7]
                for ti in range(2):
                    mm(kvb, kvb[:, ti * 128:(ti + 1) * 128], kz[:, ti * 128:(ti + 1) * 128], vt[:, n, ti * 128:(ti + 1) * 128], True, True, [kz, vt])
                tt(DVE, osq, osq.ap.rearrange("p (a b) -> p a b", a=2), kvb[:, 0:256].rearrange("p (a b) -> p a b", a=2),
                   cmats[:, 7:8, :].to_broadcast([128, 2, 128]), MUL, [kvb, cmats])
                for ti in range(2):
                    stt(Rst, Rst[:, ti, :], Rst[:, ti, :], gdec[:, ti:ti + 1], osq[:, ti * 128:(ti + 1) * 128], MUL, ADD, [Rst, gdec, osq])
                cp(POOL, Rb, Rb[:], Rst[:], [Rst])
                yield

        def sec_lru():
            for i in range(2):
                xc = LT[0]; r = LT[1]; ig = LT[2]; a_ = LT[3]; h_ = LT[4]
                ts(DVE, xc, xc[:], uL[:, i, 0:TC], pp[:, 20 + 4 * i:21 + 4 * i], pp[:, 28 + i:29 + i], MUL, ADD, [uL, pp])
                for kk in range(1, 4):
                    stt(xc, xc[:], uL[:, i, kk:kk + TC], pp[:, 20 + 4 * i + kk:21 + 4 * i + kk], xc[:], MUL, ADD, [uL, pp, xc])
                cp(POOL, xcb, xcb[:], xc[:], [xc])
                mm(B[0], B[0][:], WA[:, i, :], xcb[:], True, True, [WA, xcb]); mm(B[1], B[1][:], WX[:, i, :], xcb[:], True, True, [WX, xcb])
                act(r, r[:], B[0][:], AF.Sigmoid, [B[0], pp], bias=pp[:, 30 + i:31 + i])
                act(ig, ig[:], B[1][:], AF.Sigmoid, [B[1], pp], bias=pp[:, 32 + i:33 + i])
                act(a_, a_[:], r[:], AF.Exp, [r, cl], scale=cl[:, i:i + 1])
                act(r, r[:], r[:], AF.Exp, [r, cl2], scale=cl2[:, i:i + 1])
                act(r, r[:], r[:], AF.Sqrt, [r, misc], bias=one_ap, scale=-1.0)
                tt(POOL, ig, ig[:], ig[:], xc[:], MUL, [ig, xc]); tt(DVE, r, r[:], r[:], ig[:], MUL, [r, ig])
                k.op(DVE, lambda e: e.tensor_tensor_scan(out=h_[:], data0=a_[:], data1=r[:], initial=hst[:, i:i + 1], op0=MUL, op1=ADD),
                     reads=[a_, r, hst], writes=[h_])
                cp(ACT, hst, hst[:, i:i + 1], h_[:, TC - 1:TC], [h_])
                cp(POOL, yTt[6 + i], yTb[:, 6 + i, :], h_[:], [h_])
                cp(POOL, uL, uL[:, i, 0:3], uL[:, i, TC:TC + 3], [uL])
                yield

        def sec_p():
            bk = B[5]
            for (ct, ctsw, dst) in ((8, 56, qrot), (10, 58, krot)):
                for i in range(2):
                    b = proj_fm(l, ct + i, bk); tt(DVE, rt1, rt1[:], b[:], ropeC[:], MUL, [b, ropeC])
                    b2 = proj_fm(l, ctsw + i, bk); tt(DVE, rt2, rt2[:], b2[:], ropeS[:], MUL, [b2, ropeS])
                    tt(POOL, dst, dst[:, i, :], rt1[:], rt2[:], ADD, [rt1, rt2])
                    yield
            proj_tm(l, 12, vt, lambda blk: vt[:, blk, :], [bk])
            yield
            for i in range(2):
                b = proj_fm(l, 14 + i, bk); cp(DVE, uL, uL[:, i, 3:3 + TC], b[:], [b])
                yield

        def run_wave(gens):
            st = [[g, n, 0, True] for g, n in gens]
            while any(x[3] for x in st):
                cand = [x for x in st if x[3]]
                x = min(cand, key=lambda y: y[2] / max(y[1], 1))
                try:
                    next(x[0]); x[2] += 1
                except StopIteration:
                    x[3] = False
        npairs = sum(c * 4 + qi + 1 for qi in range(4))
        wave_a = []
        if 'sb' in parts: wave_a.append((sec_sb(), 4 * npairs + 9))
        if 's5' in parts: wave_a.append((sec_s5(), 16 * 6 + 1))
        wave_a.append((sec_p(), 8))
        run_wave(wave_a)
        wave_b = []
        if 'ret' in parts: wave_b.append((sec_ret(), 4))
        if 'lru' in parts: wave_b.append((sec_lru(), 2))
        run_wave(wave_b)

    def phaseO(l, c):
        xc_ = xTc[c]
        for ct in range(8):
            b = proj_fm(l, 16 + ct)
            act(prodS[ct % 4], prodS[ct % 4][:], b[:], AF.Silu, [b])
            tt(DVE, yTt[ct], yTb[:, ct, :], yTb[:, ct, :], prodS[ct % 4][:], MUL, [yTt[ct], prodS[ct % 4]])
        if 'O1' in parts: return
        for f in range(8):
            wb = WB[f % 2]
            k.dma(POOL, wb[:], wbr_d[l].rearrange("n (k p) d -> p (n k) d", p=128)[:, :, f * 128:(f + 1) * 128], writes=[wb])
            for n in range(4):
                bm = proj_fm(l, 24 + n * 8 + f)
                by = B[2 + n % 2]
                for kk in range(2):
                    mm(by, by[:], wb[:, n * 2 + kk, :], yTb[:, 2 * n + kk, :], kk == 0, kk == 1, [wb, yTt[2 * n + kk]])
                s_ = sg[n % 2]
                act(s_, s_[:], bm[:], AF.Sigmoid, [bm])
                tt(DVE, prodS[n], prodS[n][:], s_[:], by[:], MUL, [s_, by])
            bs = B[4]
            for n in range(4):
                mm(bs, bs[:], identb, prodS[n][:], n == 0, n == 3, [cmats, prodS[n]])
            cp(ACT, merged, merged[:, f, :], bs[:], [bs])
        if 'O2' in parts: return
        for half in range(2):
            cs_ = slice(half * 256, (half + 1) * 256)
            ssb = B[5]
            pend = None
            for e in range(8):
                w = load_w(wout_d[l].rearrange("(k p) c -> p k c", p=128)[:, :, e * 128:(e + 1) * 128])
                b = B[pbn[0] % 2]; pbn[0] += 1
                for d in range(8):
                    mm(b, b[:, 0:256], w[:, d, :], merged[:, d, cs_], d == 0, d == 7, [w, merged])
                if pend is not None:
                    pe_ = pend
                    mm(ssb, ssb[:, 0:256], onesb, sqb[pe_ % 2][:, 0:256], pe_ == 0, pe_ == 7, [cmats, sqb[pe_ % 2]])
                cp(DVE, outT, outT[:, e, :], b[:, 0:256], [b])
                act(sqb[e % 2], sqb[e % 2][:, 0:256], outT[:, e, :], AF.Square, [outT])
                pend = e
            mm(ssb, ssb[:, 0:256], onesb, sqb[pend % 2][:, 0:256], pend == 0, pend == 7, [cmats, sqb[pend % 2]])
            if 'x1' in parts or 'x2' in parts: continue
            act(sg[0], sg[0][:, 0:256], ssb[:, 0:256], AF.Sqrt, [ssb, misc], bias=eps_ap, scale=1.0 / D)
            recip(rstd, rstd[:, 0:256], sg[0][:, 0:256], [sg[0]])
            for e in range(8):
                tb_ = sg[e % 2]
                stt(tb_, tb_[:, 0:256], outT[:, e, :], pp[:, 8 + e:9 + e], rstd[:, 0:256], MUL, MUL, [outT, pp, rstd])
                tt(POOL, xc_, xc_[:, e, cs_], xc_[:, e, cs_], tb_[:, 0:256], ADD, [xc_, tb_])

    for sq in range(NSEQ):
        k.barrier(); load_seq(sq)
        for l in range(L):
            k.barrier()
            if 'setup' in parts: layer_setup(l)
            k.barrier()
            if 'prenorm' in parts: prenorm(l, 0)
            k.barrier()
            for c in range(NCH):
                if 'proj' in parts: phaseM(l, c)
                if "yT" in dbg_d and sq == 0 and l == 0:
                    k.dma(POOL, dbg_d["yT"][:, :, c * TC:(c + 1) * TC], yTb[:], reads=yTt)
                k.barrier()
                if 'O' in parts: phaseO(l, c)
                if c + 1 < NCH and 'prenorm' in parts: prenorm(l, c + 1)
                k.barrier()
        store_seq(sq)
    k.barrier()
    import os
    if os.environ.get('KSTAT'): print('ENGINE COUNTS', {E.name: (E.cnt, E.real) for E in k.engs}, 'dma', {E.name: sum(E.dvals)//16 for E in k.engs})
    return nc, k


def _prep_shared(inp, S, L):
    f32 = np.float32
    g = lambda n: np.asarray(inp[n], dtype=f32)[:L]
    w_in = g('w_in')
    perm = []
    for base in (1024, 1280):
        for h in range(4):
            b0 = base + 64 * h
            perm += list(range(b0 + 32, b0 + 64)) + list(range(b0, b0 + 32))
    win = np.ascontiguousarray(np.concatenate([w_in, w_in[:, :, perm]], axis=2))
    pre_g, post_g = g('pre_norm_g'), g('post_norm_g')
    pp = np.zeros((L, 128, 40), f32)
    for l in range(L):
        pp[l, :, 0:8] = pre_g[l].reshape(8, 128).T
        pp[l, :, 8:16] = post_g[l].reshape(8, 128).T
        pp[l, :, 16:18] = g('ssm_d')[l].reshape(2, 128).T
        pp[l, :, 18:20] = g('ssm_b_glu')[l].reshape(2, 128).T
        cw = g('lru_conv_w')[l]
        for i in range(2):
            pp[l, :, 20 + 4 * i:24 + 4 * i] = cw[:, 128 * i:128 * (i + 1)].T
        pp[l, :, 28:30] = g('lru_conv_b')[l].reshape(2, 128).T
        pp[l, :, 30:32] = g('lru_b_a')[l].reshape(2, 128).T
        pp[l, :, 32:34] = g('lru_b_x')[l].reshape(2, 128).T
        pp[l, :, 34:36] = g('lru_lambda')[l].reshape(2, 128).T
    lruw = np.zeros((L, 2, 128, 2, 128), f32)
    for l in range(L):
        for ax, nm in enumerate(('lru_w_a', 'lru_w_x')):
            w = g(nm)[l]
            for i in range(2):
                for bl in range(2):
                    lruw[l, ax, 64 * bl:64 * bl + 64, i, 64 * bl:64 * bl + 64] = w[2 * i + bl]
    a_re, a_im, ldt = g('ssm_a_re'), g('ssm_a_im'), g('ssm_log_dt')
    s5row = np.stack([a_re.reshape(L, 1024), a_im.reshape(L, 1024), np.repeat(ldt, 64, axis=1)], axis=1)
    s5col = np.zeros((L, 128, 3, 16), f32)
    for l in range(L):
        s5col[l, :, 0, :] = np.concatenate([a_re[l].T, a_re[l].T], axis=0)
        s5col[l, :, 1, :] = np.concatenate([a_im[l].T, a_im[l].T], axis=0)
        s5col[l, :, 2, :] = np.broadcast_to(ldt[l][None, :], (128, 16))
    bblk = np.zeros((L, 2, 128, 2, 512), f32)
    for l in range(L):
        for ri, nm in enumerate(('ssm_b_re', 'ssm_b_im')):
            bb = g(nm)[l]
            for gg in range(16):
                kh, gl = gg // 8, gg % 8
                bblk[l, ri, gl * 16:(gl + 1) * 16, kh, gl * 64:(gl + 1) * 64] = bb[gg].T
    cab = np.zeros((L, 128, 32, 128), f32)
    c_re, c_im = g('ssm_c_re'), g('ssm_c_im')
    for l in range(L):
        for gg in range(16):
            co = 16 * (gg % 8)
            cab[l, 0:64, 2 * gg, co:co + 16] = c_re[l, gg].T
            cab[l, 64:128, 2 * gg, co:co + 16] = c_im[l, gg].T
            cab[l, 0:64, 2 * gg + 1, co:co + 16] = c_im[l, gg].T
            cab[l, 64:128, 2 * gg + 1, co:co + 16] = c_re[l, gg].T
    p = np.arange(128)
    invf = (np.float32(10000.0) ** (-(np.arange(32, dtype=f32) / np.float32(32)))).astype(f32)
    ang = (np.arange(S, dtype=f32)[None, :] * invf[(p % 64) % 32][:, None]).astype(f32)
    ropeC = np.cos(ang).astype(f32)
    ropeS = (np.sin(ang) * np.where((p % 64) < 32, -1.0, 1.0)[:, None]).astype(f32)
    gam = 1.0 - 2.0 ** (-5.0 - np.arange(4))
    ii = np.arange(128)
    dtab = np.zeros((128, 2, 256), f32)
    for par in range(2):
        for ti in range(2):
            h = 2 * ti + par
            rel = ii[None, :] - ii[:, None]
            dtab[:, par, ti * 128:(ti + 1) * 128] = np.where(rel >= 0, gam[h] ** np.maximum(rel, 0), 0.0) / 8.0
    ztab = np.zeros((128, 256), f32)
    for h in range(4):
        ztab[:, h * 64:(h + 1) * 64] = (gam[h] ** (127 - ii) / 8.0)[:, None]
    xitab = np.zeros((128, 2, 128), f32); gdec = np.zeros((128, 2), f32)
    for ti in range(2):
        for hl in range(2):
            h = 2 * ti + hl
            xitab[64 * hl:64 * hl + 64, ti, :] = (gam[h] ** (ii + 1.0))[None, :]
            gdec[64 * hl:64 * hl + 64, ti] = gam[h] ** 128
    cm = np.zeros((128, 8, 128), f32)
    cm[:, 0, :] = np.eye(128)
    cm[:, 1, :] = -1.0 * (ii[:, None] >= ii[None, :])
    cm[:, 2, :] = -1.0
    cm[:, 3, :] = (ii[:, None] <= ii[None, :])
    cm[:, 4, :] = (ii[:, None] < ii[None, :])
    for m in range(64):
        cm[m + 64, 5, m] = -1.0
        cm[m, 5, m + 64] = 1.0
    cm[:, 6, :] = 1.0
    cm[0:64, 7, 0:64] = 1.0; cm[64:128, 7, 64:128] = 1.0
    misc = np.zeros((128, 132), f32)
    misc[:, 0] = p; misc[:, 1] = np.where(p < 64, 1.0, -1.0); misc[:, 2] = EPS; misc[:, 3] = 1.0
    misc[:, 4:132] = np.arange(128)[None, :]
    return dict(win=win, wbr=g('w_branch'), wout=g('w_out'), wglu=g('ssm_w_glu'), pp=pp, lruw=lruw, s5row=np.ascontiguousarray(s5row),
                s5col=s5col, bblk=bblk, cab=cab, ropeC=ropeC, ropeS=ropeS, dtab=dtab, ztab=ztab, xitab=xitab, gdec=gdec,
                cmats=cm, misc=misc)


def run(inp, S, NSEQ, DEPTH, ncores, dbg=None, parts=None):
    shared = _prep_shared(inp, S, DEPTH)
    x = np.asarray(inp['x'], dtype=np.float32)
    nc = build(S, NSEQ, DEPTH, dbg, parts)
    in_maps = []
    for i in range(ncores):
        m = dict(shared); m['x'] = np.ascontiguousarray(x[i * NSEQ:(i + 1) * NSEQ]); in_maps.append(m)
    res = run_bass_kernel_spmd(nc, in_maps, core_ids=list(range(ncores)))
    return res


def kernel(**inputs):
    x = np.asarray(inputs['x'])
    Bn, S, _ = x.shape
    ncores = 8
    NSEQ = Bn // ncores
    res = run(inputs, S, NSEQ, 2, ncores)
    return np.concatenate([r["y"] for r in res.results], axis=0).astype(np.float32)
```

```python
import math
import numpy as np
import concourse.bass as bass
import concourse.mybir as mybir
from concourse.bass_utils import run_bass_kernel_spmd

F32 = mybir.dt.float32; BF16 = mybir.dt.bfloat16; I32 = mybir.dt.int32
AF = mybir.ActivationFunctionType; ALU = mybir.AluOpType; AX = mybir.AxisListType
D = 1024; TC = 512; NBK = 4; EPS = 1e-6
NCOLT = 60
PI = math.pi


class T:
    __slots__ = ('ap', 'w', 'r')
    def __init__(s, ap): s.ap = ap; s.w = None; s.r = {}
    def __getitem__(s, idx): return s.ap[idx]


class Eng:
    def __init__(s, nc, name, eng, is_pe=False):
        s.name = name; s.eng = eng; s.sem = nc.alloc_semaphore("sem_" + name); s.cnt = 0; s.real = 0; s.known = {}; s.is_pe = is_pe
        s.dsems = []; s.dvals = []; s.dnext = 0


class K:
    def __init__(s, nc, ndma=(16, 16, 4), needed=None):
        s.nc = nc; s.needed = needed; s.waited = set(); s.rmap = {}
        s.PE = Eng(nc, "pe", nc.tensor, True); s.ACT = Eng(nc, "act", nc.scalar); s.DVE = Eng(nc, "dve", nc.vector)
        s.POOL = Eng(nc, "pool", nc.gpsimd); s.SP = Eng(nc, "sp", nc.sync)
        s.engs = [s.PE, s.ACT, s.DVE, s.POOL, s.SP]
        for E, n in ((s.SP, ndma[0]), (s.POOL, ndma[1]), (s.ACT, ndma[2])):
            E.dsems = [nc.alloc_semaphore(f"d_{E.name}_{i}") for i in range(n)]; E.dvals = [0] * n
        s.nt = 0
    def sb(s, shape, dt, name=None):
        s.nt += 1
        return T(s.nc.alloc_sbuf_tensor("s_" + (name or f"t{s.nt}"), list(shape), dt).ap())
    def ps(s, shape, dt=F32, name=None):
        s.nt += 1
        return T(s.nc.alloc_psum_tensor("ps_" + (name or f"p{s.nt}"), list(shape), dt).ap())
    def _waits(s, E, reads, writes):
        deps = {}
        def add(ev, war=False):
            if ev is None: return
            key, sem, val, who = ev
            if who is E and (E.is_pe or war): return
            if deps.get(key, (None, 0))[1] < val: deps[key] = (sem, val)
        for t in reads: add(t.w)
        for t in writes:
            add(t.w)
            for ev in t.r.values(): add(ev, True)
        for key, (sem, val) in deps.items():
            if E.known.get(key, 0) >= val: continue
            s._wait(E, key, sem, val)
    def _wait(s, E, key, sem, val):
        rv = val
        if not key.startswith("d_"):
            s.waited.add((key, val))
            if s.needed is not None: rv = s.rmap[(key, val)]
        E.eng.wait_ge(sem, rv); E.known[key] = val
    def _post(s, ev, reads, writes):
        for t in writes: t.w = ev; t.r = {}
        for t in reads: t.r[ev[0]] = ev
    def op(s, E, fn, reads=(), writes=()):
        s._waits(E, reads, writes)
        ins = fn(E.eng); E.cnt += 1
        if s.needed is None or (E.name, E.cnt) in s.needed:
            E.real += 1; ins.then_inc(E.sem, 1); s.rmap[(E.name, E.cnt)] = E.real
        s._post((E.name, E.sem, E.cnt, E), reads, writes)
        return ins
    def dma(s, Q, out, in_, reads=(), writes=(), **kw):
        i = Q.dnext; Q.dnext = (i + 1) % len(Q.dsems); sem = Q.dsems[i]; key = f"d_{Q.name}_{i}"
        if Q.dvals[i] > 0 and Q.known.get(key, 0) < Q.dvals[i]:
            Q.eng.wait_ge(sem, Q.dvals[i]); Q.known[key] = Q.dvals[i]
        s._waits(Q, reads, writes)
        ins = Q.eng.dma_start(out=out, in_=in_, **kw); Q.dvals[i] += 16; ins.then_inc(sem, 16)
        ev = (key, sem, Q.dvals[i], None)
        s._post(ev, reads, writes)
        return ev
    def barrier(s):
        for E in s.engs:
            for Fg in s.engs:
                if Fg is E or Fg.cnt == 0: continue
                if E.known.get(Fg.name, 0) < Fg.cnt:
                    s._wait(E, Fg.name, Fg.sem, Fg.cnt)
            for Q in s.engs:
                for i, sem in enumerate(Q.dsems):
                    key = f"d_{Q.name}_{i}"
                    if Q.dvals[i] > 0 and E.known.get(key, 0) < Q.dvals[i]:
                        E.eng.wait_ge(sem, Q.dvals[i]); E.known[key] = Q.dvals[i]


def build(S, NSEQ, DEPTH, dbg=None, parts=None):
    _, k1 = _build(S, NSEQ, DEPTH, dbg, parts, None)
    nc, k2 = _build(S, NSEQ, DEPTH, dbg, parts, set(k1.waited))
    return nc


def _build(S, NSEQ, DEPTH, dbg, parts, needed):
    nc = bass.Bass("TRN2", target_bir_lowering=False)
    k = K(nc, needed=needed)
    if parts is None: parts = {"setup", "prenorm", "proj", "sb", "s5", "ret", "lru", "O"}
    PE, ACT, DVE, POOL, SP = k.PE, k.ACT, k.DVE, k.POOL, k.SP
    NCH = S // TC
    NBLK = S // 128
    L = DEPTH
    def din(name, shape): return nc.dram_tensor(name, list(shape), F32, kind="ExternalInput").ap()
    x_d = din("x", [NSEQ, S, D]); win_d = din("win", [L, D, 7680]); wbr_d = din("wbr", [L, 4, 256, D])
    wout_d = din("wout", [L, D, D]); wglu_d = din("wglu", [L, 256, 256]); pp_d = din("pp", [L, 128, 40])
    lruw_d = din("lruw", [L, 2, 128, 2, 128]); s5row_d = din("s5row", [L, 3, 1024]); s5col_d = din("s5col", [L, 128, 3, 16])
    bblk_d = din("bblk", [L, 2, 128, 2, 512]); cab_d = din("cab", [L, 128, 32, 128])
    ropeC_d = din("ropeC", [128, S]); ropeS_d = din("ropeS", [128, S])
    dtab_d = din("dtab", [128, 2, 256]); ztab_d = din("ztab", [128, 256]); xitab_d = din("xitab", [128, 2, 128])
    gdec_d = din("gdec", [128, 2]); cm_d = din("cmats", [128, 8, 128]); misc_d = din("misc", [128, 132])
    y_d = nc.dram_tensor("y", [NSEQ, S, D], F32, kind="ExternalOutput").ap()
    dbg_d = {}
    if dbg:
        for nm, shp in dbg.items():
            dbg_d[nm] = nc.dram_tensor(nm, list(shp), F32, kind="ExternalOutput").ap()

    xTb = nc.alloc_sbuf_tensor("s_xT", [128, 8, S], F32).ap()
    xTc = [T(xTb[:, :, c * TC:(c + 1) * TC]) for c in range(S // TC)]
    hT = k.sb([128, 8, TC], BF16, "hT")
    WT = [k.sb([128, 8, 128], BF16, f"WT{i}") for i in range(5)]
    wtn = [0]
    pp = k.sb([128, 40], F32, "pp")
    cmats = k.sb([128, 8, 128], BF16, "cmats")
    identb = cmats[:, 0, :]; ntri = cmats[:, 1, :]; nones = cmats[:, 2, :]; tri = cmats[:, 3, :]
    Jm = cmats[:, 5, :]; onesb = cmats[:, 6, :]
    identf = k.sb([128, 128], F32, "identf")
    misc = k.sb([128, 132], F32, "misc")
    mask2 = k.sb([128, 256], BF16, "mask2")
    zrow = k.sb([1, 256], BF16, "zrow")
    ropeC = k.sb([128, TC], BF16, "ropeC"); ropeS = k.sb([128, TC], BF16, "ropeS")
    dtab = k.sb([128, 2, 256], BF16, "dtab"); ztab = k.sb([128, 256], BF16, "ztab"); xitab = k.sb([128, 2, 128], BF16, "xitab")
    gdec = k.sb([128, 2], F32, "gdec")
    KT = k.sb([128, 2, S], BF16, "KT"); Vc = k.sb([128, NBLK, 256], BF16, "Vc"); QT = k.sb([128, 2, TC], BF16, "QT")
    sbE = [k.sb([128, 256], F32, f"sbE{i}") for i in range(2)]
    sbSP = [k.sb([128, 256], BF16, f"sbSP{i}") for i in range(2)]
    sbW = [k.sb([128, 256], BF16, f"sbW{i}") for i in range(2)]
    sbC = [k.sb([128, 256], BF16, f"sbC{i}") for i in range(2)]
    osb = k.sb([128, 256], BF16, "osb")
    uT = k.sb([128, 2, TC], BF16, "uT")
    Bx = [k.sb([128, 1024], BF16, f"Bx{i}") for i in range(2)]; Bsw = [k.sb([128, 1024], BF16, f"Bsw{i}") for i in range(2)]
    Pr = k.sb([128, 16, 64], BF16, "Pr"); Pi = k.sb([128, 16, 64], BF16, "Pi")
    T1 = k.sb([128, 16, 128], BF16, "T1"); T2 = k.sb([128, 16, 128], BF16, "T2")
    W1 = k.sb([128, 16], F32, "W1"); W2 = k.sb([128, 16], F32, "W2")
    Cm = k.sb([128, 32, 128], BF16, "Cm")
    G1 = [k.sb([128, 4, 128], BF16, f"G1_{i}") for i in range(2)]; G2 = [k.sb([128, 4, 128], BF16, f"G2_{i}") for i in range(2)]
    A1 = [k.sb([128, 4, 128], BF16, f"A1_{i}") for i in range(2)]; A2 = [k.sb([128, 4, 128], BF16, f"A2_{i}") for i in range(2)]
    carry = [k.sb([128, 4], F32, f"carry{i}") for i in range(4)]
    sfull = k.sb([128, 4], F32, "sfull"); U1 = k.sb([128, 4], BF16, "U1"); U2 = k.sb([128, 4], BF16, "U2")
    ygb = k.sb([128, 2, TC], BF16, "ygb"); wglu = k.sb([128, 2, 256], BF16, "wglu")
    uL = k.sb([128, 2, TC + 4], F32, "uL"); hst = k.sb([128, 2], F32, "hst")
    WA = k.sb([128, 2, 128], BF16, "WA"); WX = k.sb([128, 2, 128], BF16, "WX")
    cl = k.sb([128, 2], F32, "cl"); cl2 = k.sb([128, 2], F32, "cl2"); cltmp = k.sb([128, 2], F32, "cltmp")
    Rst = k.sb([128, 2, 128], F32, "Rst"); Rb = k.sb([128, 2, 128], BF16, "Rb")
    st4 = [k.sb([128, 4], F32, f"st4_{i}") for i in range(6)]
    yTb = nc.alloc_sbuf_tensor("s_yT", [128, 8, TC], BF16).ap()
    yTt = [T(yTb[:, i, :]) for i in range(8)]
    WB = [k.sb([128, 8, 128], BF16, f"WB{i}") for i in range(2)]
    colp = k.sb([128, 3, 16], F32, "colp"); colq = k.sb([128, 3, 16], F32, "colq")
    UN = nc.alloc_sbuf_tensor("UN", [128, 7168], F32).ap()
    def carve(off_f32, n_f32, dt, shape):
        v = UN[:, off_f32:off_f32 + n_f32]
        if dt is BF16: v = v.bitcast(BF16)
        if len(shape) == 3: v = v.rearrange("p (a b) -> p a b", a=shape[1])
        return T(v)
    LT = [carve(512 * i, 512, F32, [128, 512]) for i in range(5)]
    xcb = carve(2560, 256, BF16, [128, 512])
    qrot = carve(2816, 512, BF16, [128, 2, 512]); krot = carve(3328, 512, BF16, [128, 2, 512])
    qxi = carve(3840, 512, BF16, [128, 2, 512])
    vt = carve(4352, 512, BF16, [128, 4, 256]); kz = carve(4864, 128, BF16, [128, 256])
    PTt = carve(4992, 256, BF16, [128, 2, 256])
    osbf = carve(5248, 256, F32, [128, 256]); osq = carve(5504, 256, F32, [128, 256]); onb = carve(5760, 128, BF16, [128, 256])
    rt1 = carve(5888, 512, F32, [128, 512]); rt2 = carve(6400, 512, F32, [128, 512])
    merged = carve(0, 2048, BF16, [128, 8, 512])
    prodS = [carve(2048 + 256 * i, 256, BF16, [128, 512]) for i in range(4)]
    sg = [carve(3072, 512, F32, [128, 512]), carve(3584, 512, F32, [128, 512])]
    outT = carve(4096, 2048, F32, [128, 8, 256])
    stg = [carve(4096, 1024, F32, [128, 1024]), carve(5120, 1024, F32, [128, 1024])]
    sqb = [carve(6144, 256, BF16, [128, 512]), carve(6400, 256, BF16, [128, 512])]
    rstd = carve(6656, 512, F32, [128, 512])
    SRt = [carve(512 * i, 512, F32, [128, 512]) for i in range(14)]
    B = [k.ps([128, 512], F32, f"bank{i}") for i in range(8)]

    def mm(out_t, out_ap, lhsT, rhs, start, stop, reads):
        k.op(PE, lambda e: e.matmul(out_ap, lhsT=lhsT, rhs=rhs, start=start, stop=stop), reads=reads, writes=[out_t])
    def act(out_t, out_ap, in_ap, func, reads, bias=None, scale=None):
        kw = {}
        if bias is not None: kw['bias'] = bias
        if scale is not None: kw['scale'] = scale
        k.op(ACT, lambda e: e.activation(out=out_ap, in_=in_ap, func=func, **kw), reads=reads, writes=[out_t])
    def tt(E, out_t, out_ap, a, b, op, reads):
        k.op(E, lambda e: e.tensor_tensor(out=out_ap, in0=a, in1=b, op=op), reads=reads, writes=[out_t])
    def ts(E, out_t, out_ap, a, s1, s2, op0, op1, reads):
        if op1 is None:
            k.op(E, lambda e: e.tensor_scalar(out=out_ap, in0=a, scalar1=s1, scalar2=None, op0=op0), reads=reads, writes=[out_t])
        else:
            k.op(E, lambda e: e.tensor_scalar(out=out_ap, in0=a, scalar1=s1, scalar2=s2, op0=op0, op1=op1), reads=reads, writes=[out_t])
    def stt(out_t, out_ap, a, sc, b, op0, op1, reads):
        k.op(DVE, lambda e: e.scalar_tensor_tensor(out=out_ap, in0=a, scalar=sc, in1=b, op0=op0, op1=op1), reads=reads, writes=[out_t])
    def cp(E, out_t, out_ap, in_ap, reads):
        if E is ACT:
            k.op(E, lambda e: e.copy(out=out_ap, in_=in_ap), reads=reads, writes=[out_t])
        else:
            k.op(E, lambda e: e.tensor_copy(out=out_ap, in_=in_ap), reads=reads, writes=[out_t])
    def memset(E, t, ap, v):
        k.op(E, lambda e: e.memset(ap, v), writes=[t])
    def dbg_out(name, t, ap):
        if name in dbg_d:
            k.dma(SP, dbg_d[name], ap, reads=[t])

    def load_w(src_ap):
        w = WT[wtn[0] % 5]; wtn[0] += 1
        k.dma(POOL, w[:], src_ap, writes=[w])
        return w
    def win_tile(l, ct):
        return win_d[l].rearrange("(k p) c -> p k c", p=128)[:, :, ct * 128:(ct + 1) * 128]
    pbn = [0]
    def proj_fm(l, ct, bank=None):
        w = load_w(win_tile(l, ct))
        if bank is None:
            b = B[pbn[0] % 2]; pbn[0] += 1
        else:
            b = bank
        for kk in range(8):
            mm(b, b[:], w[:, kk, :], hT[:, kk, :], kk == 0, kk == 7, [w, hT])
        return b
    def proj_tm(l, ct0, dst_t, dst_fn, banks=None):
        w0 = load_w(win_tile(l, ct0)); w1 = load_w(win_tile(l, ct0 + 1))
        for half in range(2):
            if banks is None:
                b = B[pbn[0] % 2]; pbn[0] += 1
            else:
                b = banks[half % len(banks)]
            for bi in range(2):
                blk = half * 2 + bi
                for j, w in enumerate((w0, w1)):
                    o = bi * 256 + j * 128
                    for kk in range(8):
                        mm(b, b[:, o:o + 128], hT[:, kk, blk * 128:(blk + 1) * 128], w[:, kk, :], kk == 0, kk == 7, [w, hT])
            for bi in range(2):
                blk = half * 2 + bi
                cp(ACT, dst_t, dst_fn(blk), b[:, bi * 256:(bi + 1) * 256], [b])

    def sincos(ang, vw, tmps):
        ki, kr, sn, cs = tmps
        kiv = vw(ki).bitcast(I32)
        ts(DVE, kr, vw(kr), vw(ang), 1.0 / (2 * PI), None, ALU.mult, None, [ang])
        cp(DVE, ki, kiv, vw(kr), [kr])
        cp(DVE, kr, vw(kr), kiv, [ki])
        C1 = 6.28125; C2 = 2 * PI - 6.28125
        stt(sn, vw(sn), vw(kr), -C1, vw(ang), ALU.mult, ALU.add, [kr, ang])
        stt(sn, vw(sn), vw(kr), -C2, vw(sn), ALU.mult, ALU.add, [kr, sn])
        ts(DVE, sn, vw(sn), vw(sn), -PI, PI, ALU.max, ALU.min, [sn])
        ts(DVE, cs, vw(cs), vw(sn), PI / 2, -2 * PI, ALU.is_gt, ALU.mult, [sn])
        stt(cs, vw(cs), vw(sn), PI / 2, vw(cs), ALU.add, ALU.add, [sn, cs])
        ts(DVE, cs, vw(cs), vw(cs), -PI, PI, ALU.max, ALU.min, [cs])
        act(cs, vw(cs), vw(cs), AF.Sin, [cs])
        act(sn, vw(sn), vw(sn), AF.Sin, [sn])

    k.dma(POOL, cmats[:], cm_d, writes=[cmats])
    k.dma(SP, misc[:], misc_d, writes=[misc])
    k.dma(POOL, dtab[:], dtab_d, writes=[dtab]); k.dma(POOL, ztab[:], ztab_d, writes=[ztab]); k.dma(POOL, xitab[:], xitab_d, writes=[xitab])
    k.dma(SP, gdec[:], gdec_d, writes=[gdec])
    memset(POOL, zrow, zrow[:], 0.0)
    for i in range(2):
        cp(POOL, mask2, mask2[:, i * 128:(i + 1) * 128], cmats[:, 4, :], [cmats])
    cp(DVE, identf, identf[:], cmats[:, 0, :], [cmats])
    tauc = misc[:, 0:1]; sgn1 = misc[:, 1:2]; eps_ap = misc[:, 2:3]; one_ap = misc[:, 3:4]; taurow = misc[:, 4:132]


    flat = lambda t: t.ap
    v3 = lambda t: t.ap.rearrange("p (g n) -> p g n", g=8)
    v4 = lambda t: t.ap.rearrange("p (g n) -> p g n", g=4)
    sm = lambda t: t.ap[:, 0:16]
    MUL, ADD, SUB = ALU.mult, ALU.add, ALU.subtract

    def recip(out_t, out_ap, in_ap, reads):
        k.op(DVE, lambda e: e.reciprocal(out=out_ap, in_=in_ap), reads=reads, writes=[out_t])
    def transpose(out_t, out_ap, in_ap, ident_ap, reads):
        k.op(PE, lambda e: e.transpose(out=out_ap, in_=in_ap, identity=ident_ap), reads=reads, writes=[out_t])

    def layer_setup(l):
        k.dma(SP, pp[:], pp_d[l], writes=[pp])
        k.dma(POOL, WA[:], lruw_d[l, 0], writes=[WA]); k.dma(POOL, WX[:], lruw_d[l, 1], writes=[WX])
        k.dma(POOL, wglu[:], wglu_d[l].rearrange("(k p) c -> p k c", p=128), writes=[wglu])
        k.dma(POOL, Cm[:], cab_d[l], writes=[Cm])
        k.dma(SP, colp[:], s5col_d[l], writes=[colp])
        act(cltmp, cltmp[:], pp[:, 34:36], AF.Exp, [pp], scale=-1.0)
        act(cltmp, cltmp[:], cltmp[:], AF.Ln, [cltmp, misc], bias=one_ap)
        ts(DVE, cl, cl[:], cltmp[:], -8.0, None, MUL, None, [cltmp])
        ts(DVE, cl2, cl2[:], cltmp[:], -16.0, None, MUL, None, [cltmp])
        memset(POOL, uL, uL[:, :, 0:4], 0.0); memset(POOL, hst, hst[:], 0.0)
        memset(POOL, Rst, Rst[:], 0.0); memset(POOL, Rb, Rb[:], 0.0)
        for q in range(4): memset(POOL, carry[q], carry[q][:], 0.0)
        act(colq, colq[:, 0, :], colp[:, 2, :], AF.Exp, [colp])
        tt(DVE, colq, colq[:, 1, :], colq[:, 0, :], colp[:, 0, :], MUL, [colq, colp])
        tt(DVE, colq, colq[:, 2, :], colq[:, 0, :], colp[:, 1, :], MUL, [colq, colp])
        for kh in range(2):
            are, aim, dt_, dre, dim_, mag, ki, kr, sn, cs, fr, fi, bre, bim = SRt
            for i, t in enumerate((are, aim, dt_)):
                k.dma(SP, t[:], bass.AP(s5row_d.tensor, (l * 3 + i) * 1024 + kh * 512, [[0, 128], [1, 512]]), writes=[t])
            k.dma(SP, bre[:], bblk_d[l, 0][:, kh, :], writes=[bre]); k.dma(SP, bim[:], bblk_d[l, 1][:, kh, :], writes=[bim])
            act(dt_, dt_[:], dt_[:], AF.Exp, [dt_])
            tt(DVE, dre, dre[:], dt_[:], are[:], MUL, [dt_, are]); tt(DVE, dim_, dim_[:], dt_[:], aim[:], MUL, [dt_, aim])
            act(mag, mag[:], dre[:], AF.Exp, [dre])
            sincos(dim_, flat, (ki, kr, sn, cs))
            tt(DVE, cs, cs[:], mag[:], cs[:], MUL, [mag, cs]); tt(DVE, sn, sn[:], mag[:], sn[:], MUL, [mag, sn])
            ts(DVE, cs, cs[:], cs[:], -1.0, None, ADD, None, [cs])
            tt(DVE, mag, mag[:], are[:], are[:], MUL, [are]); tt(DVE, ki, ki[:], aim[:], aim[:], MUL, [aim])
            tt(DVE, mag, mag[:], mag[:], ki[:], ADD, [mag, ki]); recip(mag, mag[:], mag[:], [mag])
            tt(DVE, fr, fr[:], cs[:], are[:], MUL, [cs, are]); tt(DVE, ki, ki[:], sn[:], aim[:], MUL, [sn, aim])
            tt(DVE, fr, fr[:], fr[:], ki[:], ADD, [fr, ki]); tt(DVE, fr, fr[:], fr[:], mag[:], MUL, [fr, mag])
            tt(DVE, fi, fi[:], sn[:], are[:], MUL, [sn, are]); tt(DVE, ki, ki[:], cs[:], aim[:], MUL, [cs, aim])
            tt(DVE, fi, fi[:], fi[:], ki[:], SUB, [fi, ki]); tt(DVE, fi, fi[:], fi[:], mag[:], MUL, [fi, mag])
            tt(DVE, kr, kr[:], fr[:], bre[:], MUL, [fr, bre]); tt(DVE, ki, ki[:], fi[:], bim[:], MUL, [fi, bim])
            tt(DVE, sn, sn[:], fr[:], bim[:], MUL, [fr, bim]); tt(DVE, cs, cs[:], fi[:], bre[:], MUL, [fi, bre])
            bxv = Bx[kh].ap.rearrange("p (g a n) -> p g a n", g=8, a=2)
            bsv = Bsw[kh].ap.rearrange("p (g a n) -> p g a n", g=8, a=2)
            tt(DVE, Bx[kh], bxv[:, :, 0, :], v3(kr), v3(ki), SUB, [kr, ki])
            tt(DVE, Bsw[kh], bsv[:, :, 1, :], v3(kr), v3(ki), SUB, [kr, ki])
            tt(DVE, Bx[kh], bxv[:, :, 1, :], v3(sn), v3(cs), ADD, [sn, cs])
            stt(Bsw[kh], bsv[:, :, 0, :], v3(sn), -1.0, v3(cs), MUL, SUB, [sn, cs])
            ts(DVE, mag, mag[:], dim_[:], tauc, None, MUL, None, [dim_, misc])
            sincos(mag, flat, (ki, kr, sn, cs))
            ts(DVE, fr, fr[:], dre[:], tauc, None, MUL, None, [dre, misc]); act(fr, fr[:], fr[:], AF.Exp, [fr], scale=-1.0)
            tt(DVE, Pr, Pr[:, 8 * kh:8 * kh + 8, :], v3(fr), v3(cs), MUL, [fr, cs])
            stt(Pi, Pi[:, 8 * kh:8 * kh + 8, :], v3(fr), -1.0, v3(sn), MUL, MUL, [fr, sn])
        for q in range(4):
            ang, mexp, ki, kr, sn, cs = SRt[0:6]
            taub = taurow.unsqueeze(1).to_broadcast([128, 4, 128])
            dimb = colq[:, 2, 4 * q:4 * q + 4].unsqueeze(2).to_broadcast([128, 4, 128])
            dreb = colq[:, 1, 4 * q:4 * q + 4].unsqueeze(2).to_broadcast([128, 4, 128])
            tt(DVE, ang, v4(ang), dimb, taub, MUL, [colq, misc])
            tt(DVE, mexp, v4(mexp), dreb, taub, MUL, [colq, misc]); act(mexp, mexp[:], mexp[:], AF.Exp, [mexp])
            sincos(ang, flat, (ki, kr, sn, cs))
            stt(T1, T1[:, 4 * q:4 * q + 4, :], v4(mexp), sgn1, v4(cs), MUL, MUL, [mexp, misc, cs])
            stt(T2, T2[:, 4 * q:4 * q + 4, :], v4(mexp), -1.0, v4(sn), MUL, MUL, [mexp, sn])
        angw, mw, ki, kr, sn, cs = SRt[6:12]
        ts(DVE, angw, sm(angw), colq[:, 2, :], 128.0, None, MUL, None, [colq])
        ts(DVE, mw, sm(mw), colq[:, 1, :], 128.0, None, MUL, None, [colq]); act(mw, sm(mw), sm(mw), AF.Exp, [mw])
        sincos(angw, sm, (ki, kr, sn, cs))
        tt(DVE, W1, W1[:], sm(mw), sm(cs), MUL, [mw, cs]); tt(DVE, W2, W2[:], sm(mw), sm(sn), MUL, [mw, sn])

    def load_seq(sq):
        for blk in range(NBLK):
            st = stg[blk % 2]; c = blk // 4; o = (blk % 4) * 128
            k.dma(SP, st[:], x_d[sq, blk * 128:(blk + 1) * 128, :], writes=[st])
            for half in range(2):
                b = B[2 + half]
                for j in range(4):
                    f = half * 4 + j
                    transpose(b, b[:, j * 128:(j + 1) * 128], st[:, f * 128:(f + 1) * 128], identf[:], [st, identf])
                cp(DVE if half == 0 else ACT, xTc[c], xTc[c][:, half * 4:(half + 1) * 4, o:o + 128],
                   b[:].rearrange("p (a b) -> p a b", a=4), [b])

    def store_seq(sq):
        for blk in range(NBLK):
            st = stg[blk % 2]; c = blk // 4; o = (blk % 4) * 128
            for half in range(2):
                b = B[2 + half]
                for j in range(4):
                    f = half * 4 + j
                    transpose(b, b[:, j * 128:(j + 1) * 128], xTc[c][:, f, o:o + 128], identf[:], [xTc[c], identf])
                cp(DVE if half == 0 else ACT, st, st[:, half * 512:(half + 1) * 512], b[:], [b])
            k.dma(SP, y_d[sq, blk * 128:(blk + 1) * 128, :], st[:], reads=[st])

    def prenorm(l, c):
        xc_ = xTc[c]; ssb = B[5]
        for f in range(8):
            s_ = sqb[f % 2]
            act(s_, s_[:], xc_[:, f, :], AF.Square, [xc_])
            mm(ssb, ssb[:], onesb, s_[:], f == 0, f == 7, [cmats, s_])
        act(sg[0], sg[0][:], ssb[:], AF.Sqrt, [ssb, misc], bias=eps_ap, scale=1.0 / D)
        recip(rstd, rstd[:], sg[0][:], [sg[0]])
        for f in range(8):
            stt(hT, hT[:, f, :], xc_[:, f, :], pp[:, f:f + 1], rstd[:], MUL, MUL, [xc_, pp, rstd])

    def phaseM(l, c):
        t0 = c * TC
        k.dma(POOL, ropeC[:], ropeC_d[:, t0:t0 + TC], writes=[ropeC]); k.dma(POOL, ropeS[:], ropeS_d[:, t0:t0 + TC], writes=[ropeS])
        def sec_sb():
            Z = [B[2], B[3]]; PO = B[4]; PTb = B[5]; pv = PTb.ap.bitcast(BF16)
            for i in range(2):
                b = proj_fm(l, i, B[2 + i]); act(QT, QT[:, i, :], b[:], AF.Copy, [b], scale=0.125)
            for i in range(2):
                b = proj_fm(l, 2 + i, B[2 + i]); cp(DVE, KT, KT[:, i, t0:t0 + TC], b[:], [b])
            yield
            proj_tm(l, 4, Vc, lambda blk: Vc[:, c * 4 + blk, :], [B[2], B[3]])
            yield
            units = []
            for qi in range(4):
                qb = c * 4 + qi
                for a in range(qb, -1, -1):
                    units.append((qi, qb, a))
            n_u = len(units)
            def S1(u, par):
                qi, qb, a = u; z = Z[par]
                mm(z, z[:, 0:256], zrow[0:1, 0:128], zrow[0:1, 0:256], True, False, [zrow])
                for ti in range(2):
                    mm(z, z[:, ti * 128:(ti + 1) * 128], KT[64 * par:64 * par + 64, ti, a * 128:(a + 1) * 128],
                       QT[64 * par:64 * par + 64, ti, qi * 128:(qi + 1) * 128], False, False, [KT, QT])
            def S2(u, par):
                qi, qb, a = u; diag = (a == qb); z = Z[par]; e_ = sbE[par]; sp_ = sbSP[par]
                act(e_, e_[:], z[:, 0:256], AF.Exp, [z])
                act(sp_, sp_[:], e_[:], AF.Ln, [e_, misc], bias=one_ap)
                if diag: tt(POOL, sp_, sp_[:], sp_[:], mask2[:], MUL, [sp_, mask2])
            def S3(u, par):
                qi, qb, a = u; diag = (a == qb); z = Z[par]; sp_ = sbSP[par]; c_ = sbC[par]
                mm(z, z[:, 0:256], ntri, sp_[:], False, diag, [cmats, sp_])
                if not diag: mm(z, z[:, 0:256], nones, c_[:], False, True, [cmats, c_])
            def S4(u, par):
                qi, qb, a = u; diag = (a == qb); z = Z[par]; sp_ = sbSP[par]; w_ = sbW[par]; c_ = sbC[par]
                act(w_, w_[:], z[:, 0:256], AF.Exp, [z])
                if diag: tt(POOL, w_, w_[:], w_[:], mask2[:], MUL, [w_, mask2])
                if a > 0:
                    if diag: cp(POOL, c_, c_[:], sp_[:], [sp_])
                    else: tt(POOL, c_, c_[:], c_[:], sp_[:], ADD, [c_, sp_])
            def S5(u, par):
                qi, qb, a = u; w_ = sbW[par]
                if a == qb and par == 0:
                    mm(PO, PO[:, 0:256], zrow[0:1, 0:128], zrow[0:1, 0:256], True, False, [zrow])
                for ti in range(2):
                    h = 2 * ti + par
                    mm(PO, PO[:, h * 64:(h + 1) * 64], w_[:, ti * 128:(ti + 1) * 128], Vc[:, a, h * 64:(h + 1) * 64],
                       False, (a == 0 and par == 1 and ti == 1), [w_, Vc])
                if a == 0 and par == 1:
                    cp(ACT, osb, osb[:], PO[:, 0:256], [PO])
                    for ti in range(2):
                        transpose(PTb, pv[:, ti * 128:(ti + 1) * 128], osb[:, ti * 128:(ti + 1) * 128], identb, [osb, cmats])
                    cp(DVE, yTt[0], yTb[:, 0:2, qi * 128:(qi + 1) * 128], pv[:, 0:256].rearrange("p (a b) -> p a b", a=2), [PTb])
                    yTt[1].w = yTt[0].w; yTt[1].r = {}
            sched = {}
            for i in range(n_u):
                for par in range(2):
                    base = 4 * i + par
                    for off, fn, ispe in ((0, S1, 1), (1, S2, 0), (2, S3, 1), (3, S4, 0), (5, S5, 1)):
                        sched.setdefault(base + off, []).append((1 - ispe, i, par, fn))
            for tk in sorted(sched):
                for _, i, par, fn in sorted(sched[tk], key=lambda x: (x[0], x[1], x[2])):
                    fn(units[i], par)
                yield

        def sec_s5():
            Yb = B[1]
            for i in range(2):
                b = proj_fm(l, 6 + i, B[6 + i]); cp(ACT, uT, uT[:, i, :], b[:], [b])
            yield
            x4 = lambda b: b.ap.rearrange("p (g a n) -> p g a n", g=4, a=2)
            g4 = lambda t: t.ap.rearrange("p g (a n) -> p g a n", a=2)
            for sc in range(4):
                for q in range(4):
                    kh = q // 2; hq = q % 2; st_ = (sc * 4 + q) % 2
                    mm(B[6], B[6][:], uT[:, kh, sc * 128:(sc + 1) * 128], Bx[kh][:, hq * 512:(hq + 1) * 512], True, True, [uT, Bx[kh]])
                    mm(B[7], B[7][:], uT[:, kh, sc * 128:(sc + 1) * 128], Bsw[kh][:, hq * 512:(hq + 1) * 512], True, True, [uT, Bsw[kh]])
                    yield
                    prb = Pr[:, 4 * q:4 * q + 4, :].unsqueeze(2).to_broadcast([128, 4, 2, 64])
                    pib = Pi[:, 4 * q:4 * q + 4, :].unsqueeze(2).to_broadcast([128, 4, 2, 64])
                    tt(DVE, G1[st_], g4(G1[st_]), x4(B[6]), prb, MUL, [B[6], Pr])
                    tt(DVE, G2[st_], g4(G2[st_]), x4(B[7]), pib, MUL, [B[7], Pi])
                    yield
                    for gl in range(4):
                        mm(B[0], B[0][:, gl * 128:(gl + 1) * 128], G1[st_][:, gl, :], tri, True, False, [G1[st_], cmats])
                        mm(B[0], B[0][:, gl * 128:(gl + 1) * 128], G2[st_][:, gl, :], tri, False, True, [G2[st_], cmats])
                    yield
                    for gl in range(4):
                        g = 4 * q + gl
                        stt(A1[st_], A1[st_][:, gl, :], B[0][:, gl * 128:(gl + 1) * 128], carry[q][:, gl:gl + 1], T1[:, g, :], ADD, MUL, [B[0], carry[q], T1])
                        stt(A2[st_], A2[st_][:, gl, :], B[0][:, gl * 128:(gl + 1) * 128], carry[q][:, gl:gl + 1], T2[:, g, :], ADD, MUL, [B[0], carry[q], T2])
                    slast = B[0][:].rearrange("p (g t) -> p g t", g=4)[:, :, 127:128].rearrange("p g o -> p (g o)")
                    tt(DVE, sfull, sfull[:], slast, carry[q][:], ADD, [B[0], carry[q]])
                    tt(DVE, U1, U1[:], sfull[:], W1[:, 4 * q:4 * q + 4], MUL, [sfull, W1])
                    tt(DVE, U2, U2[:], sfull[:], W2[:, 4 * q:4 * q + 4], MUL, [sfull, W2])
                    yield
                    yb = Yb
                    for gl in range(4):
                        g = 4 * q + gl
                        mm(yb, yb[:, kh * 128:(kh + 1) * 128], Cm[:, 2 * g, :], A1[st_][:, gl, :], (hq == 0 and gl == 0), False, [Cm, A1[st_]])
                        mm(yb, yb[:, kh * 128:(kh + 1) * 128], Cm[:, 2 * g + 1, :], A2[st_][:, gl, :], False, (hq == 1 and gl == 3), [Cm, A2[st_]])
                    if hq == 1:
                        cp(ACT, LT[kh], LT[kh][:, sc * 128:(sc + 1) * 128], yb[:, kh * 128:(kh + 1) * 128], [yb])
                    cb = B[6]
                    mm(cb, cb[:, 0:4], identb, U1[:], True, False, [cmats, U1]); mm(cb, cb[:, 0:4], Jm, U2[:], False, True, [cmats, U2])
                    yield
                    cp(ACT, carry[q], carry[q][:], cb[:, 0:4], [cb])
                    yield
            for kh in range(2):
                yv = LT[kh]; x2 = LT[2]
                stt(yv, yv[:], uT[:, kh, :], pp[:, 16 + kh:17 + kh], yv[:], MUL, ADD, [uT, pp, yv])
                act(x2, x2[:], yv[:], AF.Square, [yv])
                ts(DVE, x2, x2[:], x2[:], 0.044715, 1.0, MUL, ADD, [x2])
                tt(DVE, x2, x2[:], x2[:], yv[:], MUL, [x2, yv])
                act(x2, x2[:], x2[:], AF.Sigmoid, [x2], scale=1.5957691216057308)
                tt(DVE, yv, yv[:], yv[:], x2[:], MUL, [yv, x2])
                cp(POOL, ygb, ygb[:, kh, :], yv[:], [yv])
            for e in range(2):
                b = B[6 + e]
                for kh in range(2):
                    mm(b, b[:], wglu[:, kh, e * 128:(e + 1) * 128], ygb[:, kh, :], kh == 0, kh == 1, [wglu, ygb])
                act(LT[2 + e], LT[2 + e][:], b[:], AF.Sigmoid, [b, pp], bias=pp[:, 18 + e:19 + e])
                tt(DVE, yTt[2 + e], yTb[:, 2 + e, :], LT[e][:], LT[2 + e][:], MUL, [LT[e], LT[2 + e]])


        def sec_ret():
            for ti in range(2):
                tt(DVE, qxi, qxi[:, ti, :].rearrange("p (n i) -> p n i", n=4), qrot[:, ti, :].rearrange("p (n i) -> p n i", n=4),
                   xitab[:, ti, :].unsqueeze(1).to_broadcast([128, 4, 128]), MUL, [qrot, xitab])
            pv2 = B[6].ap.bitcast(BF16); pv = B[5].ap.bitcast(BF16)
            for n in range(4):
                SX = [B[2], B[3]]
                for par in range(2):
                    for ti in range(2):
                        mm(SX[par], SX[par][:, ti * 128:(ti + 1) * 128], krot[64 * par:64 * par + 64, ti, n * 128:(n + 1) * 128],
                           qrot[64 * par:64 * par + 64, ti, n * 128:(n + 1) * 128], True, True, [krot, qrot])
                    tt(DVE, PTt, PTt[:, par, :], SX[par][:, 0:256], dtab[:, par, :], MUL, [SX[par], dtab])
                po = B[4]
                mm(po, po[:, 0:256], zrow[0:1, 0:128], zrow[0:1, 0:256], True, False, [zrow])
                for par in range(2):
                    for ti in range(2):
                        h = 2 * ti + par
                        mm(po, po[:, h * 64:(h + 1) * 64], PTt[:, par, ti * 128:(ti + 1) * 128], vt[:, n, h * 64:(h + 1) * 64], False, False, [PTt, vt])
                for ti in range(2):
                    mm(po, po[:, ti * 128:(ti + 1) * 128], qxi[:, ti, n * 128:(n + 1) * 128], Rb[:, ti, :], False, ti == 1, [qxi, Rb])
                if 'ret1' in parts: continue
                cp(ACT, osbf, osbf[:], po[:, 0:256], [po])
                o3 = osbf.ap.rearrange("p (h e) -> p h e", h=4); q3 = osq.ap.rearrange("p (h e) -> p h e", h=4)
                k.op(DVE, lambda e: e.tensor_reduce(out=st4[0][:], in_=o3, axis=AX.X, op=ADD), reads=[osbf], writes=[st4[0]])
                act(osq, osq[:], osbf[:], AF.Square, [osbf])
                k.op(DVE, lambda e: e.tensor_reduce(out=st4[1][:], in_=q3, axis=AX.X, op=ADD), reads=[osq], writes=[st4[1]])
                ts(DVE, st4[2], st4[2][:], st4[0][:], 1.0 / 64, None, MUL, None, [st4[0]])
                tt(DVE, st4[3], st4[3][:], st4[2][:], st4[2][:], MUL, [st4[2]])
                stt(st4[3], st4[3][:], st4[1][:], 1.0 / 64, st4[3][:], MUL, SUB, [st4[1], st4[3]])
                act(st4[4], st4[4][:], st4[3][:], AF.Sqrt, [st4[3], misc], bias=eps_ap)
                recip(st4[5], st4[5][:], st4[4][:], [st4[4]])
                for h in range(4):
                    ts(DVE, onb, onb[:, h * 64:(h + 1) * 64], osbf[:, h * 64:(h + 1) * 64], st4[2][:, h:h + 1], st4[5][:, h:h + 1], SUB, MUL, [osbf, st4[2], st4[5]])
                if 'ret2' in parts: continue
                for ti in range(2):
                    transpose(B[5], pv[:, ti * 128:(ti + 1) * 128], onb[:, ti * 128:(ti + 1) * 128], identb, [onb, cmats])
                cp(ACT, yTt[4], yTb[:, 4:6, n * 128:(n + 1) * 128], pv[:, 0:256].rearrange("p (a b) -> p a b", a=2), [B[5]])
                yTt[5].w = yTt[4].w; yTt[5].r = {}
                if 'ret3' in parts: continue
                for ti in range(2):
                    transpose(B[6], pv2[:, ti * 128:(ti + 1) * 128], krot[:, ti, n * 128:(n + 1) * 128], identb, [krot, cmats])
                tt(DVE, kz, kz[:], pv2[:, 0:256], ztab[:], MUL, [B[6], ztab])
                kvb = B[7]
                for ti in range(2):
                    mm(kvb, kvb[:, ti * 128:(ti + 1) * 128], kz[:, ti * 128:(ti + 1) * 128], vt[:, n, ti * 128:(ti + 1) * 128], True, True, [kz, vt])
                tt(DVE, osq, osq.ap.rearrange("p (a b) -> p a b", a=2), kvb[:, 0:256].rearrange("p (a b) -> p a b", a=2),
                   cmats[:, 7:8, :].to_broadcast([128, 2, 128]), MUL, [kvb, cmats])
                for ti in range(2):
                    stt(Rst, Rst[:, ti, :], Rst[:, ti, :], gdec[:, ti:ti + 1], osq[:, ti * 128:(ti + 1) * 128], MUL, ADD, [Rst, gdec, osq])
                cp(POOL, Rb, Rb[:], Rst[:], [Rst])
                yield

        def sec_lru():
            for i in range(2):
                xc = LT[0]; r = LT[1]; ig = LT[2]; a_ = LT[3]; h_ = LT[4]
                ts(DVE, xc, xc[:], uL[:, i, 0:TC], pp[:, 20 + 4 * i:21 + 4 * i], pp[:, 28 + i:29 + i], MUL, ADD, [uL, pp])
                for kk in range(1, 4):
                    stt(xc, xc[:], uL[:, i, kk:kk + TC], pp[:, 20 + 4 * i + kk:21 + 4 * i + kk], xc[:], MUL, ADD, [uL, pp, xc])
                cp(POOL, xcb, xcb[:], xc[:], [xc])
                mm(B[0], B[0][:], WA[:, i, :], xcb[:], True, True, [WA, xcb]); mm(B[1], B[1][:], WX[:, i, :], xcb[:], True, True, [WX, xcb])
                act(r, r[:], B[0][:], AF.Sigmoid, [B[0], pp], bias=pp[:, 30 + i:31 + i])
                act(ig, ig[:], B[1][:], AF.Sigmoid, [B[1], pp], bias=pp[:, 32 + i:33 + i])
                act(a_, a_[:], r[:], AF.Exp, [r, cl], scale=cl[:, i:i + 1])
                act(r, r[:], r[:], AF.Exp, [r, cl2], scale=cl2[:, i:i + 1])
                act(r, r[:], r[:], AF.Sqrt, [r, misc], bias=one_ap, scale=-1.0)
                tt(POOL, ig, ig[:], ig[:], xc[:], MUL, [ig, xc]); tt(DVE, r, r[:], r[:], ig[:], MUL, [r, ig])
                k.op(DVE, lambda e: e.tensor_tensor_scan(out=h_[:], data0=a_[:], data1=r[:], initial=hst[:, i:i + 1], op0=MUL, op1=ADD),
                     reads=[a_, r, hst], writes=[h_])
                cp(ACT, hst, hst[:, i:i + 1], h_[:, TC - 1:TC], [h_])
                cp(POOL, yTt[6 + i], yTb[:, 6 + i, :], h_[:], [h_])
                cp(POOL, uL, uL[:, i, 0:3], uL[:, i, TC:TC + 3], [uL])
                yield

        def sec_p():
            bk = B[5]
            for (ct, ctsw, dst) in ((8, 56, qrot), (10, 58, krot)):
                for i in range(2):
                    b = proj_fm(l, ct + i, bk); tt(DVE, rt1, rt1[:], b[:], ropeC[:], MUL, [b, ropeC])
                    b2 = proj_fm(l, ctsw + i, bk); tt(DVE, rt2, rt2[:], b2[:], ropeS[:], MUL, [b2, ropeS])
                    tt(POOL, dst, dst[:, i, :], rt1[:], rt2[:], ADD, [rt1, rt2])
                    yield
            proj_tm(l, 12, vt, lambda blk: vt[:, blk, :], [bk])
            yield
            for i in range(2):
                b = proj_fm(l, 14 + i, bk); cp(DVE, uL, uL[:, i, 3:3 + TC], b[:], [b])
                yield

        def run_wave(gens):
            st = [[g, n, 0, True] for g, n in gens]
            while any(x[3] for x in st):
                cand = [x for x in st if x[3]]
                x = min(cand, key=lambda y: y[2] / max(y[1], 1))
                try:
                    next(x[0]); x[2] += 1
                except StopIteration:
                    x[3] = False
        npairs = sum(c * 4 + qi + 1 for qi in range(4))
        wave_a = []
        if 'sb' in parts: wave_a.append((sec_sb(), 4 * npairs + 9))
        if 's5' in parts: wave_a.append((sec_s5(), 16 * 6 + 1))
        wave_a.append((sec_p(), 8))
        run_wave(wave_a)
        wave_b = []
        if 'ret' in parts: wave_b.append((sec_ret(), 4))
        if 'lru' in parts: wave_b.append((sec_lru(), 2))
        run_wave(wave_b)

    def phaseO(l, c):
        xc_ = xTc[c]
        for ct in range(8):
            b = proj_fm(l, 16 + ct)
            act(prodS[ct % 4], prodS[ct % 4][:], b[:], AF.Silu, [b])
            tt(DVE, yTt[ct], yTb[:, ct, :], yTb[:, ct, :], prodS[ct % 4][:], MUL, [yTt[ct], prodS[ct % 4]])
        if 'O1' in parts: return
        pend_f = None
        def emit_sum(ff):
            bs = B[4]
            for n2 in range(4):
                mm(bs, bs[:], identb, prodS[n2][:], n2 == 0, n2 == 3, [cmats, prodS[n2]])
            cp(ACT, merged, merged[:, ff, :], bs[:], [bs])
        for f in range(8):
            wb = WB[f % 2]
            k.dma(POOL, wb[:], wbr_d[l].rearrange("n (k p) d -> p (n k) d", p=128)[:, :, f * 128:(f + 1) * 128], writes=[wb])
            for n in range(4):
                bm = proj_fm(l, 24 + n * 8 + f)
                by = B[2 + n % 2]
                for kk in range(2):
                    mm(by, by[:], wb[:, n * 2 + kk, :], yTb[:, 2 * n + kk, :], kk == 0, kk == 1, [wb, yTt[2 * n + kk]])
                if n == 0 and pend_f is not None:
                    emit_sum(pend_f); pend_f = None
                s_ = sg[n % 2]
                act(s_, s_[:], bm[:], AF.Sigmoid, [bm])
                tt(DVE, prodS[n], prodS[n][:], s_[:], by[:], MUL, [s_, by])
            pend_f = f
        emit_sum(pend_f)
        if 'O2' in parts: return
        for half in range(2):
            cs_ = slice(half * 256, (half + 1) * 256)
            ssb = B[5]
            pend = None
            for e in range(8):
                w = load_w(wout_d[l].rearrange("(k p) c -> p k c", p=128)[:, :, e * 128:(e + 1) * 128])
                b = B[pbn[0] % 2]; pbn[0] += 1
                for d in range(8):
                    mm(b, b[:, 0:256], w[:, d, :], merged[:, d, cs_], d == 0, d == 7, [w, merged])
                if pend is not None:
                    pe_ = pend
                    mm(ssb, ssb[:, 0:256], onesb, sqb[pe_ % 2][:, 0:256], pe_ == 0, pe_ == 7, [cmats, sqb[pe_ % 2]])
                cp(DVE, outT, outT[:, e, :], b[:, 0:256], [b])
                act(sqb[e % 2], sqb[e % 2][:, 0:256], outT[:, e, :], AF.Square, [outT])
                pend = e
            mm(ssb, ssb[:, 0:256], onesb, sqb[pend % 2][:, 0:256], pend == 0, pend == 7, [cmats, sqb[pend % 2]])
            if 'x1' in parts or 'x2' in parts: continue
            act(sg[0], sg[0][:, 0:256], ssb[:, 0:256], AF.Sqrt, [ssb, misc], bias=eps_ap, scale=1.0 / D)
            recip(rstd, rstd[:, 0:256], sg[0][:, 0:256], [sg[0]])
            for e in range(8):
                tb_ = sg[e % 2]
                stt(tb_, tb_[:, 0:256], outT[:, e, :], pp[:, 8 + e:9 + e], rstd[:, 0:256], MUL, MUL, [outT, pp, rstd])
                tt(POOL, xc_, xc_[:, e, cs_], xc_[:, e, cs_], tb_[:, 0:256], ADD, [xc_, tb_])

    for sq in range(NSEQ):
        k.barrier(); load_seq(sq)
        for l in range(L):
            k.barrier()
            if 'setup' in parts: layer_setup(l)
            k.barrier()
            if 'prenorm' in parts: prenorm(l, 0)
            k.barrier()
            for c in range(NCH):
                if 'proj' in parts: phaseM(l, c)
                if "yT" in dbg_d and sq == 0 and l == 0:
                    k.dma(POOL, dbg_d["yT"][:, :, c * TC:(c + 1) * TC], yTb[:], reads=yTt)
                k.barrier()
                if 'O' in parts: phaseO(l, c)
                if c + 1 < NCH and 'prenorm' in parts: prenorm(l, c + 1)
                k.barrier()
        store_seq(sq)
    k.barrier()
    import os
    if os.environ.get('KSTAT'): print('ENGINE COUNTS', {E.name: (E.cnt, E.real) for E in k.engs}, 'dma', {E.name: sum(E.dvals)//16 for E in k.engs})
    return nc, k


def _prep_shared(inp, S, L):
    f32 = np.float32
    g = lambda n: np.asarray(inp[n], dtype=f32)[:L]
    w_in = g('w_in')
    perm = []
    for base in (1024, 1280):
        for h in range(4):
            b0 = base + 64 * h
            perm += list(range(b0 + 32, b0 + 64)) + list(range(b0, b0 + 32))
    win = np.ascontiguousarray(np.concatenate([w_in, w_in[:, :, perm]], axis=2))
    pre_g, post_g = g('pre_norm_g'), g('post_norm_g')
    pp = np.zeros((L, 128, 40), f32)
    for l in range(L):
        pp[l, :, 0:8] = pre_g[l].reshape(8, 128).T
        pp[l, :, 8:16] = post_g[l].reshape(8, 128).T
        pp[l, :, 16:18] = g('ssm_d')[l].reshape(2, 128).T
        pp[l, :, 18:20] = g('ssm_b_glu')[l].reshape(2, 128).T
        cw = g('lru_conv_w')[l]
        for i in range(2):
            pp[l, :, 20 + 4 * i:24 + 4 * i] = cw[:, 128 * i:128 * (i + 1)].T
        pp[l, :, 28:30] = g('lru_conv_b')[l].reshape(2, 128).T
        pp[l, :, 30:32] = g('lru_b_a')[l].reshape(2, 128).T
        pp[l, :, 32:34] = g('lru_b_x')[l].reshape(2, 128).T
        pp[l, :, 34:36] = g('lru_lambda')[l].reshape(2, 128).T
    lruw = np.zeros((L, 2, 128, 2, 128), f32)
    for l in range(L):
        for ax, nm in enumerate(('lru_w_a', 'lru_w_x')):
            w = g(nm)[l]
            for i in range(2):
                for bl in range(2):
                    lruw[l, ax, 64 * bl:64 * bl + 64, i, 64 * bl:64 * bl + 64] = w[2 * i + bl]
    a_re, a_im, ldt = g('ssm_a_re'), g('ssm_a_im'), g('ssm_log_dt')
    s5row = np.stack([a_re.reshape(L, 1024), a_im.reshape(L, 1024), np.repeat(ldt, 64, axis=1)], axis=1)
    s5col = np.zeros((L, 128, 3, 16), f32)
    for l in range(L):
        s5col[l, :, 0, :] = np.concatenate([a_re[l].T, a_re[l].T], axis=0)
        s5col[l, :, 1, :] = np.concatenate([a_im[l].T, a_im[l].T], axis=0)
        s5col[l, :, 2, :] = np.broadcast_to(ldt[l][None, :], (128, 16))
    bblk = np.zeros((L, 2, 128, 2, 512), f32)
    for l in range(L):
        for ri, nm in enumerate(('ssm_b_re', 'ssm_b_im')):
            bb = g(nm)[l]
            for gg in range(16):
                kh, gl = gg // 8, gg % 8
                bblk[l, ri, gl * 16:(gl + 1) * 16, kh, gl * 64:(gl + 1) * 64] = bb[gg].T
    cab = np.zeros((L, 128, 32, 128), f32)
    c_re, c_im = g('ssm_c_re'), g('ssm_c_im')
    for l in range(L):
        for gg in range(16):
            co = 16 * (gg % 8)
            cab[l, 0:64, 2 * gg, co:co + 16] = c_re[l, gg].T
            cab[l, 64:128, 2 * gg, co:co + 16] = c_im[l, gg].T
            cab[l, 0:64, 2 * gg + 1, co:co + 16] = c_im[l, gg].T
            cab[l, 64:128, 2 * gg + 1, co:co + 16] = c_re[l, gg].T
    p = np.arange(128)
    invf = (np.float32(10000.0) ** (-(np.arange(32, dtype=f32) / np.float32(32)))).astype(f32)
    ang = (np.arange(S, dtype=f32)[None, :] * invf[(p % 64) % 32][:, None]).astype(f32)
    ropeC = np.cos(ang).astype(f32)
    ropeS = (np.sin(ang) * np.where((p % 64) < 32, -1.0, 1.0)[:, None]).astype(f32)
    gam = 1.0 - 2.0 ** (-5.0 - np.arange(4))
    ii = np.arange(128)
    dtab = np.zeros((128, 2, 256), f32)
    for par in range(2):
        for ti in range(2):
            h = 2 * ti + par
            rel = ii[None, :] - ii[:, None]
            dtab[:, par, ti * 128:(ti + 1) * 128] = np.where(rel >= 0, gam[h] ** np.maximum(rel, 0), 0.0) / 8.0
    ztab = np.zeros((128, 256), f32)
    for h in range(4):
        ztab[:, h * 64:(h + 1) * 64] = (gam[h] ** (127 - ii) / 8.0)[:, None]
    xitab = np.zeros((128, 2, 128), f32); gdec = np.zeros((128, 2), f32)
    for ti in range(2):
        for hl in range(2):
            h = 2 * ti + hl
            xitab[64 * hl:64 * hl + 64, ti, :] = (gam[h] ** (ii + 1.0))[None, :]
            gdec[64 * hl:64 * hl + 64, ti] = gam[h] ** 128
    cm = np.zeros((128, 8, 128), f32)
    cm[:, 0, :] = np.eye(128)
    cm[:, 1, :] = -1.0 * (ii[:, None] >= ii[None, :])
    cm[:, 2, :] = -1.0
    cm[:, 3, :] = (ii[:, None] <= ii[None, :])
    cm[:, 4, :] = (ii[:, None] < ii[None, :])
    for m in range(64):
        cm[m + 64, 5, m] = -1.0
        cm[m, 5, m + 64] = 1.0
    cm[:, 6, :] = 1.0
    cm[0:64, 7, 0:64] = 1.0; cm[64:128, 7, 64:128] = 1.0
    misc = np.zeros((128, 132), f32)
    misc[:, 0] = p; misc[:, 1] = np.where(p < 64, 1.0, -1.0); misc[:, 2] = EPS; misc[:, 3] = 1.0
    misc[:, 4:132] = np.arange(128)[None, :]
    return dict(win=win, wbr=g('w_branch'), wout=g('w_out'), wglu=g('ssm_w_glu'), pp=pp, lruw=lruw, s5row=np.ascontiguousarray(s5row),
                s5col=s5col, bblk=bblk, cab=cab, ropeC=ropeC, ropeS=ropeS, dtab=dtab, ztab=ztab, xitab=xitab, gdec=gdec,
                cmats=cm, misc=misc)


def run(inp, S, NSEQ, DEPTH, ncores, dbg=None, parts=None):
    shared = _prep_shared(inp, S, DEPTH)
    x = np.asarray(inp['x'], dtype=np.float32)
    nc = build(S, NSEQ, DEPTH, dbg, parts)
    in_maps = []
    for i in range(ncores):
        m = dict(shared); m['x'] = np.ascontiguousarray(x[i * NSEQ:(i + 1) * NSEQ]); in_maps.append(m)
    res = run_bass_kernel_spmd(nc, in_maps, core_ids=list(range(ncores)))
    return res


def kernel(**inputs):
    x = np.asarray(inputs['x'])
    Bn, S, _ = x.shape
    ncores = 8
    NSEQ = Bn // ncores
    res = run(inputs, S, NSEQ, 2, ncores)
    return np.concatenate([r["y"] for r in res.results], axis=0).astype(np.float32)
```
